# Optimizing a Trainium2 kernel written in Bass

```python
import jax, jax.numpy as jnp
from jax import lax
import numpy as np

D_MODEL = 1024
BATCH = 16
SEQ = 2048
DEPTH = 1

N_META = 16
RET_HEADS = 8
RET_DK = 64
RET_DV = 128
RET_CHUNK = 128
RET_W = RET_HEADS * RET_DV
ATT_Q_HEADS = 16
ATT_KV_HEADS = 4
ATT_GROUP = ATT_Q_HEADS // ATT_KV_HEADS
ATT_DH = 64
ATT_W = ATT_Q_HEADS * ATT_DH
WINDOW = 128
ATT_BLOCK = 128
D_FF = -(-8 * D_MODEL // (3 * 256)) * 256
IN_SIZES = (RET_HEADS * RET_DK, RET_HEADS * RET_DK, RET_W, RET_W,
            ATT_Q_HEADS * ATT_DH, ATT_KV_HEADS * ATT_DH, ATT_KV_HEADS * ATT_DH,
            D_MODEL, D_MODEL)
IN_COLS = sum(IN_SIZES)
RMS_EPS = 1e-6
GN_EPS = 1e-5
NEG_INF = -1e30

kernel_name = "hybrid_retention_swa_alibi_meta_encoder"


def rmsnorm(x, g):
    x32 = x.astype(jnp.float32)
    y = x32 * lax.rsqrt(jnp.mean(x32 * x32, axis=-1, keepdims=True) + RMS_EPS)
    return (y * g.astype(jnp.float32)).astype(x.dtype)


def alibi_slopes(n_heads):
    return 2.0 ** (-8.0 * jnp.arange(1, n_heads + 1, dtype=jnp.float32) / n_heads)


def _retention_scan(q, k, v, log_gamma, strict):
    C = q.shape[3]
    idx = jnp.arange(C, dtype=jnp.float32)
    diff = idx[:, None] - idx[None, :]
    mask = (diff > 0) if strict else (diff >= 0)
    decay_intra = jnp.where(mask, jnp.exp(jnp.where(mask, diff, 0.0)[None] * log_gamma[:, None, None]), 0.0)
    q_decay = jnp.exp((idx[None, :] + 1.0) * log_gamma[:, None])[..., None]
    k_decay = jnp.exp((C - 1.0 - idx[None, :]) * log_gamma[:, None])[..., None]
    chunk_decay = jnp.exp(C * log_gamma)[:, None, None]

    def step(state, qkv):
        qc, kc, vc = qkv
        scores = jnp.einsum('bhid,bhjd->bhij', qc, kc) * decay_intra
        out = (jnp.einsum('bhij,bhje->bhie', scores, vc)
               + jnp.einsum('bhid,bhde->bhie', qc * q_decay, state))
        state = chunk_decay * state + jnp.einsum('bhjd,bhje->bhde', kc * k_decay, vc)
        return state, out

    B, H, dk, dv = q.shape[1], q.shape[2], q.shape[4], v.shape[4]
    state0 = jnp.zeros((B, H, dk, dv), jnp.float32)
    _, out = lax.scan(step, state0, (q, k, v))
    return out


def bidirectional_retention(q, k, v, g, logit_fwd, logit_bwd, gn_gain):
    B, L = q.shape[0], q.shape[1]
    pad = (-L) % RET_CHUNK
    n_chunks = (L + pad) // RET_CHUNK

    def to_chunks(t):
        t = jnp.pad(t.astype(jnp.float32), ((0, 0), (pad, 0), (0, 0), (0, 0)))
        return t.reshape(B, n_chunks, RET_CHUNK, t.shape[2], t.shape[3]).transpose(1, 0, 3, 2, 4)

    qc = to_chunks(q)
    kc = to_chunks(k) * (RET_DK ** -0.5)
    vc = to_chunks(v)
    log_gf = jax.nn.log_sigmoid(logit_fwd.astype(jnp.float32))
    log_gb = jax.nn.log_sigmoid(logit_bwd.astype(jnp.float32))
    flip = lambda t: t[::-1, :, :, ::-1]
    fwd = _retention_scan(qc, kc, vc, log_gf, False)
    bwd = flip(_retention_scan(flip(qc), flip(kc), flip(vc), log_gb, True))
    o = (fwd + bwd).transpose(1, 0, 3, 2, 4).reshape(B, n_chunks * RET_CHUNK, RET_HEADS, RET_DV)[:, pad:]
    mu = jnp.mean(o, axis=-1, keepdims=True)
    var = jnp.mean(jnp.square(o - mu), axis=-1, keepdims=True)
    o = ((o - mu) * lax.rsqrt(var + GN_EPS)).reshape(B, L, RET_W) * gn_gain.astype(jnp.float32)
    return (jax.nn.silu(g.astype(jnp.float32)) * o).astype(g.dtype)


def windowed_gqa_alibi(q, k, v, sink):
    B, L = q.shape[0], q.shape[1]
    S = L - N_META
    nb = S // ATT_BLOCK
    K, G, BLK, dh = ATT_KV_HEADS, ATT_GROUP, ATT_BLOCK, ATT_DH
    q = (q * (dh ** -0.5)).reshape(B, L, K, G, dh).transpose(0, 2, 3, 1, 4)
    k = k.transpose(0, 2, 1, 3)
    v = v.transpose(0, 2, 1, 3)
    slopes = alibi_slopes(ATT_Q_HEADS).reshape(K, G)
    sink = sink.astype(jnp.float32).reshape(K, G)
    qm, qr = q[:, :, :, :N_META], q[:, :, :, N_META:]
    km, kr = k[:, :, :N_META], k[:, :, N_META:]
    vm, vr = v[:, :, :N_META], v[:, :, N_META:]

    qb = qr.reshape(B, K, G, nb, BLK, dh)

    def band(t):
        tp = jnp.pad(t, ((0, 0), (0, 0), (BLK, BLK), (0, 0))).reshape(B, K, nb + 2, BLK, dh)
        return jnp.concatenate([tp[:, :, :-2], tp[:, :, 1:-1], tp[:, :, 2:]], axis=3)

    kb, vb = band(kr), band(vr)
    r = jnp.arange(BLK)
    j = jnp.arange(3 * BLK)
    dist = jnp.abs(j[None, :] - BLK - r[:, None])
    key_pos = jnp.arange(nb)[:, None] * BLK - BLK + j[None, :]
    valid = (dist <= WINDOW)[None] & ((key_pos >= 0) & (key_pos < S))[:, None, :]
    s_band = (jnp.einsum('bkgnqd,bknsd->bkgnqs', qb, kb).astype(jnp.float32)
              - slopes[:, :, None, None, None] * dist.astype(jnp.float32))
    s_band = jnp.where(valid, s_band, NEG_INF)
    s_meta = jnp.einsum('bkgnqd,bksd->bkgnqs', qb, km).astype(jnp.float32)
    s_sink = jnp.broadcast_to(sink[None, :, :, None, None, None], (B, K, G, nb, BLK, 1))
    p = jax.nn.softmax(jnp.concatenate([s_meta, s_band, s_sink], axis=-1), axis=-1).astype(v.dtype)
    o_real = (jnp.einsum('bkgnqs,bksd->bkgnqd', p[..., :N_META], vm)
              + jnp.einsum('bkgnqs,bknsd->bkgnqd', p[..., N_META:N_META + 3 * BLK], vb))
    o_real = o_real.reshape(B, K, G, S, dh)

    kf, vf = kr[:, :, :BLK], vr[:, :, :BLK]
    dist_m = N_META + jnp.arange(BLK)[None, :] - jnp.arange(N_META)[:, None]
    sm_meta = jnp.einsum('bkgid,bksd->bkgis', qm, km).astype(jnp.float32)
    sm_real = (jnp.einsum('bkgid,bksd->bkgis', qm, kf).astype(jnp.float32)
               - slopes[:, :, None, None] * dist_m.astype(jnp.float32))
    sm_real = jnp.where(dist_m <= WINDOW, sm_real, NEG_INF)
    sm_sink = jnp.broadcast_to(sink[None, :, :, None, None], (B, K, G, N_META, 1))
    pm = jax.nn.softmax(jnp.concatenate([sm_meta, sm_real, sm_sink], axis=-1), axis=-1).astype(v.dtype)
    o_meta = (jnp.einsum('bkgis,bksd->bkgid', pm[..., :N_META], vm)
              + jnp.einsum('bkgis,bksd->bkgid', pm[..., N_META:N_META + BLK], vf))

    o = jnp.concatenate([o_meta, o_real], axis=3)
    return o.transpose(0, 3, 1, 2, 4).reshape(B, L, ATT_W)


def hybrid_layer(x, w_in, ret_logit_f, ret_logit_b, ret_gn, attn_sink,
                 w_branch_ret, w_branch_att, w_out, norm_mix, norm_ffn, w_gate_up, w_down):
    B, L, _ = x.shape
    h = rmsnorm(x, norm_mix)
    proj = h @ w_in
    split_at = [int(c) for c in np.cumsum(IN_SIZES)[:-1]]
    rq, rk, rv, rg, aq, ak, av, ga, gb = jnp.split(proj, split_at, axis=-1)
    ret = bidirectional_retention(rq.reshape(B, L, RET_HEADS, RET_DK), rk.reshape(B, L, RET_HEADS, RET_DK),
                                  rv.reshape(B, L, RET_HEADS, RET_DV), rg, ret_logit_f, ret_logit_b, ret_gn)
    att = windowed_gqa_alibi(aq.reshape(B, L, ATT_Q_HEADS, ATT_DH), ak.reshape(B, L, ATT_KV_HEADS, ATT_DH),
                             av.reshape(B, L, ATT_KV_HEADS, ATT_DH), attn_sink)
    merged = jax.nn.sigmoid(ga) * (ret @ w_branch_ret) + jax.nn.sigmoid(gb) * (att @ w_branch_att)
    x = x + merged @ w_out
    h = rmsnorm(x, norm_ffn)
    a, b = jnp.split(h @ w_gate_up, 2, axis=-1)
    return x + (jax.nn.silu(a) * b) @ w_down


def setup_inputs(seed: int = 0) -> dict:
    key = jax.random.key(seed)
    ks = jax.random.split(key, 16)
    f32 = jnp.float32
    nrm = lambda k, shape, fan_in: jax.random.normal(k, shape, f32) * (fan_in ** -0.5)
    base = 1.0 - 2.0 ** (-5.0 - jnp.arange(RET_HEADS, dtype=f32))
    base_logit = jnp.log(base) - jnp.log1p(-base)
    return {
        "x": jax.random.normal(ks[0], (BATCH, SEQ, D_MODEL), f32),
        "meta_tokens": jax.random.normal(ks[1], (N_META, D_MODEL), f32),
        "w_in": nrm(ks[2], (DEPTH, D_MODEL, IN_COLS), D_MODEL),
        "ret_decay_logit_fwd": base_logit[None] + 0.05 * jax.random.normal(ks[3], (DEPTH, RET_HEADS), f32),
        "ret_decay_logit_bwd": base_logit[None] + 0.05 * jax.random.normal(ks[4], (DEPTH, RET_HEADS), f32),
        "ret_gn_gain": 1.0 + 0.05 * jax.random.normal(ks[5], (DEPTH, RET_W), f32),
        "attn_sink": 0.5 * jax.random.normal(ks[6], (DEPTH, ATT_Q_HEADS), f32),
        "w_branch_ret": nrm(ks[7], (DEPTH, RET_W, D_MODEL), RET_W),
        "w_branch_att": nrm(ks[8], (DEPTH, ATT_W, D_MODEL), ATT_W),
        "w_out": nrm(ks[9], (DEPTH, D_MODEL, D_MODEL), D_MODEL),
        "norm_mix": 1.0 + 0.05 * jax.random.normal(ks[10], (DEPTH, D_MODEL), f32),
        "norm_ffn": 1.0 + 0.05 * jax.random.normal(ks[11], (DEPTH, D_MODEL), f32),
        "w_gate_up": nrm(ks[12], (DEPTH, D_MODEL, 2 * D_FF), D_MODEL),
        "w_down": nrm(ks[13], (DEPTH, D_FF, D_MODEL), D_FF),
        "norm_final": 1.0 + 0.05 * jax.random.normal(ks[14], (D_MODEL,), f32),
    }


def reference(x, meta_tokens, w_in, ret_decay_logit_fwd, ret_decay_logit_bwd, ret_gn_gain, attn_sink,
              w_branch_ret, w_branch_att, w_out, norm_mix, norm_ffn, w_gate_up, w_down, norm_final):
    B = x.shape[0]
    meta = jnp.broadcast_to(meta_tokens.astype(x.dtype)[None], (B, N_META, x.shape[2]))
    h = jnp.concatenate([meta, x], axis=1)
    for layer in range(DEPTH):
        h = hybrid_layer(h, w_in[layer], ret_decay_logit_fwd[layer], ret_decay_logit_bwd[layer],
                         ret_gn_gain[layer], attn_sink[layer], w_branch_ret[layer], w_branch_att[layer],
                         w_out[layer], norm_mix[layer], norm_ffn[layer], w_gate_up[layer], w_down[layer])
    return rmsnorm(h, norm_final)[:, N_META:]
```

```python
import numpy as np
from contextlib import ExitStack
import concourse.bass as bass
import concourse.mybir as mb
from concourse.bass_utils import run_bass_kernel_spmd

F32 = mb.dt.float32
BF = mb.dt.bfloat16
ALU = mb.AluOpType
AF = mb.ActivationFunctionType
ENGS = ("pe", "act", "dve", "pool", "sp")


class _Op:
    __slots__ = ("eng", "fn", "deps", "dma", "sem", "val", "waits", "vc")

    def __init__(self, eng, fn, deps, dma):
        self.eng = eng
        self.fn = fn
        self.deps = deps
        self.dma = dma
        self.sem = None
        self.val = 0
        self.waits = ()
        self.vc = None


class Prog:
    CH = 8000

    def __init__(self):
        self.ops = []
        self.last_w = {}
        self.rd = {}
        self.last_eng = {}
        self.open_dma = []

    def op(self, eng, fn, r=(), w=(), dma=None):
        i = len(self.ops)
        deps = {}
        for k in r:
            lw = self.last_w.get(k)
            if lw is not None:
                deps[lw] = "raw"
            if eng != "pe" and isinstance(k, tuple) and k[0] == "ps":
                rr = self.rd.get(k)
                if rr:
                    for e2, x in rr[0].items():
                        if e2 != eng:
                            deps.setdefault(x, "xr")
        for k in w:
            lw = self.last_w.get(k)
            if lw is not None:
                deps.setdefault(lw, "waw")
            rr = self.rd.get(k)
            if rr:
                for x in rr[0].values():
                    deps.setdefault(x, "war")
                for x in rr[1]:
                    deps.setdefault(x, "war")
        final = []
        for d, kind in deps.items():
            od = self.ops[d]
            if od.eng == eng and od.dma is None and dma is None:
                if eng == "pe" or kind == "war":
                    continue
            final.append(d)
        self.ops.append(_Op(eng, fn, final, dma))
        for k in r:
            rr = self.rd.get(k)
            if rr is None:
                rr = self.rd[k] = ({}, [])
            if dma is None:
                rr[0][eng] = i
            else:
                rr[1].append(i)
        for k in w:
            self.last_w[k] = i
            self.rd[k] = ({}, [])
        if dma is None:
            self.last_eng[eng] = i
        else:
            self.open_dma.append(i)
        return i

    def barrier(self):
        deps = list(self.last_eng.values()) + list(self.open_dma)
        for e in ENGS:
            self.ops.append(_Op(e, None, list(deps), None))
        self.last_w = {}
        self.rd = {}
        self.open_dma = []

    def plan(self, nc, stack):
        ops = self.ops
        has_dep = [False] * len(ops)
        for o in ops:
            for d in o.deps:
                has_dep[d] = True
        eng_cnt = {e: 0 for e in ENGS}
        dma_cnt = {}
        semnames = {}
        for i, o in enumerate(ops):
            if o.fn is None:
                continue
            if o.dma is not None:
                c = dma_cnt.get(o.dma, 0) + 1
                dma_cnt[o.dma] = c
                o.sem = ("dma", o.dma)
                o.val = 16 * c
                semnames[o.sem] = None
            elif has_dep[i]:
                c = eng_cnt[o.eng]
                eng_cnt[o.eng] = c + 1
                o.sem = ("eng", o.eng, c // self.CH)
                o.val = c % self.CH + 1
                semnames[o.sem] = None
        K = {e: {} for e in ENGS}
        for o in ops:
            Ke = K[o.eng]
            waits = {}
            for d in sorted(o.deps):
                od = ops[d]
                sm, v = od.sem, od.val
                if Ke.get(sm, 0) >= v:
                    continue
                if waits.get(sm, 0) < v:
                    waits[sm] = v
                for s2, v2 in od.vc.items():
                    if Ke.get(s2, 0) < v2:
                        Ke[s2] = v2
            o.waits = list(waits.items())
            if o.sem is not None:
                vc = dict(Ke)
                vc[o.sem] = o.val
                if o.dma is None:
                    for ep in range(o.sem[2]):
                        vc[("eng", o.eng, ep)] = self.CH
                o.vc = vc
        self.semobj = {}
        for j, sm in enumerate(semnames):
            self.semobj[sm] = stack.enter_context(nc.semaphore("s%d" % j))

    def emit(self, nc, block):
        ops = self.ops
        semobj = self.semobj
        per = {e: [o for o in ops if o.eng == e] for e in ENGS}
        reg = {"pe": block.tensor, "act": block.scalar, "dve": block.vector,
               "pool": block.gpsimd, "sp": block.sync}

        def mk(lst):
            def body(e):
                for o in lst:
                    for sm, v in o.waits:
                        e.wait_ge(semobj[sm], v)
                    if o.fn is not None:
                        ins = o.fn(e)
                        if o.sem is not None:
                            ins.then_inc(semobj[o.sem], 16 if o.dma is not None else 1)
            return body

        for e in ENGS:
            if per[e]:
                reg[e](mk(per[e]))


class _Stop(Exception):
    pass


class Arena:
    def __init__(self, hf32, nbytes):
        self.hf = hf32
        self.hb = hf32.bitcast(BF)
        self.n = nbytes
        self.off = 0
        self.peak = 0

    def alloc(self, shape, dt, at=None):
        es = 4 if dt == F32 else 2
        nel = 1
        for s in shape[1:]:
            nel *= s
        if at is not None:
            off = at
        else:
            off = (self.off + 63) // 64 * 64
            self.off = off + nel * es
            self.peak = max(self.peak, self.off)
            assert self.off <= self.n, ("arena overflow", self.off, self.n)
        h = self.hf if dt == F32 else self.hb
        ap = h[:, off // es: off // es + nel]
        fd = shape[1:]
        if len(fd) > 1:
            names = "abcde"[:len(fd)]
            pat = "p (%s) -> p %s" % (" ".join(names), " ".join(names))
            ap = ap.rearrange(pat, **{n_: s for n_, s in zip(names, fd)})
        return ap


D = 1024
SEQ = 2048
NSEQ = 2
NT = 17
LP = NT * 128
FQ = ((0, 6), (6, 12), (12, 17), (17, 22))
ARENA_BYTES = 207 * 1024


def build(dbg=False, upto=None):
    nc = bass.Bass("TRN2", target_bir_lowering=False)

    def din(name, shape):
        return nc.dram_tensor(name, shape, F32, kind="ExternalInput").ap()

    x_d = din("x", [NSEQ, SEQ, D])
    meta_d = din("meta", [16, D])
    wret_d = din("wret", [4, 128, 8 * 768])
    watt_d = din("watt", [4, 128, 8 * 384])
    wav_d = din("wav", [128, 8 * 256])
    wmrg_d = din("wmrg", [8, 128, 8 * 512])
    wout_d = din("wout", [128, 8 * 1024])
    wgu_d = din("wgu", [22, 128, 8 * 256])
    wd_d = din("wd", [128, 22 * 1024])
    lgf_d = din("lgf", [8])
    lgb_d = din("lgb", [8])
    gn_d = din("gn", [D])
    sink_d = din("sink", [16])
    gmix_d = din("gmix", [D])
    gffn_d = din("gffn", [D])
    gfin_d = din("gfin", [D])
    ident_d = din("ident", [128, 128])
    r1_d = din("r1", [128, 128])
    r2_d = din("r2", [128, 128])
    iq_d = din("iq", [128, 128])
    kt_d = din("kt", [128, 2])
    em_d = din("em", [128, 48 * 128])
    mm_d = din("mm4", [128, 512])
    y_d = nc.dram_tensor("y", [NSEQ, SEQ, D], F32, kind="ExternalOutput").ap()
    dbg_d = {}
    if dbg:
        for nm, shp in (("d_ht", [128, 8 * LP]), ("d_ret", [128, 8 * SEQ]), ("d_att", [128, 8 * SEQ]),
                        ("d_mg", [128, 8 * SEQ]), ("d_x1", [128, 16 * D])):
            dbg_d[nm] = nc.dram_tensor(nm, shp, F32, kind="ExternalOutput").ap()

    P = Prog()
    with ExitStack() as st:
        arena_t = st.enter_context(nc.sbuf_tensor("arena", [128, ARENA_BYTES // 4], F32))
        A = Arena(arena_t, ARENA_BYTES)
        PS = [st.enter_context(nc.psum_tensor("ps%d" % i, [128, 512], F32)) for i in range(8)]
        PSB = [p.bitcast(BF) for p in PS]

        def MM(out, lhsT, rhs, start, stop, r, w):
            P.op("pe", lambda e: e.matmul(out, lhsT=lhsT, rhs=rhs, start=start, stop=stop), r, w)

        def TR(out, in_, r, w):
            P.op("pe", lambda e: e.transpose(out=out, in_=in_, identity=ident), r, w)

        def ACT(out, in_, func, r, w, **kw):
            P.op("act", lambda e: e.activation(out=out, in_=in_, func=func, **kw), r, w)

        def TT(eng, out, a, b, op, r, w):
            P.op(eng, lambda e: e.tensor_tensor(out=out, in0=a, in1=b, op=op), r, w)

        def TS(eng, out, a, s1, s2, op0, op1, r, w):
            if s2 is None:
                P.op(eng, lambda e: e.tensor_scalar(out=out, in0=a, scalar1=s1, scalar2=None, op0=op0), r, w)
            else:
                P.op(eng, lambda e: e.tensor_scalar(out=out, in0=a, scalar1=s1, scalar2=s2, op0=op0, op1=op1), r, w)

        def STT(eng, out, a, s, b, op0, op1, r, w):
            P.op(eng, lambda e: e.scalar_tensor_tensor(out=out, in0=a, scalar=s, in1=b, op0=op0, op1=op1), r, w)

        def CP(eng, out, in_, r, w):
            if eng == "act":
                ACT(out, in_, AF.Copy, r, w)
            else:
                P.op(eng, lambda e: e.tensor_copy(out=out, in_=in_), r, w)

        def MS(eng, ap, val, w):
            P.op(eng, lambda e: e.memset(ap, val), (), w)

        def RC(out, in_, r, w):
            P.op("dve", lambda e: e.reciprocal(out=out, in_=in_), r, w)

        def DMA(eng, out, in_, r, w, sem):
            P.op(eng, lambda e: e.dma_start(out=out, in_=in_), r, w, dma=sem)

        def WDMA(dst, src2d, key, nsplit):
            K_ = dst.shape[1]
            N_ = dst.shape[2]
            step = (K_ + nsplit - 1) // nsplit
            keys = []
            for pi, k0 in enumerate(range(0, K_, step)):
                k1 = min(K_, k0 + step)
                kk = (key, pi)
                DMA("pool", dst[:, k0:k1, :], src2d[:, k0 * N_:k1 * N_].rearrange("p (k n) -> p k n", k=k1 - k0), (), [kk], kk)
                keys.append(kk)
            return keys

        bank_state = {"pool": [0, 1, 2, 3], "i": 0, "a": 0, "b": 0}

        def nb():
            p = bank_state["pool"]
            b = p[bank_state["i"] % len(p)]
            bank_state["i"] += 1
            return b

        def nbA():
            bank_state["a"] += 1
            return 4 + bank_state["a"] % 2

        def nbB():
            bank_state["b"] += 1
            return 6 + bank_state["b"] % 2

        def bk(b):
            return ("ps", b)

        def pipeline(n, phases):
            mx = max(sk for sk, _ in phases)
            for i in range(n + mx):
                for sk, fn in phases:
                    j = i - sk
                    if 0 <= j < n:
                        fn(j)

        ident = A.alloc([128, 128], BF)
        DT = A.alloc([128, 8, 128], BF)
        QD = A.alloc([128, 8, 128], F32)
        KD = A.alloc([128, 2, 8], F32)
        CD = A.alloc([128, 8], F32)
        SC = A.alloc([128, 8], F32)
        LG = A.alloc([128, 16], F32)
        ES = A.alloc([128, 16], F32)
        KT = A.alloc([128, 2], F32)
        EM = A.alloc([128, 3, 4, 2, 2, 128], BF)
        MM4 = A.alloc([128, 2, 2, 128], BF)
        gmix = A.alloc([128, D], F32)
        gffn = A.alloc([128, D], F32)
        gfin = A.alloc([128, D], F32)
        gng = A.alloc([128, D], F32)
        SS = A.alloc([128, 64], F32)
        HT = A.alloc([128, 8, LP], BF)
        X1 = A.alloc([128, 16, D], F32)
        off_x1 = A.off - 16 * D * 4
        RETATT = A.hb[:, off_x1 // 2: off_x1 // 2 + 16 * SEQ].rearrange("p (a b) -> p a b", a=16)
        RET = RETATT[:, 0:8, :]
        ATT = RETATT[:, 8:16, :]
        base_mark = A.off

        m0 = A.off
        R1 = A.alloc([128, 128], F32)
        R2 = A.alloc([128, 128], F32)
        IQ = A.alloc([128, 128], F32)
        TMPa = A.alloc([128, 8, 128], F32)
        TMPb = A.alloc([128, 8, 128], F32)
        LGt = A.alloc([128, 16], F32)
        for (dst, src, nm) in ((R1, r1_d, "R1"), (R2, r2_d, "R2"), (IQ, iq_d, "IQ"), (KT, kt_d, "KT")):
            DMA("sp", dst, src, (), [nm], nm)
        DMA("pool", ident, ident_d, (), ["ident"], "ident")
        DMA("pool", EM.rearrange("p a b c d e -> p (a b c d e)"), em_d, (), ["EM"], "EM")
        DMA("pool", MM4.rearrange("p a b c -> p (a b c)"), mm_d, (), ["MM4"], "MM4")
        for (dst, src, nm) in ((gmix, gmix_d, "gmix"), (gffn, gffn_d, "gffn"), (gfin, gfin_d, "gfin"), (gng, gn_d, "gng")):
            DMA("sp", dst, src.partition_broadcast(128), (), [nm], nm)
        DMA("sp", LG[:, 0:8], lgf_d.partition_broadcast(128), (), ["LGa"], "LGa")
        DMA("sp", LG[:, 8:16], lgb_d.partition_broadcast(128), (), ["LGb"], "LGb")
        DMA("sp", ES, sink_d.partition_broadcast(128), (), ["ESr"], "ESr")
        ACT(LGt, LG, AF.Exp, ["LGa", "LGb"], ["LGt"], scale=-1.0)
        TS("dve", LGt, LGt, 1.0, None, ALU.add, None, ["LGt"], ["LGt2"])
        ACT(LG, LGt, AF.Ln, ["LGt2"], ["LG1"])
        TS("dve", LG, LG, -1.0, None, ALU.mult, None, ["LG1"], ["LG"])
        CP("dve", SC[0:64, :], LG[0:64, 0:8], ["LG"], ["SCa"])
        CP("dve", SC[64:128, :], LG[64:128, 8:16], ["LG"], ["SCb"])
        ACT(ES, ES, AF.Exp, ["ESr"], ["ES"])
        ACT(CD, SC, AF.Exp, ["SCa", "SCb"], ["CD"], scale=128.0)
        ACT(KD[:, 0, :], LG[:, 0:8], AF.Exp, ["LG", "KT"], ["KDa"], scale=KT[:, 0:1])
        ACT(KD[:, 1, :], LG[:, 8:16], AF.Exp, ["LG", "KT"], ["KDb"], scale=KT[:, 1:2])
        TS("dve", KD, KD, 1.0, None, ALU.mult, None, ["KDa", "KDb"], ["KD"])
        for h in range(8):
            TS("dve", TMPa[:, h, :], R1, LG[:, h:h + 1], None, ALU.mult, None, ["R1", "LG"], [("TMPa", h)])
            STT("dve", TMPb[:, h, :], R2, LG[:, 8 + h:9 + h], TMPa[:, h, :], ALU.mult, ALU.add,
                ["R2", "LG", ("TMPa", h)], [("TMPb", h)])
            ACT(DT[:, h, :], TMPb[:, h, :], AF.Exp, [("TMPb", h)], ["DT"])
            ACT(QD[:, h, :], IQ, AF.Exp, ["IQ", "SCa", "SCb"], ["QD"], scale=SC[:, h:h + 1])
        P.barrier()
        A.off = m0

        def stopif(nm):
            if upto == nm:
                raise _Stop()

        def seq_body(b):
            m_s = A.off
            off_att = off_x1 + 8 * SEQ * 2
            WR = [A.alloc([128, 8, 768], BF, at=off_att + i_ * 8 * 768 * 2) for i_ in range(2)]
            TMPQ = A.alloc([128, SEQ], BF, at=off_att + 2 * 8 * 768 * 2)
            wr_keys = {0: WDMA(WR[0], wret_d[0], ("WR", 0), 4)}
            XT = [A.alloc([128, D], F32) for _ in range(4)]
            HN = [A.alloc([128, D], BF) for _ in range(2)]
            JK = [A.alloc([128, D], BF) for _ in range(2)]
            bank_state["pool"] = [0, 1, 2, 3]

            def s0_A(c):
                xt = XT[c % 4]
                kx = ("xt", c % 4)
                if c == 0:
                    MS("pool", xt, 0.0, [kx])
                    DMA("sp", xt[112:128, :], meta_d, (), [kx], kx)
                else:
                    DMA("sp", xt, x_d[b, (c - 1) * 128:c * 128, :], (), [kx], kx)
                ACT(JK[c % 2], xt, AF.Square, [kx], [("ss", c), ("JK", c % 2)], accum_out=SS[:, c:c + 1])

            def s0_B1(c):
                TS("dve", SS[:, 20 + c:21 + c], SS[:, c:c + 1], 1.0 / D, 1e-6, ALU.mult, ALU.add, [("ss", c)], [("v1", c)])
                ACT(SS[:, 40 + c:41 + c], SS[:, 20 + c:21 + c], AF.Sqrt, [("v1", c)], [("v2", c)])

            def s0_B2(c):
                xt = XT[c % 4]
                kx = ("xt", c % 4)
                RC(SS[:, 20 + c:21 + c], SS[:, 40 + c:41 + c], [("v2", c)], [("rs", c)])
                STT("dve", HN[c % 2], xt, SS[:, 20 + c:21 + c], gmix, ALU.mult, ALU.mult, [kx, ("rs", c), "gmix"], [("hn", c % 2)])

            def s0_C(c):
                hn = HN[c % 2]
                bnk = nb()
                for k in range(8):
                    TR(PSB[bnk][:, k * 128:(k + 1) * 128], hn[:, k * 128:(k + 1) * 128], [("hn", c % 2)], [bk(bnk)])
                CP("act" if c % 2 else "dve", HT[:, :, c * 128:(c + 1) * 128],
                   PSB[bnk][:, :].rearrange("p (k t) -> p k t", k=8), [bk(bnk)], [("HT", c)])

            pipeline(NT, [(0, s0_A), (1, s0_B1), (2, s0_B2), (3, s0_C)])
            A.off = m_s
            P.barrier()
            if dbg and b == 0:
                DMA("pool", dbg_d["d_ht"], HT.rearrange("p k t -> p (k t)"), [("HT", 0)], (), "dbg")
            stopif("S0")

            QT = A.alloc([128, SEQ], BF)
            QDT = A.alloc([128, 2, SEQ], BF)
            KTt = A.alloc([128, LP], BF)
            KDEC = A.alloc([128, NT, 2, 128], BF)
            V = A.alloc([128, NT, 256], BF)
            SST = A.alloc([128, 2, 2, 128], F32)
            SBF = A.alloc([128, NT, 2, 128], BF)
            SGT = A.alloc([128, 2, 256], F32)
            SG = A.alloc([128, 16, 256], BF)
            STt = A.alloc([128, 2, 2, 128], BF)
            BST = A.alloc([128, 2, 2, 6], F32)
            MV = A.alloc([128, 4, 2, 2], F32)
            VE = A.alloc([128, 2, 2], F32)
            VS = A.alloc([128, 2, 2], F32)
            RS = A.alloc([128, 2, 2], F32)
            YF = A.alloc([128, 2, 256], F32)
            YB = A.alloc([128, 3, 256], BF)
            NMR = A.alloc([128, 2, 2], F32)
            OB = A.alloc([128, 4, 256], F32)
            pre_off = (A.off + 63) // 64 * 64
            assert pre_off >= m_s + 64 * 1024 - 16 * 1024 and pre_off + 10240 <= ARENA_BYTES, (pre_off, m_s)
            WAV = A.alloc([128, 8, 256], BF, at=pre_off)
            WA0 = A.alloc([128, 8, 384], BF, at=pre_off + 4096)
            bank_state["pool"] = [0, 1, 2, 3]
            for hp in range(4):
                wb = hp % 2
                W = WR[wb]
                kW = wr_keys[wb]
                if hp + 1 < 4:
                    wr_keys[1 - wb] = WDMA(WR[1 - wb], wret_d[hp + 1], ("WR", 1 - wb), 4)
                else:
                    kWAV = WDMA(WAV, wav_d, "WAV", 1)
                    wa_keys = {0: WDMA(WA0, watt_d[0], ("WA", 0), 2)}
                stopif("S1w")
                h0, h1 = 2 * hp, 2 * hp + 1
                for tb in range(4):
                    bnk = nb()
                    hk = [("HT", c) for c in range(1 + 4 * tb, 5 + 4 * tb)]
                    for kc in range(8):
                        MM(PS[bnk][:, :], W[:, kc, 0:128], HT[:, kc, 128 + tb * 512:128 + (tb + 1) * 512],
                           kc == 0, kc == 7, kW + hk, [bk(bnk)])
                    CP("act", QT[:, tb * 512:(tb + 1) * 512], PS[bnk][:, :], [bk(bnk)], [("QT", tb)])
                    TT("dve", QDT[0:64, 0, tb * 512:(tb + 1) * 512].rearrange("p (a t) -> p a t", a=4),
                       PS[bnk][0:64, :].rearrange("p (a t) -> p a t", a=4),
                       QD[0:64, h0:h0 + 1, :].to_broadcast([64, 4, 128]), ALU.mult, [bk(bnk), "QD"], [("QDT", 0, tb, "o")])
                    TT("dve", QDT[64:128, 1, tb * 512:(tb + 1) * 512].rearrange("p (a t) -> p a t", a=4),
                       PS[bnk][64:128, :].rearrange("p (a t) -> p a t", a=4),
                       QD[64:128, h1:h1 + 1, :].to_broadcast([64, 4, 128]), ALU.mult, [bk(bnk), "QD"], [("QDT", 1, tb, "o")])
                qtk = [("QT", tb) for tb in range(4)]
                DMA("sp", TMPQ[64:128, :], QT[0:64, :], qtk, ["TMPQa"], "TMPQa")
                DMA("sp", TMPQ[0:64, :], QT[64:128, :], qtk, ["TMPQb"], "TMPQb")
                TT("dve", QDT[64:128, 0, :].rearrange("p (a t) -> p a t", a=16), TMPQ[64:128, :].rearrange("p (a t) -> p a t", a=16),
                   QD[64:128, h0:h0 + 1, :].to_broadcast([64, 16, 128]), ALU.mult, ["TMPQa", "QD"], [("QDT", 0, "s")])
                TT("dve", QDT[0:64, 1, :].rearrange("p (a t) -> p a t", a=16), TMPQ[0:64, :].rearrange("p (a t) -> p a t", a=16),
                   QD[0:64, h1:h1 + 1, :].to_broadcast([64, 16, 128]), ALU.mult, ["TMPQb", "QD"], [("QDT", 1, "s")])
                stopif("S1a")
                for tb in range(5):
                    n_ = 512 if tb < 4 else 128
                    c0 = tb * 512
                    bnk = nb()
                    hk = [("HT", c) for c in range(4 * tb, min(4 * tb + 4, NT))]
                    for kc in range(8):
                        MM(PS[bnk][:, 0:n_], W[:, kc, 128:256], HT[:, kc, c0:c0 + n_], kc == 0, kc == 7, kW + hk, [bk(bnk)])
                    ACT(KTt[:, c0:c0 + n_], PS[bnk][:, 0:n_], AF.Copy, [bk(bnk)], [("KT", tb)], scale=0.125)
                stopif("S1b")
                for c in range(NT):
                    bt = nb()
                    TR(PSB[bt][:, 0:128], KTt[:, c * 128:(c + 1) * 128], [("KT", c // 4)], [bk(bt)])
                    bnk = nb()
                    for kc in range(8):
                        MM(PS[bnk][:, 0:256], HT[:, kc, c * 128:(c + 1) * 128], W[:, kc, 256:512], kc == 0, kc == 7,
                           kW + [("HT", c)], [bk(bnk)])
                    for dr in range(2):
                        TT("dve", KDEC[:, c, :, dr * 64:(dr + 1) * 64], PSB[bt][:, 0:128].rearrange("p (h d) -> p h d", h=2),
                           KD[:, dr, 2 * hp:2 * hp + 2].unsqueeze(2).to_broadcast([128, 2, 64]), ALU.mult,
                           [bk(bt), "KD"], [("KDEC", c, dr)])
                    CP("act", V[:, c, :], PS[bnk][:, 0:256], [bk(bnk)], [("V", c)])
                stopif("S1c")
                MS("pool", SST[0:64, 0], 0.0, [("SST", 0, 0, 0), ("SST", 0, 0, 1)])
                MS("pool", SBF[0:64, 0], 0.0, [("SBF", 0, 0)])
                MS("pool", SST[64:128, 0], 0.0, [("SST", 1, 0, 0), ("SST", 1, 0, 1)])
                MS("pool", SBF[64:128, 16], 0.0, [("SBF", 1, 16)])
                def scan_step(dr, j, hp=hp):
                    lo = dr * 64
                    c = j if dr == 0 else 16 - j
                    cn = c + 1 if dr == 0 else c - 1
                    bnk = nb()
                    for h in range(2):
                        MM(PS[bnk][:, h * 128:(h + 1) * 128], KDEC[:, c, h, :], V[:, c, h * 128:(h + 1) * 128], True, True,
                           [("KDEC", c, 0), ("KDEC", c, 1), ("V", c)], [bk(bnk)])
                    for h in range(2):
                        hh = 2 * hp + h
                        STT("dve", SST[lo:lo + 64, (j + 1) % 2, h, :], SST[lo:lo + 64, j % 2, h, :], CD[lo:lo + 64, hh:hh + 1],
                            PS[bnk][lo:lo + 64, h * 128:(h + 1) * 128], ALU.mult, ALU.add,
                            [("SST", dr, j % 2, h), bk(bnk), "CD"], [("SST", dr, (j + 1) % 2, h)])
                    CP("act", SBF[lo:lo + 64, cn], SST[lo:lo + 64, (j + 1) % 2],
                       [("SST", dr, (j + 1) % 2, 0), ("SST", dr, (j + 1) % 2, 1)], [("SBF", dr, cn)])

                def gate_group(c, hp=hp, W=W, kW=kW):
                    bnk = nb()
                    for kc in range(8):
                        MM(PS[bnk][:, 0:256], HT[:, kc, c * 128:(c + 1) * 128], W[:, kc, 512:768], kc == 0, kc == 7,
                           kW + [("HT", c)], [bk(bnk)])
                    ACT(SGT[:, c % 2, :], PS[bnk][:, 0:256], AF.Silu, [bk(bnk)], [("SGT", c % 2)])
                    TT("pool", SG[:, c - 1, :], SGT[:, c % 2, :], gng[:, hp * 256:(hp + 1) * 256], ALU.mult,
                       [("SGT", c % 2), "gng"], [("SG", c - 1)])

                for j in range(16):
                    scan_step(0, j)
                    scan_step(1, j)
                    gate_group(j + 1)
                stopif("S1d")
                stopif("S1e1")

                def e_A(j, hp=hp):
                    c = j + 1
                    par = c % 2
                    bA, bB = nbA(), nbB()
                    MM(PS[bA][:, 0:128], KTt[0:64, c * 128:(c + 1) * 128], QT[0:64, (c - 1) * 128:c * 128], True, True,
                       [("KT", c // 4), ("QT", (c - 1) // 4)], [bk(bA)])
                    MM(PS[bB][:, 0:128], KTt[64:128, c * 128:(c + 1) * 128], QT[64:128, (c - 1) * 128:c * 128], True, True,
                       [("KT", c // 4), ("QT", (c - 1) // 4)], [bk(bB)])
                    TT("dve", STt[:, par, 0, :], PS[bA][:, 0:128], DT[:, 2 * hp, :], ALU.mult, [bk(bA), "DT"], [("ST", par, 0)])
                    TT("dve", STt[:, par, 1, :], PS[bB][:, 0:128], DT[:, 2 * hp + 1, :], ALU.mult, [bk(bB), "DT"], [("ST", par, 1)])

                def e_B1(j, hp=hp):
                    c = j + 1
                    par = c % 2
                    o3 = c % 4
                    bo = 2 + (c % 2)
                    for h in range(2):
                        MM(PS[bo][:, h * 128:(h + 1) * 128], STt[:, par, h, :], V[:, c, h * 128:(h + 1) * 128], True, False,
                           [("ST", par, h), ("V", c)], [bk(bo)])
                        MM(PS[bo][:, h * 128:(h + 1) * 128], QDT[:, h, (c - 1) * 128:c * 128], SBF[:, c, h, :], False, True,
                           [("QDT", h, (c - 1) // 4, "o"), ("QDT", h, "s"), ("SBF", 0, c), ("SBF", 1, c)], [bk(bo)])
                    CP("act", OB[:, o3, :], PS[bo][:, 0:256], [bk(bo)], [("OB", o3)])

                def e_B1b(j, hp=hp):
                    c = j + 1
                    par = c % 2
                    o3, m4 = c % 4, c % 4
                    for h in range(2):
                        P.op("dve", lambda e, o_=BST[:, par, h, :], i_=OB[:, o3, h * 128:(h + 1) * 128]: e.bn_stats(out=o_, in_=i_),
                             [("OB", o3)], [("BST", par, h)])
                        P.op("dve", lambda e, o_=MV[:, m4, h, :], i_=BST[:, par, h, :]: e.bn_aggr(out=o_, in_=i_),
                             [("BST", par, h)], [("MV", m4, h)])
                    TS("dve", VE[:, par, :], MV[:, m4, :, 1], 1e-5, None, ALU.add, None, [("MV", m4, 0), ("MV", m4, 1)], [("VE", par)])

                def e_S(j, hp=hp):
                    c = j + 1
                    par = c % 2
                    ACT(VS[:, par, :], VE[:, par, :], AF.Sqrt, [("VE", par)], [("VS", par)])

                def e_B2(j, hp=hp):
                    c = j + 1
                    par = c % 2
                    o3, m4 = c % 4, c % 4
                    RC(RS[:, par, :], VS[:, par, :], [("VS", par)], [("RS", par)])
                    STT("dve", NMR[:, par, :], MV[:, m4, :, 0], -1.0, RS[:, par, :], ALU.mult, ALU.mult,
                        [("MV", m4, 0), ("MV", m4, 1), ("RS", par)], [("NMR", par)])
                    for h in range(2):
                        ACT(YF[:, par, h * 128:(h + 1) * 128], OB[:, o3, h * 128:(h + 1) * 128], AF.Identity,
                            [("OB", o3), ("RS", par), ("NMR", par)], [("YF", par, h)],
                            scale=RS[:, par, h:h + 1], bias=NMR[:, par, h:h + 1])
                    TT("pool", YB[:, c % 3, :], YF[:, par, :], SG[:, c - 1, :], ALU.mult,
                       [("YF", par, 0), ("YF", par, 1), ("SG", c - 1)], [("YB", c % 3)])

                def e_C(j, hp=hp):
                    c = j + 1
                    par = c % 2
                    bt = c % 2
                    for h in range(2):
                        TR(PSB[bt][:, h * 128:(h + 1) * 128], YB[:, c % 3, h * 128:(h + 1) * 128], [("YB", c % 3)], [bk(bt)])
                    CP("act", RET[:, 2 * hp:2 * hp + 2, (c - 1) * 128:c * 128],
                       PSB[bt][:, 0:256].rearrange("p (h t) -> p h t", h=2), [bk(bt)], [("RET", hp, c)])

                pipeline(16, [(0, e_A), (1, e_B1), (2, e_B1b), (3, e_S), (4, e_B2), (6, e_C)])
            A.off = m_s
            WA = [WA0, A.alloc([128, 8, 384], BF)]
            WM0 = A.alloc([128, 8, 512], BF, at=m_s + 40 * 1024)
            P.barrier()
            if dbg and b == 0:
                DMA("pool", dbg_d["d_ret"], RET.rearrange("p k t -> p (k t)"), (), (), "dbg")
            stopif("S1")

            VA = A.alloc([128, NT, 4, 80], BF)
            AQ = A.alloc([128, 2, SEQ], BF)
            AK = A.alloc([128, LP], BF)
            PT = A.alloc([128, 2, 4, 2, 2, 128], BF)
            DEN = A.alloc([128, 2, 4], F32)
            RDEN = A.alloc([128, 2, 4], F32)
            ATM = A.alloc([128, 2, 4, 64], BF)
            MS("pool", VA, 1.0, ["VA1"])
            for c in range(NT):
                bnk = nb()
                for kc in range(8):
                    MM(PS[bnk][:, 0:256], HT[:, kc, c * 128:(c + 1) * 128], WAV[:, kc, :], kc == 0, kc == 7, kWAV + [("HT", c)], [bk(bnk)])
                CP("act" if c % 2 else "dve", VA[:, c, :, 0:64], PS[bnk][:, 0:256].rearrange("p (g d) -> p g d", g=4),
                   [bk(bnk), "VA1"], [("VA", c)])
            mstate = {"c": 0}
            for kh in range(4):
                wb = kh % 2
                W = WA[wb]
                kW = wa_keys[wb]
                if kh + 1 < 4:
                    wa_keys[1 - wb] = WDMA(WA[1 - wb], watt_d[kh + 1], ("WA", 1 - wb), 2)
                else:
                    assert A.off <= m_s + 40 * 1024, A.off - m_s
                    wm_keys = {0: WDMA(WM0, wmrg_d[0], ("WM", 0), 2)}
                for pr in range(2):
                    for tb in range(4):
                        bnk = nb()
                        hk = [("HT", c) for c in range(1 + 4 * tb, 5 + 4 * tb)]
                        for kc in range(8):
                            MM(PS[bnk][:, :], W[:, kc, pr * 128:(pr + 1) * 128], HT[:, kc, 128 + tb * 512:128 + (tb + 1) * 512],
                               kc == 0, kc == 7, kW + hk, [bk(bnk)])
                        CP("act" if tb % 2 else "dve", AQ[:, pr, tb * 512:(tb + 1) * 512], PS[bnk][:, :], [bk(bnk)], [("AQ", pr, tb)])
                for tb in range(5):
                    n_ = 512 if tb < 4 else 128
                    c0 = tb * 512
                    bnk = nb()
                    hk = [("HT", c) for c in range(4 * tb, min(4 * tb + 4, NT))]
                    for kc in range(8):
                        MM(PS[bnk][:, 0:n_], W[:, kc, 256:384], HT[:, kc, c0:c0 + n_], kc == 0, kc == 7, kW + hk, [bk(bnk)])
                    CP("act" if tb % 2 else "dve", AK[:, c0:c0 + n_], PS[bnk][:, 0:n_], [bk(bnk)], [("AK", tb)])
                def blocks_of(n):
                    return [(0, None)] + [(bb, bb - (n - 1)) for bb in (n - 1, n, n + 1) if 1 <= bb <= 16]

                def a_A(j, kh=kh, W=W, kW=kW):
                    n = j + 1
                    par = n % 2
                    blocks = blocks_of(n)
                    qk = [("AQ", 0, (n - 1) // 4), ("AQ", 1, (n - 1) // 4)]
                    for r0 in range(0, len(blocks), 2):
                        rb = blocks[r0:r0 + 2]
                        bA, bB = nbA(), nbB()
                        for j_, (bb, o_) in enumerate(rb):
                            col = j_ * 256
                            MM(PS[bA][:, col:col + 256].rearrange("p (g t) -> p g t", g=2), AK[0:64, bb * 128:(bb + 1) * 128],
                               AQ[0:64, :, (n - 1) * 128:n * 128], True, True, [("AK", bb // 4)] + qk, [bk(bA)])
                            MM(PS[bB][:, col:col + 256].rearrange("p (g t) -> p g t", g=2), AK[64:128, bb * 128:(bb + 1) * 128],
                               AQ[64:128, :, (n - 1) * 128:n * 128], True, True, [("AK", bb // 4)] + qk, [bk(bB)])
                        nr = len(rb)
                        for two, bX in ((0, bA), (1, bB)):
                            ACT(PT[:, par, r0:r0 + nr, two, :, :], PS[bX][:, 0:nr * 256].rearrange("p (j g t) -> p j g t", j=nr, g=2),
                                AF.Exp, [bk(bX)], [("PT", par, r0 + j2, two) for j2 in range(nr)], scale=0.125)
                        for j_, (bb, o_) in enumerate(rb):
                            si = r0 + j_
                            msk = MM4 if o_ is None else EM[:, o_, kh]
                            mstate["c"] += 1
                            TT("dve", PT[:, par, si], PT[:, par, si], msk, ALU.mult,
                               [("PT", par, si, 0), ("PT", par, si, 1), "EM", "MM4"], [("PT", par, si, 0), ("PT", par, si, 1)])

                def a_B(j, kh=kh):
                    n = j + 1
                    par = n % 2
                    blocks = blocks_of(n)
                    bv = nb()
                    pvbank[n] = bv
                    nbk = len(blocks)
                    for g in range(4):
                        for si, (bb, o_) in enumerate(blocks):
                            MM(PS[bv][:, g * 128:g * 128 + 65], PT[:, par, si, g % 2, g // 2, :], VA[:, bb, kh, 0:65],
                               si == 0, si == nbk - 1, [("PT", par, si, g % 2), ("VA", bb)], [bk(bv)])
                    pv = PS[bv][:, :].rearrange("p (g t) -> p g t", g=4)
                    TT("dve", DEN[:, par, :], pv[:, :, 64], ES[:, 4 * kh:4 * kh + 4], ALU.add, [bk(bv), "ES"], [("DEN", par)])
                    RC(RDEN[:, par, :], DEN[:, par, :], [("DEN", par)], [("RDEN", par)])
                    TT("dve", ATM[:, par], pv[:, :, 0:64], RDEN[:, par, :].unsqueeze(2).to_broadcast([128, 4, 64]), ALU.mult,
                       [bk(bv), ("RDEN", par)], [("ATM", par)])

                def a_C(j, kh=kh):
                    n = j + 1
                    par = n % 2
                    bt = nb()
                    atf = ATM[:, par].rearrange("p g d -> p (g d)")
                    for pr in range(2):
                        TR(PSB[bt][:, pr * 128:(pr + 1) * 128], atf[:, pr * 128:(pr + 1) * 128], [("ATM", par)], [bk(bt)])
                    CP("dve", ATT[:, 2 * kh:2 * kh + 2, (n - 1) * 128:n * 128],
                       PSB[bt][:, 0:256].rearrange("p (h t) -> p h t", h=2), [bk(bt)], [("ATT", kh, n)])

                pvbank = {}
                pipeline(16, [(0, a_A), (1, a_B), (2, a_C)])
            A.off = m_s
            MG = A.alloc([128, 8, SEQ], BF)
            m_mg = A.off
            WM1 = A.alloc([128, 8, 512], BF)
            assert A.off <= m_s + 40 * 1024
            A.off = m_s + 48 * 1024
            WM = [WM0, WM1]
            P.barrier()
            if dbg and b == 0:
                DMA("pool", dbg_d["d_att"], ATT.rearrange("p k t -> p (k t)"), (), (), "dbg")
            stopif("S2")

            m_s3 = A.off
            SGA = A.alloc([128, 2, 512], F32)
            SGB = A.alloc([128, 2, 512], F32)
            T1 = A.alloc([128, 2, 512], F32)
            T2 = A.alloc([128, 2, 512], F32)
            bank_state["pool"] = [0, 1, 2, 3]
            it = 0
            for m in range(8):
                wb = m % 2
                W = WM[wb]
                kW = wm_keys[wb]
                if m + 1 < 8:
                    wm_keys[1 - wb] = WDMA(WM[1 - wb], wmrg_d[m + 1], ("WM", 1 - wb), 2)
                for tb in range(4):
                    it += 1
                    par = it % 2
                    bs = [nb() for _ in range(4)]
                    srcs = (HT[:, :, 128 + tb * 512:128 + (tb + 1) * 512], HT[:, :, 128 + tb * 512:128 + (tb + 1) * 512],
                            RET[:, :, tb * 512:(tb + 1) * 512], ATT[:, :, tb * 512:(tb + 1) * 512])
                    for q_ in range(4):
                        for kc in range(8):
                            MM(PS[bs[q_]][:, :], W[:, kc, q_ * 128:(q_ + 1) * 128], srcs[q_][:, kc, :], kc == 0, kc == 7, kW, [bk(bs[q_])])
                    stopif("S3a")
                    ACT(SGA[:, par, :], PS[bs[0]][:, :], AF.Sigmoid, [bk(bs[0])], [("SGA", par)])
                    ACT(SGB[:, par, :], PS[bs[1]][:, :], AF.Sigmoid, [bk(bs[1])], [("SGB", par)])
                    TT("dve", T1[:, par, :], PS[bs[2]][:, :], SGA[:, par, :], ALU.mult, [bk(bs[2]), ("SGA", par)], [("T1", par)])
                    TT("dve", T2[:, par, :], PS[bs[3]][:, :], SGB[:, par, :], ALU.mult, [bk(bs[3]), ("SGB", par)], [("T2", par)])
                    stopif("S3c")
                    TT("pool", MG[:, m, tb * 512:(tb + 1) * 512], T1[:, par, :], T2[:, par, :], ALU.add,
                       [("T1", par), ("T2", par)], [("MG", m, tb)])
                    stopif("S3d")
            A.off = m_mg
            WO = A.alloc([128, 8, D], BF)
            P.barrier()
            kWO = WDMA(WO, wout_d, "WO", 4)
            if dbg and b == 0:
                DMA("pool", dbg_d["d_mg"], MG.rearrange("p k t -> p (k t)"), (), (), "dbg")
            stopif("S3")

            XR = [A.alloc([128, D], F32) for _ in range(2)]
            WG01 = [A.alloc([128, 8, 256], BF, at=m_s + 64 * 1024 + i_ * 4096) for i_ in range(2)]
            wg_keys = {0: WDMA(WG01[0], wgu_d[0], ("WG", 0), 2), 1: WDMA(WG01[1], wgu_d[1], ("WG", 1), 2)}
            H2 = [A.alloc([128, D], BF) for _ in range(2)]
            JK = [A.alloc([128, D], BF) for _ in range(2)]
            assert A.off <= m_s + 64 * 1024 and m_s + 72 * 1024 <= ARENA_BYTES, (A.off - m_s, m_s)
            def s4_A(t):
                par = t % 2
                kx = ("XR", par)
                DMA("sp", XR[par], x_d[b, t * 128:(t + 1) * 128, :], (), [kx], kx)
                for hf in range(2):
                    bnk = nb()
                    for m in range(8):
                        MM(PS[bnk][:, :], MG[:, m, t * 128:(t + 1) * 128], WO[:, m, hf * 512:(hf + 1) * 512], m == 0, m == 7, kWO, [bk(bnk)])
                    TT("dve", X1[:, t, hf * 512:(hf + 1) * 512], PS[bnk][:, :], XR[par][:, hf * 512:(hf + 1) * 512], ALU.add,
                       [bk(bnk), kx], [("X1", t, hf)])
                ACT(JK[t % 2], X1[:, t, :], AF.Square, [("X1", t, 0), ("X1", t, 1)], [("ss", t), ("JK", t % 2)], accum_out=SS[:, t:t + 1])

            def s4_B1(t):
                TS("dve", SS[:, 20 + t:21 + t], SS[:, t:t + 1], 1.0 / D, 1e-6, ALU.mult, ALU.add, [("ss", t)], [("v1", t)])
                ACT(SS[:, 40 + t:41 + t], SS[:, 20 + t:21 + t], AF.Sqrt, [("v1", t)], [("v2", t)])

            def s4_B2(t):
                par = t % 2
                RC(SS[:, 20 + t:21 + t], SS[:, 40 + t:41 + t], [("v2", t)], [("rs", t)])
                STT("dve", H2[par], X1[:, t, :], SS[:, 20 + t:21 + t], gffn, ALU.mult, ALU.mult,
                    [("X1", t, 0), ("X1", t, 1), ("rs", t), "gffn"], [("H2", par)])

            def s4_C(t):
                par = t % 2
                bnk = nb()
                for k in range(8):
                    TR(PSB[bnk][:, k * 128:(k + 1) * 128], H2[par][:, k * 128:(k + 1) * 128], [("H2", par)], [bk(bnk)])
                CP("act", HT[:, :, t * 128:(t + 1) * 128], PSB[bnk][:, :].rearrange("p (k t) -> p k t", k=8), [bk(bnk)], [("HT", t)])

            pipeline(16, [(0, s4_A), (1, s4_B1), (2, s4_B2), (3, s4_C)])
            A.off = m_s
            AT = A.alloc([128, 6, SEQ], BF)
            WD = A.alloc([128, 6, D], BF)
            WG = [WG01[0], WG01[1], A.alloc([128, 8, 256], BF)]
            P.barrier()
            if dbg and b == 0:
                DMA("sp", dbg_d["d_x1"], X1.rearrange("p k t -> p (k t)"), (), (), "dbg")
            stopif("S4")

            SA = A.alloc([128, 2, 512], F32)
            OT = [A.alloc([128, D], F32) for _ in range(2)]
            JK = [A.alloc([128, D], BF) for _ in range(2)]
            assert A.off <= m_s + 64 * 1024
            it = 0
            for qi, (f0, f1) in enumerate(FQ):
                nf = f1 - f0
                kWD = WDMA(WD[:, 0:nf, :], wd_d[:, f0 * D:f1 * D], "WD", 2)
                for f in range(f0, f1):
                    wb = f % 3
                    W = WG[wb]
                    kW = wg_keys[wb]
                    if f + 2 < 22:
                        w2 = (f + 2) % 3
                        wg_keys[w2] = WDMA(WG[w2], wgu_d[f + 2], ("WG", w2), 2)
                    for tb in range(4):
                        it += 1
                        par = it % 2
                        ba_, bb_ = nb(), nb()
                        hk = [("HT", c) for c in range(4 * tb, 4 * tb + 4)]
                        for kc in range(8):
                            MM(PS[ba_][:, :], W[:, kc, 0:128], HT[:, kc, tb * 512:(tb + 1) * 512], kc == 0, kc == 7, kW + hk, [bk(ba_)])
                        for kc in range(8):
                            MM(PS[bb_][:, :], W[:, kc, 128:256], HT[:, kc, tb * 512:(tb + 1) * 512], kc == 0, kc == 7, kW + hk, [bk(bb_)])
                        ACT(SA[:, par, :], PS[ba_][:, :], AF.Silu, [bk(ba_)], [("SA", par)])
                        TT("dve", AT[:, f - f0, tb * 512:(tb + 1) * 512], PS[bb_][:, :], SA[:, par, :], ALU.mult,
                           [bk(bb_), ("SA", par)], [("AT", f - f0, tb)])
                last = qi == len(FQ) - 1

                def d_A(t, nf=nf, kWD=kWD, last=last):
                    ak_ = [("AT", fl, t // 4) for fl in range(nf)]
                    for hf in range(2):
                        bnk = nb()
                        for fl in range(nf):
                            MM(PS[bnk][:, :], AT[:, fl, t * 128:(t + 1) * 128], WD[:, fl, hf * 512:(hf + 1) * 512], fl == 0, fl == nf - 1,
                               kWD + ak_, [bk(bnk)])
                        TT("dve", X1[:, t, hf * 512:(hf + 1) * 512], PS[bnk][:, :], X1[:, t, hf * 512:(hf + 1) * 512], ALU.add,
                           [bk(bnk), ("X1", t, hf)], [("X1", t, hf)])
                    if last:
                        ACT(JK[t % 2], X1[:, t, :], AF.Square, [("X1", t, 0), ("X1", t, 1)], [("ss", t), ("JK", t % 2)],
                            accum_out=SS[:, t:t + 1])

                def d_B1(t):
                    TS("dve", SS[:, 20 + t:21 + t], SS[:, t:t + 1], 1.0 / D, 1e-6, ALU.mult, ALU.add, [("ss", t)], [("v1", t)])
                    ACT(SS[:, 40 + t:41 + t], SS[:, 20 + t:21 + t], AF.Sqrt, [("v1", t)], [("v2", t)])

                def d_B2(t):
                    par = t % 2
                    RC(SS[:, 20 + t:21 + t], SS[:, 40 + t:41 + t], [("v2", t)], [("rs", t)])
                    STT("dve", OT[par], X1[:, t, :], SS[:, 20 + t:21 + t], gfin, ALU.mult, ALU.mult,
                        [("X1", t, 0), ("X1", t, 1), ("rs", t), "gfin"], [("OT", par)])
                    DMA("sp", y_d[b, t * 128:(t + 1) * 128, :], OT[par], [("OT", par)], (), ("OT", par))

                if last:
                    pipeline(16, [(0, d_A), (1, d_B1), (2, d_B2)])
                else:
                    pipeline(16, [(0, d_A)])
            A.off = m_s
            P.barrier()

        for b_ in range(NSEQ):
            try:
                seq_body(b_)
            except _Stop:
                break
        P.barrier()
        P.plan(nc, st)
        with nc.Block() as block:
            P.emit(nc, block)
    return nc, P, A


def _tileK(W):
    K, N = W.shape
    return np.ascontiguousarray(W.reshape(K // 128, 128, N).transpose(1, 0, 2)).reshape(128, (K // 128) * N)


def _tables():
    i = np.arange(128, dtype=np.float32)
    r1 = np.maximum(i[None, :] - i[:, None], 0.0)
    r2 = np.maximum(i[:, None] - i[None, :], 0.0)
    iq = np.concatenate([np.tile((i + 1.0)[None], (64, 1)), np.tile((128.0 - i)[None], (64, 1))], 0)
    kt = np.stack([127.0 - i, i], 1)
    slopes = 2.0 ** (-8.0 * np.arange(1, 17, dtype=np.float64) / 16.0)
    s = np.arange(128)[:, None].astype(np.float64)
    t = np.arange(128)[None, :].astype(np.float64)
    d0 = t - s + 128.0
    d1 = np.abs(t - s)
    d2 = s + 128.0 - t
    em = np.zeros((128, 3, 4, 2, 2, 128), np.float64)
    for h in range(16):
        kh, g = h // 4, h % 4
        two, gp = g % 2, g // 2
        em[:, 0, kh, two, gp, :] = np.where(d0 <= 128, np.exp(-slopes[h] * d0), 0.0)
        em[:, 1, kh, two, gp, :] = np.exp(-slopes[h] * d1)
        em[:, 2, kh, two, gp, :] = np.where(d2 <= 128, np.exp(-slopes[h] * d2), 0.0)
    mm4 = np.zeros((128, 512), np.float32)
    mm4[112:, :] = 1.0
    return dict(ident=np.eye(128, dtype=np.float32), r1=r1.astype(np.float32), r2=r2.astype(np.float32),
                iq=iq.astype(np.float32), kt=kt.astype(np.float32),
                em=em.reshape(128, -1).astype(np.float32), mm4=mm4)


def _prep_shared(meta_tokens, w_in, ret_decay_logit_fwd, ret_decay_logit_bwd, ret_gn_gain, attn_sink,
                 w_branch_ret, w_branch_att, w_out, norm_mix, norm_ffn, w_gate_up, w_down, norm_final):
    f = np.float32
    wi = np.asarray(w_in[0], f)
    wbr = np.asarray(w_branch_ret[0], f)
    wba = np.asarray(w_branch_att[0], f)
    wgu = np.asarray(w_gate_up[0], f)
    wret = []
    for hp in range(4):
        h0, h1 = 2 * hp, 2 * hp + 1
        q0 = wi[:, h0 * 64:(h0 + 1) * 64]
        q1 = wi[:, h1 * 64:(h1 + 1) * 64]
        cols = [q0, q1, wi[:, 512 + hp * 128:512 + (hp + 1) * 128],
                wi[:, 1024 + hp * 256:1024 + (hp + 1) * 256], wi[:, 2048 + hp * 256:2048 + (hp + 1) * 256]]
        wret.append(_tileK(np.concatenate(cols, 1)))
    watt = []
    for kh in range(4):
        k_ = wi[:, 4096 + kh * 64:4096 + (kh + 1) * 64]
        watt.append(_tileK(np.concatenate([wi[:, 3072 + kh * 256:3072 + (kh + 1) * 256], k_, k_], 1)))
    wmrg = []
    for m in range(8):
        sl = slice(m * 128, (m + 1) * 128)
        wmrg.append(_tileK(np.concatenate([wi[:, 4608:5632][:, sl], wi[:, 5632:6656][:, sl], wbr[:, sl], wba[:, sl]], 1)))
    wgut = []
    for ff in range(22):
        wgut.append(_tileK(np.concatenate([wgu[:, ff * 128:(ff + 1) * 128], wgu[:, 2816 + ff * 128:2816 + (ff + 1) * 128]], 1)))
    d = dict(meta=np.asarray(meta_tokens, f), wret=np.stack(wret), watt=np.stack(watt), wav=_tileK(wi[:, 4352:4608]),
             wmrg=np.stack(wmrg), wout=_tileK(np.asarray(w_out[0], f)), wgu=np.stack(wgut), wd=_tileK(np.asarray(w_down[0], f)),
             lgf=np.asarray(ret_decay_logit_fwd[0], f), lgb=np.asarray(ret_decay_logit_bwd[0], f),
             gn=np.asarray(ret_gn_gain[0], f), sink=np.asarray(attn_sink[0], f), gmix=np.asarray(norm_mix[0], f),
             gffn=np.asarray(norm_ffn[0], f), gfin=np.asarray(norm_final, f))
    d.update(_tables())
    return {k: np.ascontiguousarray(v) for k, v in d.items()}


_CACHE = {}


def kernel(x, meta_tokens, w_in, ret_decay_logit_fwd, ret_decay_logit_bwd, ret_gn_gain, attn_sink,
           w_branch_ret, w_branch_att, w_out, norm_mix, norm_ffn, w_gate_up, w_down, norm_final):
    x = np.asarray(x, np.float32)
    shared = _prep_shared(meta_tokens, w_in, ret_decay_logit_fwd, ret_decay_logit_bwd, ret_gn_gain, attn_sink,
                          w_branch_ret, w_branch_att, w_out, norm_mix, norm_ffn, w_gate_up, w_down, norm_final)
    if "nc" not in _CACHE:
        _CACHE["nc"] = build()[0]
    nc = _CACHE["nc"]
    n = 8
    in_maps = []
    for c in range(n):
        m = dict(shared)
        m["x"] = np.ascontiguousarray(x[NSEQ * c:NSEQ * (c + 1)])
        in_maps.append(m)
    res = run_bass_kernel_spmd(nc, in_maps, core_ids=list(range(n)))
    return np.concatenate([np.asarray(r["y"], np.float32) for r in res.results], axis=0)
```

```python
import numpy as np
from contextlib import ExitStack
import concourse.bass as bass
import concourse.mybir as mb
from concourse.bass_utils import run_bass_kernel_spmd

F32 = mb.dt.float32
BF = mb.dt.bfloat16
ALU = mb.AluOpType
AF = mb.ActivationFunctionType
ENGS = ("pe", "act", "dve", "pool", "sp")


class _Op:
    __slots__ = ("eng", "fn", "deps", "dma", "sem", "val", "waits", "vc")

    def __init__(self, eng, fn, deps, dma):
        self.eng = eng
        self.fn = fn
        self.deps = deps
        self.dma = dma
        self.sem = None
        self.val = 0
        self.waits = ()
        self.vc = None


class Prog:
    CH = 8000

    def __init__(self):
        self.ops = []
        self.last_w = {}
        self.rd = {}
        self.last_eng = {}
        self.open_dma = []

    def op(self, eng, fn, r=(), w=(), dma=None):
        i = len(self.ops)
        deps = {}
        for k in r:
            lw = self.last_w.get(k)
            if lw is not None:
                deps[lw] = "raw"
            if eng != "pe" and isinstance(k, tuple) and k[0] == "ps":
                rr = self.rd.get(k)
                if rr:
                    for e2, x in rr[0].items():
                        if e2 != eng:
                            deps.setdefault(x, "xr")
        for k in w:
            lw = self.last_w.get(k)
            if lw is not None:
                deps.setdefault(lw, "waw")
            rr = self.rd.get(k)
            if rr:
                for x in rr[0].values():
                    deps.setdefault(x, "war")
                for x in rr[1]:
                    deps.setdefault(x, "war")
        final = []
        for d, kind in deps.items():
            od = self.ops[d]
            if od.eng == eng and od.dma is None and dma is None:
                if eng == "pe" or kind == "war":
                    continue
            final.append(d)
        self.ops.append(_Op(eng, fn, final, dma))
        for k in r:
            rr = self.rd.get(k)
            if rr is None:
                rr = self.rd[k] = ({}, [])
            if dma is None:
                rr[0][eng] = i
            else:
                rr[1].append(i)
        for k in w:
            self.last_w[k] = i
            self.rd[k] = ({}, [])
        if dma is None:
            self.last_eng[eng] = i
        else:
            self.open_dma.append(i)
        return i

    def barrier(self):
        deps = list(self.last_eng.values()) + list(self.open_dma)
        for e in ENGS:
            self.ops.append(_Op(e, None, list(deps), None))
        self.last_w = {}
        self.rd = {}
        self.open_dma = []

    def plan(self, nc, stack):
        ops = self.ops
        has_dep = [False] * len(ops)
        for o in ops:
            for d in o.deps:
                has_dep[d] = True
        eng_cnt = {e: 0 for e in ENGS}
        dma_cnt = {}
        semnames = {}
        for i, o in enumerate(ops):
            if o.fn is None:
                continue
            if o.dma is not None:
                c = dma_cnt.get(o.dma, 0) + 1
                dma_cnt[o.dma] = c
                o.sem = ("dma", o.dma)
                o.val = 16 * c
                semnames[o.sem] = None
            elif has_dep[i]:
                c = eng_cnt[o.eng]
                eng_cnt[o.eng] = c + 1
                o.sem = ("eng", o.eng, c // self.CH)
                o.val = c % self.CH + 1
                semnames[o.sem] = None
        K = {e: {} for e in ENGS}
        for o in ops:
            Ke = K[o.eng]
            waits = {}
            for d in sorted(o.deps):
                od = ops[d]
                sm, v = od.sem, od.val
                if Ke.get(sm, 0) >= v:
                    continue
                if waits.get(sm, 0) < v:
                    waits[sm] = v
                for s2, v2 in od.vc.items():
                    if Ke.get(s2, 0) < v2:
                        Ke[s2] = v2
            o.waits = list(waits.items())
            if o.sem is not None:
                vc = dict(Ke)
                vc[o.sem] = o.val
                if o.dma is None:
                    for ep in range(o.sem[2]):
                        vc[("eng", o.eng, ep)] = self.CH
                o.vc = vc
        self.semobj = {}
        for j, sm in enumerate(semnames):
            self.semobj[sm] = stack.enter_context(nc.semaphore("s%d" % j))

    def emit(self, nc, block):
        ops = self.ops
        semobj = self.semobj
        per = {e: [o for o in ops if o.eng == e] for e in ENGS}
        reg = {"pe": block.tensor, "act": block.scalar, "dve": block.vector,
               "pool": block.gpsimd, "sp": block.sync}

        def mk(lst):
            def body(e):
                for o in lst:
                    for sm, v in o.waits:
                        e.wait_ge(semobj[sm], v)
                    if o.fn is not None:
                        ins = o.fn(e)
                        if o.sem is not None:
                            ins.then_inc(semobj[o.sem], 16 if o.dma is not None else 1)
            return body

        for e in ENGS:
            if per[e]:
                reg[e](mk(per[e]))


class _Stop(Exception):
    pass


class Arena:
    def __init__(self, hf32, nbytes):
        self.hf = hf32
        self.hb = hf32.bitcast(BF)
        self.n = nbytes
        self.off = 0
        self.peak = 0

    def alloc(self, shape, dt, at=None):
        es = 4 if dt == F32 else 2
        nel = 1
        for s in shape[1:]:
            nel *= s
        if at is not None:
            off = at
        else:
            off = (self.off + 63) // 64 * 64
            self.off = off + nel * es
            self.peak = max(self.peak, self.off)
            assert self.off <= self.n, ("arena overflow", self.off, self.n)
        h = self.hf if dt == F32 else self.hb
        ap = h[:, off // es: off // es + nel]
        fd = shape[1:]
        if len(fd) > 1:
            names = "abcde"[:len(fd)]
            pat = "p (%s) -> p %s" % (" ".join(names), " ".join(names))
            ap = ap.rearrange(pat, **{n_: s for n_, s in zip(names, fd)})
        return ap


D = 1024
SEQ = 2048
NSEQ = 2
NT = 17
LP = NT * 128
FQ = ((0, 6), (6, 12), (12, 17), (17, 22))
ARENA_BYTES = 207 * 1024


def build(dbg=False, upto=None):
    nc = bass.Bass("TRN2", target_bir_lowering=False)

    def din(name, shape):
        return nc.dram_tensor(name, shape, F32, kind="ExternalInput").ap()

    x_d = din("x", [NSEQ, SEQ, D])
    meta_d = din("meta", [16, D])
    wret_d = din("wret", [4, 128, 8 * 768])
    watt_d = din("watt", [4, 128, 8 * 384])
    wav_d = din("wav", [128, 8 * 256])
    wmrg_d = din("wmrg", [8, 128, 8 * 512])
    wout_d = din("wout", [128, 8 * 1024])
    wgu_d = din("wgu", [22, 128, 8 * 256])
    wd_d = din("wd", [128, 22 * 1024])
    lgf_d = din("lgf", [8])
    lgb_d = din("lgb", [8])
    gn_d = din("gn", [D])
    sink_d = din("sink", [16])
    gmix_d = din("gmix", [D])
    gffn_d = din("gffn", [D])
    gfin_d = din("gfin", [D])
    ident_d = din("ident", [128, 128])
    r1_d = din("r1", [128, 128])
    r2_d = din("r2", [128, 128])
    iq_d = din("iq", [128, 128])
    kt_d = din("kt", [128, 2])
    em_d = din("em", [128, 48 * 128])
    mm_d = din("mm4", [128, 512])
    swp_d = din("swp", [128, 128])
    y_d = nc.dram_tensor("y", [NSEQ, SEQ, D], F32, kind="ExternalOutput").ap()
    dbg_d = {}
    if dbg:
        for nm, shp in (("d_ht", [128, 8 * LP]), ("d_ret", [128, 8 * SEQ]), ("d_att", [128, 8 * SEQ]),
                        ("d_mg", [128, 8 * SEQ]), ("d_x1", [128, 16 * D])):
            dbg_d[nm] = nc.dram_tensor(nm, shp, F32, kind="ExternalOutput").ap()

    P = Prog()
    with ExitStack() as st:
        arena_t = st.enter_context(nc.sbuf_tensor("arena", [128, ARENA_BYTES // 4], F32))
        A = Arena(arena_t, ARENA_BYTES)
        PS = [st.enter_context(nc.psum_tensor("ps%d" % i, [128, 512], F32)) for i in range(8)]
        PSB = [p.bitcast(BF) for p in PS]

        def MM(out, lhsT, rhs, start, stop, r, w):
            P.op("pe", lambda e: e.matmul(out, lhsT=lhsT, rhs=rhs, start=start, stop=stop), r, w)

        def TR(out, in_, r, w):
            P.op("pe", lambda e: e.transpose(out=out, in_=in_, identity=ident), r, w)

        def ACT(out, in_, func, r, w, **kw):
            P.op("act", lambda e: e.activation(out=out, in_=in_, func=func, **kw), r, w)

        def TT(eng, out, a, b, op, r, w):
            P.op(eng, lambda e: e.tensor_tensor(out=out, in0=a, in1=b, op=op), r, w)

        def TS(eng, out, a, s1, s2, op0, op1, r, w):
            if s2 is None:
                P.op(eng, lambda e: e.tensor_scalar(out=out, in0=a, scalar1=s1, scalar2=None, op0=op0), r, w)
            else:
                P.op(eng, lambda e: e.tensor_scalar(out=out, in0=a, scalar1=s1, scalar2=s2, op0=op0, op1=op1), r, w)

        def STT(eng, out, a, s, b, op0, op1, r, w):
            P.op(eng, lambda e: e.scalar_tensor_tensor(out=out, in0=a, scalar=s, in1=b, op0=op0, op1=op1), r, w)

        def CP(eng, out, in_, r, w):
            if eng == "act":
                ACT(out, in_, AF.Copy, r, w)
            else:
                P.op(eng, lambda e: e.tensor_copy(out=out, in_=in_), r, w)

        def MS(eng, ap, val, w):
            P.op(eng, lambda e: e.memset(ap, val), (), w)

        def RC(out, in_, r, w):
            P.op("dve", lambda e: e.reciprocal(out=out, in_=in_), r, w)

        def DMA(eng, out, in_, r, w, sem):
            P.op(eng, lambda e: e.dma_start(out=out, in_=in_), r, w, dma=sem)

        def WDMA(dst, src2d, key, nsplit):
            K_ = dst.shape[1]
            N_ = dst.shape[2]
            step = (K_ + nsplit - 1) // nsplit
            keys = []
            for pi, k0 in enumerate(range(0, K_, step)):
                k1 = min(K_, k0 + step)
                kk = (key, pi)
                DMA("pool", dst[:, k0:k1, :], src2d[:, k0 * N_:k1 * N_].rearrange("p (k n) -> p k n", k=k1 - k0), (), [kk], kk)
                keys.append(kk)
            return keys

        bank_state = {"pool": [0, 1, 2, 3], "i": 0, "a": 0, "b": 0}

        def nb():
            p = bank_state["pool"]
            b = p[bank_state["i"] % len(p)]
            bank_state["i"] += 1
            return b

        def nbA():
            bank_state["a"] += 1
            return 4 + bank_state["a"] % 2

        def nbB():
            bank_state["b"] += 1
            return 6 + bank_state["b"] % 2

        def bk(b):
            return ("ps", b)

        def pipeline(n, phases):
            mx = max(sk for sk, _ in phases)
            for i in range(n + mx):
                for sk, fn in phases:
                    j = i - sk
                    if 0 <= j < n:
                        fn(j)

        ident = A.alloc([128, 128], BF)
        SWP = A.alloc([128, 128], BF)
        DT = A.alloc([128, 8, 128], BF)
        QD = A.alloc([128, 8, 128], F32)
        KD = A.alloc([128, 2, 8], F32)
        CD = A.alloc([128, 8], F32)
        SC = A.alloc([128, 8], F32)
        LG = A.alloc([128, 16], F32)
        ES = A.alloc([128, 16], F32)
        KT = A.alloc([128, 2], F32)
        EM = A.alloc([128, 3, 4, 2, 2, 128], BF)
        MM4 = A.alloc([128, 2, 2, 128], BF)
        gmix = A.alloc([128, D], F32)
        gffn = A.alloc([128, D], F32)
        gfin = A.alloc([128, D], F32)
        gng = A.alloc([128, D], F32)
        SS = A.alloc([128, 64], F32)
        HT = A.alloc([128, 8, LP], BF)
        X1 = A.alloc([128, 16, D], F32)
        off_x1 = A.off - 16 * D * 4
        RETATT = A.hb[:, off_x1 // 2: off_x1 // 2 + 16 * SEQ].rearrange("p (a b) -> p a b", a=16)
        RET = RETATT[:, 0:8, :]
        ATT = RETATT[:, 8:16, :]
        base_mark = A.off

        m0 = A.off
        R1 = A.alloc([128, 128], F32)
        R2 = A.alloc([128, 128], F32)
        IQ = A.alloc([128, 128], F32)
        TMPa = A.alloc([128, 8, 128], F32)
        TMPb = A.alloc([128, 8, 128], F32)
        LGt = A.alloc([128, 16], F32)
        for (dst, src, nm) in ((R1, r1_d, "R1"), (R2, r2_d, "R2"), (IQ, iq_d, "IQ"), (KT, kt_d, "KT")):
            DMA("sp", dst, src, (), [nm], nm)
        DMA("pool", ident, ident_d, (), ["ident"], "ident")
        DMA("pool", SWP, swp_d, (), ["SWP"], "SWP")
        DMA("pool", EM.rearrange("p a b c d e -> p (a b c d e)"), em_d, (), ["EM"], "EM")
        DMA("pool", MM4.rearrange("p a b c -> p (a b c)"), mm_d, (), ["MM4"], "MM4")
        for (dst, src, nm) in ((gmix, gmix_d, "gmix"), (gffn, gffn_d, "gffn"), (gfin, gfin_d, "gfin"), (gng, gn_d, "gng")):
            DMA("sp", dst, src.partition_broadcast(128), (), [nm], nm)
        DMA("sp", LG[:, 0:8], lgf_d.partition_broadcast(128), (), ["LGa"], "LGa")
        DMA("sp", LG[:, 8:16], lgb_d.partition_broadcast(128), (), ["LGb"], "LGb")
        DMA("sp", ES, sink_d.partition_broadcast(128), (), ["ESr"], "ESr")
        ACT(LGt, LG, AF.Exp, ["LGa", "LGb"], ["LGt"], scale=-1.0)
        TS("dve", LGt, LGt, 1.0, None, ALU.add, None, ["LGt"], ["LGt2"])
        ACT(LG, LGt, AF.Ln, ["LGt2"], ["LG1"])
        TS("dve", LG, LG, -1.0, None, ALU.mult, None, ["LG1"], ["LG"])
        CP("dve", SC[0:64, :], LG[0:64, 0:8], ["LG"], ["SCa"])
        CP("dve", SC[64:128, :], LG[64:128, 8:16], ["LG"], ["SCb"])
        ACT(ES, ES, AF.Exp, ["ESr"], ["ES"])
        ACT(CD, SC, AF.Exp, ["SCa", "SCb"], ["CD"], scale=128.0)
        ACT(KD[:, 0, :], LG[:, 0:8], AF.Exp, ["LG", "KT"], ["KDa"], scale=KT[:, 0:1])
        ACT(KD[:, 1, :], LG[:, 8:16], AF.Exp, ["LG", "KT"], ["KDb"], scale=KT[:, 1:2])
        TS("dve", KD, KD, 1.0, None, ALU.mult, None, ["KDa", "KDb"], ["KD"])
        for h in range(8):
            TS("dve", TMPa[:, h, :], R1, LG[:, h:h + 1], None, ALU.mult, None, ["R1", "LG"], [("TMPa", h)])
            STT("dve", TMPb[:, h, :], R2, LG[:, 8 + h:9 + h], TMPa[:, h, :], ALU.mult, ALU.add,
                ["R2", "LG", ("TMPa", h)], [("TMPb", h)])
            ACT(DT[:, h, :], TMPb[:, h, :], AF.Exp, [("TMPb", h)], ["DT"])
            ACT(QD[:, h, :], IQ, AF.Exp, ["IQ", "SCa", "SCb"], ["QD"], scale=SC[:, h:h + 1])
        P.barrier()
        A.off = m0

        def stopif(nm):
            if upto == nm:
                raise _Stop()

        def seq_body(b):
            m_s = A.off
            off_att = off_x1 + 8 * SEQ * 2
            WR = [A.alloc([128, 8, 768], BF, at=off_att + i_ * 8 * 768 * 2) for i_ in range(2)]
            wr_keys = {0: WDMA(WR[0], wret_d[0], ("WR", 0), 4)}
            XT = [A.alloc([128, D], F32) for _ in range(4)]
            HN = [A.alloc([128, D], BF) for _ in range(2)]
            JK = [A.alloc([128, D], BF) for _ in range(2)]
            bank_state["pool"] = [0, 1, 2, 3]

            def s0_A(c):
                xt = XT[c % 4]
                kx = ("xt", c % 4)
                if c == 0:
                    MS("pool", xt, 0.0, [kx])
                    DMA("sp", xt[112:128, :], meta_d, (), [kx], kx)
                else:
                    DMA("sp", xt, x_d[b, (c - 1) * 128:c * 128, :], (), [kx], kx)
                ACT(JK[c % 2], xt, AF.Square, [kx], [("ss", c), ("JK", c % 2)], accum_out=SS[:, c:c + 1])

            def s0_B1(c):
                TS("dve", SS[:, 20 + c:21 + c], SS[:, c:c + 1], 1.0 / D, 1e-6, ALU.mult, ALU.add, [("ss", c)], [("v1", c)])
                ACT(SS[:, 40 + c:41 + c], SS[:, 20 + c:21 + c], AF.Sqrt, [("v1", c)], [("v2", c)])

            def s0_B2(c):
                xt = XT[c % 4]
                kx = ("xt", c % 4)
                RC(SS[:, 20 + c:21 + c], SS[:, 40 + c:41 + c], [("v2", c)], [("rs", c)])
                STT("dve", HN[c % 2], xt, SS[:, 20 + c:21 + c], gmix, ALU.mult, ALU.mult, [kx, ("rs", c), "gmix"], [("hn", c % 2)])

            def s0_C(c):
                hn = HN[c % 2]
                bnk = nb()
                for k in range(8):
                    TR(PSB[bnk][:, k * 128:(k + 1) * 128], hn[:, k * 128:(k + 1) * 128], [("hn", c % 2)], [bk(bnk)])
                CP("act" if c % 2 else "dve", HT[:, :, c * 128:(c + 1) * 128],
                   PSB[bnk][:, :].rearrange("p (k t) -> p k t", k=8), [bk(bnk)], [("HT", c)])

            pipeline(NT, [(0, s0_A), (1, s0_B1), (2, s0_B2), (3, s0_C)])
            A.off = m_s
            P.barrier()
            if dbg and b == 0:
                DMA("pool", dbg_d["d_ht"], HT.rearrange("p k t -> p (k t)"), [("HT", 0)], (), "dbg")
            stopif("S0")

            QT = A.alloc([128, SEQ], BF)
            QDT = A.alloc([128, 2, SEQ], BF)
            KTt = A.alloc([128, LP], BF)
            KDEC = A.alloc([128, NT, 2, 128], BF)
            V = A.alloc([128, NT, 256], BF)
            SST = A.alloc([128, 2, 2, 128], F32)
            SBF = A.alloc([128, NT, 2, 128], BF)
            SGT = A.alloc([128, 2, 256], F32)
            SG = A.alloc([128, 16, 256], BF)
            STt = A.alloc([128, 2, 2, 128], BF)
            BST = A.alloc([128, 2, 2, 6], F32)
            MV = A.alloc([128, 4, 2, 2], F32)
            VE = A.alloc([128, 2, 2], F32)
            VS = A.alloc([128, 2, 2], F32)
            RS = A.alloc([128, 2, 2], F32)
            YF = A.alloc([128, 2, 256], F32)
            YB = A.alloc([128, 3, 256], BF)
            NMR = A.alloc([128, 2, 2], F32)
            OB = A.alloc([128, 4, 256], F32)
            pre_off = (A.off + 63) // 64 * 64
            assert pre_off >= m_s + 64 * 1024 - 16 * 1024 and pre_off + 10240 <= ARENA_BYTES, (pre_off, m_s)
            WAV = A.alloc([128, 8, 256], BF, at=pre_off)
            WA0 = A.alloc([128, 8, 384], BF, at=pre_off + 4096)
            bank_state["pool"] = [0, 1, 2, 3]
            for hp in range(4):
                wb = hp % 2
                W = WR[wb]
                kW = wr_keys[wb]
                if hp + 1 < 4:
                    wr_keys[1 - wb] = WDMA(WR[1 - wb], wret_d[hp + 1], ("WR", 1 - wb), 4)
                else:
                    kWAV = WDMA(WAV, wav_d, "WAV", 1)
                    wa_keys = {0: WDMA(WA0, watt_d[0], ("WA", 0), 2)}
                stopif("S1w")
                h0, h1 = 2 * hp, 2 * hp + 1
                for tb in range(4):
                    bnk = nb()
                    hk = [("HT", c) for c in range(1 + 4 * tb, 5 + 4 * tb)]
                    for kc in range(8):
                        MM(PS[bnk][:, :], W[:, kc, 0:128], HT[:, kc, 128 + tb * 512:128 + (tb + 1) * 512],
                           kc == 0, kc == 7, kW + hk, [bk(bnk)])
                    CP("act", QT[:, tb * 512:(tb + 1) * 512], PS[bnk][:, :], [bk(bnk)], [("QT", tb)])
                    TT("dve", QDT[0:64, 0, tb * 512:(tb + 1) * 512].rearrange("p (a t) -> p a t", a=4),
                       PS[bnk][0:64, :].rearrange("p (a t) -> p a t", a=4),
                       QD[0:64, h0:h0 + 1, :].to_broadcast([64, 4, 128]), ALU.mult, [bk(bnk), "QD"], [("QDT", 0, tb, "o")])
                    TT("dve", QDT[64:128, 1, tb * 512:(tb + 1) * 512].rearrange("p (a t) -> p a t", a=4),
                       PS[bnk][64:128, :].rearrange("p (a t) -> p a t", a=4),
                       QD[64:128, h1:h1 + 1, :].to_broadcast([64, 4, 128]), ALU.mult, [bk(bnk), "QD"], [("QDT", 1, tb, "o")])
                    for ts_ in ([tb - 1] if tb > 0 else []) + ([3] if tb == 3 else []):
                        bs = nb()
                        MM(PS[bs][:, :], SWP, QT[:, ts_ * 512:(ts_ + 1) * 512], True, True, [("QT", ts_), "SWP"], [bk(bs)])
                        TT("dve", QDT[0:64, 1, ts_ * 512:(ts_ + 1) * 512].rearrange("p (a t) -> p a t", a=4),
                           PS[bs][0:64, :].rearrange("p (a t) -> p a t", a=4),
                           QD[0:64, h1:h1 + 1, :].to_broadcast([64, 4, 128]), ALU.mult, [bk(bs), "QD"], [("QDT", 1, ts_, "s")])
                        TT("dve", QDT[64:128, 0, ts_ * 512:(ts_ + 1) * 512].rearrange("p (a t) -> p a t", a=4),
                           PS[bs][64:128, :].rearrange("p (a t) -> p a t", a=4),
                           QD[64:128, h0:h0 + 1, :].to_broadcast([64, 4, 128]), ALU.mult, [bk(bs), "QD"], [("QDT", 0, ts_, "s")])
                stopif("S1a")
                for tb in range(5):
                    n_ = 512 if tb < 4 else 128
                    c0 = tb * 512
                    bnk = nb()
                    hk = [("HT", c) for c in range(4 * tb, min(4 * tb + 4, NT))]
                    for kc in range(8):
                        MM(PS[bnk][:, 0:n_], W[:, kc, 128:256], HT[:, kc, c0:c0 + n_], kc == 0, kc == 7, kW + hk, [bk(bnk)])
                    ACT(KTt[:, c0:c0 + n_], PS[bnk][:, 0:n_], AF.Copy, [bk(bnk)], [("KT", tb)], scale=0.125)
                stopif("S1b")
                for c in range(NT):
                    bt = nb()
                    TR(PSB[bt][:, 0:128], KTt[:, c * 128:(c + 1) * 128], [("KT", c // 4)], [bk(bt)])
                    bnk = nb()
                    for kc in range(8):
                        MM(PS[bnk][:, 0:256], HT[:, kc, c * 128:(c + 1) * 128], W[:, kc, 256:512], kc == 0, kc == 7,
                           kW + [("HT", c)], [bk(bnk)])
                    for dr in range(2):
                        TT("dve", KDEC[:, c, :, dr * 64:(dr + 1) * 64], PSB[bt][:, 0:128].rearrange("p (h d) -> p h d", h=2),
                           KD[:, dr, 2 * hp:2 * hp + 2].unsqueeze(2).to_broadcast([128, 2, 64]), ALU.mult,
                           [bk(bt), "KD"], [("KDEC", c, dr)])
                    CP("act", V[:, c, :], PS[bnk][:, 0:256], [bk(bnk)], [("V", c)])
                stopif("S1c")
                MS("pool", SST[0:64, 0], 0.0, [("SST", 0, 0, 0), ("SST", 0, 0, 1)])
                MS("pool", SBF[0:64, 0], 0.0, [("SBF", 0, 0)])
                MS("pool", SST[64:128, 0], 0.0, [("SST", 1, 0, 0), ("SST", 1, 0, 1)])
                MS("pool", SBF[64:128, 16], 0.0, [("SBF", 1, 16)])
                def scan_step(dr, j, hp=hp):
                    lo = dr * 64
                    c = j if dr == 0 else 16 - j
                    cn = c + 1 if dr == 0 else c - 1
                    bnk = nb()
                    for h in range(2):
                        MM(PS[bnk][:, h * 128:(h + 1) * 128], KDEC[:, c, h, :], V[:, c, h * 128:(h + 1) * 128], True, True,
                           [("KDEC", c, 0), ("KDEC", c, 1), ("V", c)], [bk(bnk)])
                    for h in range(2):
                        hh = 2 * hp + h
                        STT("dve", SST[lo:lo + 64, (j + 1) % 2, h, :], SST[lo:lo + 64, j % 2, h, :], CD[lo:lo + 64, hh:hh + 1],
                            PS[bnk][lo:lo + 64, h * 128:(h + 1) * 128], ALU.mult, ALU.add,
                            [("SST", dr, j % 2, h), bk(bnk), "CD"], [("SST", dr, (j + 1) % 2, h)])
                    CP("act", SBF[lo:lo + 64, cn], SST[lo:lo + 64, (j + 1) % 2],
                       [("SST", dr, (j + 1) % 2, 0), ("SST", dr, (j + 1) % 2, 1)], [("SBF", dr, cn)])

                def gate_group(c, hp=hp, W=W, kW=kW):
                    bnk = nb()
                    for kc in range(8):
                        MM(PS[bnk][:, 0:256], HT[:, kc, c * 128:(c + 1) * 128], W[:, kc, 512:768], kc == 0, kc == 7,
                           kW + [("HT", c)], [bk(bnk)])
                    ACT(SGT[:, c % 2, :], PS[bnk][:, 0:256], AF.Silu, [bk(bnk)], [("SGT", c % 2)])
                    TT("pool", SG[:, c - 1, :], SGT[:, c % 2, :], gng[:, hp * 256:(hp + 1) * 256], ALU.mult,
                       [("SGT", c % 2), "gng"], [("SG", c - 1)])

                for j in range(16):
                    scan_step(0, j)
                    scan_step(1, j)
                    gate_group(j + 1)
                stopif("S1d")
                stopif("S1e1")

                def e_A(j, hp=hp):
                    c = j + 1
                    par = c % 2
                    bA, bB = nbA(), nbB()
                    MM(PS[bA][:, 0:128], KTt[0:64, c * 128:(c + 1) * 128], QT[0:64, (c - 1) * 128:c * 128], True, True,
                       [("KT", c // 4), ("QT", (c - 1) // 4)], [bk(bA)])
                    MM(PS[bB][:, 0:128], KTt[64:128, c * 128:(c + 1) * 128], QT[64:128, (c - 1) * 128:c * 128], True, True,
                       [("KT", c // 4), ("QT", (c - 1) // 4)], [bk(bB)])
                    TT("dve", STt[:, par, 0, :], PS[bA][:, 0:128], DT[:, 2 * hp, :], ALU.mult, [bk(bA), "DT"], [("ST", par, 0)])
                    TT("dve", STt[:, par, 1, :], PS[bB][:, 0:128], DT[:, 2 * hp + 1, :], ALU.mult, [bk(bB), "DT"], [("ST", par, 1)])

                def e_B1(j, hp=hp):
                    c = j + 1
                    par = c % 2
                    o3 = c % 4
                    bo = 2 + (c % 2)
                    for h in range(2):
                        MM(PS[bo][:, h * 128:(h + 1) * 128], STt[:, par, h, :], V[:, c, h * 128:(h + 1) * 128], True, False,
                           [("ST", par, h), ("V", c)], [bk(bo)])
                        MM(PS[bo][:, h * 128:(h + 1) * 128], QDT[:, h, (c - 1) * 128:c * 128], SBF[:, c, h, :], False, True,
                           [("QDT", h, (c - 1) // 4, "o"), ("QDT", h, (c - 1) // 4, "s"), ("SBF", 0, c), ("SBF", 1, c)], [bk(bo)])
                    CP("act", OB[:, o3, :], PS[bo][:, 0:256], [bk(bo)], [("OB", o3)])

                def e_B1b(j, hp=hp):
                    c = j + 1
                    par = c % 2
                    o3, m4 = c % 4, c % 4
                    for h in range(2):
                        P.op("dve", lambda e, o_=BST[:, par, h, :], i_=OB[:, o3, h * 128:(h + 1) * 128]: e.bn_stats(out=o_, in_=i_),
                             [("OB", o3)], [("BST", par, h)])
                        P.op("dve", lambda e, o_=MV[:, m4, h, :], i_=BST[:, par, h, :]: e.bn_aggr(out=o_, in_=i_),
                             [("BST", par, h)], [("MV", m4, h)])
                    TS("dve", VE[:, par, :], MV[:, m4, :, 1], 1e-5, None, ALU.add, None, [("MV", m4, 0), ("MV", m4, 1)], [("VE", par)])

                def e_S(j, hp=hp):
                    c = j + 1
                    par = c % 2
                    ACT(VS[:, par, :], VE[:, par, :], AF.Sqrt, [("VE", par)], [("VS", par)])

                def e_B2(j, hp=hp):
                    c = j + 1
                    par = c % 2
                    o3, m4 = c % 4, c % 4
                    RC(RS[:, par, :], VS[:, par, :], [("VS", par)], [("RS", par)])
                    STT("dve", NMR[:, par, :], MV[:, m4, :, 0], -1.0, RS[:, par, :], ALU.mult, ALU.mult,
                        [("MV", m4, 0), ("MV", m4, 1), ("RS", par)], [("NMR", par)])
                    for h in range(2):
                        ACT(YF[:, par, h * 128:(h + 1) * 128], OB[:, o3, h * 128:(h + 1) * 128], AF.Identity,
                            [("OB", o3), ("RS", par), ("NMR", par)], [("YF", par, h)],
                            scale=RS[:, par, h:h + 1], bias=NMR[:, par, h:h + 1])
                    TT("pool", YB[:, c % 3, :], YF[:, par, :], SG[:, c - 1, :], ALU.mult,
                       [("YF", par, 0), ("YF", par, 1), ("SG", c - 1)], [("YB", c % 3)])

                def e_C(j, hp=hp):
                    c = j + 1
                    par = c % 2
                    bt = c % 2
                    for h in range(2):
                        TR(PSB[bt][:, h * 128:(h + 1) * 128], YB[:, c % 3, h * 128:(h + 1) * 128], [("YB", c % 3)], [bk(bt)])
                    CP("act", RET[:, 2 * hp:2 * hp + 2, (c - 1) * 128:c * 128],
                       PSB[bt][:, 0:256].rearrange("p (h t) -> p h t", h=2), [bk(bt)], [("RET", hp, c)])

                pipeline(16, [(0, e_A), (1, e_B1), (2, e_B1b), (3, e_S), (4, e_B2), (6, e_C)])
            A.off = m_s
            WA = [WA0, A.alloc([128, 8, 384], BF)]
            WM0 = A.alloc([128, 8, 512], BF, at=m_s + 40 * 1024)
            P.barrier()
            if dbg and b == 0:
                DMA("pool", dbg_d["d_ret"], RET.rearrange("p k t -> p (k t)"), (), (), "dbg")
            stopif("S1")

            VA = A.alloc([128, NT, 4, 80], BF)
            AQ = A.alloc([128, 2, SEQ], BF)
            AK = A.alloc([128, LP], BF)
            PT = A.alloc([128, 2, 4, 2, 2, 128], BF)
            DEN = A.alloc([128, 2, 4], F32)
            RDEN = A.alloc([128, 2, 4], F32)
            ATM = A.alloc([128, 2, 4, 64], BF)
            MS("pool", VA, 1.0, ["VA1"])
            for c in range(NT):
                bnk = nb()
                for kc in range(8):
                    MM(PS[bnk][:, 0:256], HT[:, kc, c * 128:(c + 1) * 128], WAV[:, kc, :], kc == 0, kc == 7, kWAV + [("HT", c)], [bk(bnk)])
                CP("act" if c % 2 else "dve", VA[:, c, :, 0:64], PS[bnk][:, 0:256].rearrange("p (g d) -> p g d", g=4),
                   [bk(bnk), "VA1"], [("VA", c)])
            mstate = {"c": 0}
            for kh in range(4):
                wb = kh % 2
                W = WA[wb]
                kW = wa_keys[wb]
                if kh + 1 < 4:
                    wa_keys[1 - wb] = WDMA(WA[1 - wb], watt_d[kh + 1], ("WA", 1 - wb), 2)
                else:
                    assert A.off <= m_s + 40 * 1024, A.off - m_s
                    wm_keys = {0: WDMA(WM0, wmrg_d[0], ("WM", 0), 2)}
                for pr in range(2):
                    for tb in range(4):
                        bnk = nb()
                        hk = [("HT", c) for c in range(1 + 4 * tb, 5 + 4 * tb)]
                        for kc in range(8):
                            MM(PS[bnk][:, :], W[:, kc, pr * 128:(pr + 1) * 128], HT[:, kc, 128 + tb * 512:128 + (tb + 1) * 512],
                               kc == 0, kc == 7, kW + hk, [bk(bnk)])
                        CP("act" if tb % 2 else "dve", AQ[:, pr, tb * 512:(tb + 1) * 512], PS[bnk][:, :], [bk(bnk)], [("AQ", pr, tb)])
                for tb in range(5):
                    n_ = 512 if tb < 4 else 128
                    c0 = tb * 512
                    bnk = nb()
                    hk = [("HT", c) for c in range(4 * tb, min(4 * tb + 4, NT))]
                    for kc in range(8):
                        MM(PS[bnk][:, 0:n_], W[:, kc, 256:384], HT[:, kc, c0:c0 + n_], kc == 0, kc == 7, kW + hk, [bk(bnk)])
                    CP("act" if tb % 2 else "dve", AK[:, c0:c0 + n_], PS[bnk][:, 0:n_], [bk(bnk)], [("AK", tb)])
                def blocks_of(n):
                    return [(0, None)] + [(bb, bb - (n - 1)) for bb in (n - 1, n, n + 1) if 1 <= bb <= 16]

                def a_A(j, kh=kh, W=W, kW=kW):
                    n = j + 1
                    par = n % 2
                    blocks = blocks_of(n)
                    qk = [("AQ", 0, (n - 1) // 4), ("AQ", 1, (n - 1) // 4)]
                    for r0 in range(0, len(blocks), 2):
                        rb = blocks[r0:r0 + 2]
                        bA, bB = nbA(), nbB()
                        for j_, (bb, o_) in enumerate(rb):
                            col = j_ * 256
                            MM(PS[bA][:, col:col + 256].rearrange("p (g t) -> p g t", g=2), AK[0:64, bb * 128:(bb + 1) * 128],
                               AQ[0:64, :, (n - 1) * 128:n * 128], True, True, [("AK", bb // 4)] + qk, [bk(bA)])
                            MM(PS[bB][:, col:col + 256].rearrange("p (g t) -> p g t", g=2), AK[64:128, bb * 128:(bb + 1) * 128],
                               AQ[64:128, :, (n - 1) * 128:n * 128], True, True, [("AK", bb // 4)] + qk, [bk(bB)])
                        nr = len(rb)
                        for two, bX in ((0, bA), (1, bB)):
                            ACT(PT[:, par, r0:r0 + nr, two, :, :], PS[bX][:, 0:nr * 256].rearrange("p (j g t) -> p j g t", j=nr, g=2),
                                AF.Exp, [bk(bX)], [("PT", par, r0 + j2, two) for j2 in range(nr)], scale=0.125)
                        for j_, (bb, o_) in enumerate(rb):
                            si = r0 + j_
                            msk = MM4 if o_ is None else EM[:, o_, kh]
                            mstate["c"] += 1
                            TT("dve", PT[:, par, si], PT[:, par, si], msk, ALU.mult,
                               [("PT", par, si, 0), ("PT", par, si, 1), "EM", "MM4"], [("PT", par, si, 0), ("PT", par, si, 1)])

                def a_B(j, kh=kh):
                    n = j + 1
                    par = n % 2
                    blocks = blocks_of(n)
                    bv = nb()
                    pvbank[n] = bv
                    nbk = len(blocks)
                    for g in range(4):
                        for si, (bb, o_) in enumerate(blocks):
                            MM(PS[bv][:, g * 128:g * 128 + 65], PT[:, par, si, g % 2, g // 2, :], VA[:, bb, kh, 0:65],
                               si == 0, si == nbk - 1, [("PT", par, si, g % 2), ("VA", bb)], [bk(bv)])
                    pv = PS[bv][:, :].rearrange("p (g t) -> p g t", g=4)
                    TT("dve", DEN[:, par, :], pv[:, :, 64], ES[:, 4 * kh:4 * kh + 4], ALU.add, [bk(bv), "ES"], [("DEN", par)])
                    RC(RDEN[:, par, :], DEN[:, par, :], [("DEN", par)], [("RDEN", par)])
                    TT("dve", ATM[:, par], pv[:, :, 0:64], RDEN[:, par, :].unsqueeze(2).to_broadcast([128, 4, 64]), ALU.mult,
                       [bk(bv), ("RDEN", par)], [("ATM", par)])

                def a_C(j, kh=kh):
                    n = j + 1
                    par = n % 2
                    bt = nb()
                    atf = ATM[:, par].rearrange("p g d -> p (g d)")
                    for pr in range(2):
                        TR(PSB[bt][:, pr * 128:(pr + 1) * 128], atf[:, pr * 128:(pr + 1) * 128], [("ATM", par)], [bk(bt)])
                    CP("dve", ATT[:, 2 * kh:2 * kh + 2, (n - 1) * 128:n * 128],
                       PSB[bt][:, 0:256].rearrange("p (h t) -> p h t", h=2), [bk(bt)], [("ATT", kh, n)])

                pvbank = {}
                pipeline(16, [(0, a_A), (1, a_B), (2, a_C)])
            A.off = m_s
            MG = A.alloc([128, 8, SEQ], BF)
            m_mg = A.off
            WM1 = A.alloc([128, 8, 512], BF)
            assert A.off <= m_s + 40 * 1024
            A.off = m_s + 48 * 1024
            WM = [WM0, WM1]
            P.barrier()
            if dbg and b == 0:
                DMA("pool", dbg_d["d_att"], ATT.rearrange("p k t -> p (k t)"), (), (), "dbg")
            stopif("S2")

            m_s3 = A.off
            SGA = A.alloc([128, 2, 512], F32)
            SGB = A.alloc([128, 2, 512], F32)
            T1 = A.alloc([128, 2, 512], F32)
            T2 = A.alloc([128, 2, 512], F32)
            bank_state["pool"] = [0, 1, 2, 3]
            it = 0
            for m in range(8):
                wb = m % 2
                W = WM[wb]
                kW = wm_keys[wb]
                if m + 1 < 8:
                    wm_keys[1 - wb] = WDMA(WM[1 - wb], wmrg_d[m + 1], ("WM", 1 - wb), 2)
                for tb in range(4):
                    it += 1
                    par = it % 2
                    bs = [nb() for _ in range(4)]
                    srcs = (HT[:, :, 128 + tb * 512:128 + (tb + 1) * 512], HT[:, :, 128 + tb * 512:128 + (tb + 1) * 512],
                            RET[:, :, tb * 512:(tb + 1) * 512], ATT[:, :, tb * 512:(tb + 1) * 512])
                    for q_ in range(4):
                        for kc in range(8):
                            MM(PS[bs[q_]][:, :], W[:, kc, q_ * 128:(q_ + 1) * 128], srcs[q_][:, kc, :], kc == 0, kc == 7, kW, [bk(bs[q_])])
                    stopif("S3a")
                    ACT(SGA[:, par, :], PS[bs[0]][:, :], AF.Sigmoid, [bk(bs[0])], [("SGA", par)])
                    ACT(SGB[:, par, :], PS[bs[1]][:, :], AF.Sigmoid, [bk(bs[1])], [("SGB", par)])
                    TT("dve", T1[:, par, :], PS[bs[2]][:, :], SGA[:, par, :], ALU.mult, [bk(bs[2]), ("SGA", par)], [("T1", par)])
                    TT("dve", T2[:, par, :], PS[bs[3]][:, :], SGB[:, par, :], ALU.mult, [bk(bs[3]), ("SGB", par)], [("T2", par)])
                    stopif("S3c")
                    TT("pool", MG[:, m, tb * 512:(tb + 1) * 512], T1[:, par, :], T2[:, par, :], ALU.add,
                       [("T1", par), ("T2", par)], [("MG", m, tb)])
                    stopif("S3d")
            A.off = m_mg
            WO = A.alloc([128, 8, D], BF)
            P.barrier()
            kWO = WDMA(WO, wout_d, "WO", 4)
            if dbg and b == 0:
                DMA("pool", dbg_d["d_mg"], MG.rearrange("p k t -> p (k t)"), (), (), "dbg")
            stopif("S3")

            XR = [A.alloc([128, D], F32) for _ in range(2)]
            WG01 = [A.alloc([128, 8, 256], BF, at=m_s + 64 * 1024 + i_ * 4096) for i_ in range(2)]
            wg_keys = {0: WDMA(WG01[0], wgu_d[0], ("WG", 0), 2), 1: WDMA(WG01[1], wgu_d[1], ("WG", 1), 2)}
            H2 = [A.alloc([128, D], BF) for _ in range(2)]
            JK = [A.alloc([128, D], BF) for _ in range(2)]
            assert A.off <= m_s + 64 * 1024 and m_s + 72 * 1024 <= ARENA_BYTES, (A.off - m_s, m_s)
            def s4_A(t):
                par = t % 2
                kx = ("XR", par)
                DMA("sp", XR[par], x_d[b, t * 128:(t + 1) * 128, :], (), [kx], kx)
                for hf in range(2):
                    bnk = nb()
                    for m in range(8):
                        MM(PS[bnk][:, :], MG[:, m, t * 128:(t + 1) * 128], WO[:, m, hf * 512:(hf + 1) * 512], m == 0, m == 7, kWO, [bk(bnk)])
                    TT("dve", X1[:, t, hf * 512:(hf + 1) * 512], PS[bnk][:, :], XR[par][:, hf * 512:(hf + 1) * 512], ALU.add,
                       [bk(bnk), kx], [("X1", t, hf)])
                ACT(JK[t % 2], X1[:, t, :], AF.Square, [("X1", t, 0), ("X1", t, 1)], [("ss", t), ("JK", t % 2)], accum_out=SS[:, t:t + 1])

            def s4_B1(t):
                TS("dve", SS[:, 20 + t:21 + t], SS[:, t:t + 1], 1.0 / D, 1e-6, ALU.mult, ALU.add, [("ss", t)], [("v1", t)])
                ACT(SS[:, 40 + t:41 + t], SS[:, 20 + t:21 + t], AF.Sqrt, [("v1", t)], [("v2", t)])

            def s4_B2(t):
                par = t % 2
                RC(SS[:, 20 + t:21 + t], SS[:, 40 + t:41 + t], [("v2", t)], [("rs", t)])
                STT("dve", H2[par], X1[:, t, :], SS[:, 20 + t:21 + t], gffn, ALU.mult, ALU.mult,
                    [("X1", t, 0), ("X1", t, 1), ("rs", t), "gffn"], [("H2", par)])

            def s4_C(t):
                par = t % 2
                bnk = nb()
                for k in range(8):
                    TR(PSB[bnk][:, k * 128:(k + 1) * 128], H2[par][:, k * 128:(k + 1) * 128], [("H2", par)], [bk(bnk)])
                CP("act", HT[:, :, t * 128:(t + 1) * 128], PSB[bnk][:, :].rearrange("p (k t) -> p k t", k=8), [bk(bnk)], [("HT", t)])

            pipeline(16, [(0, s4_A), (1, s4_B1), (2, s4_B2), (3, s4_C)])
            A.off = m_s
            AT = A.alloc([128, 6, SEQ], BF)
            WD = A.alloc([128, 6, D], BF)
            WG = [WG01[0], WG01[1], A.alloc([128, 8, 256], BF)]
            P.barrier()
            if dbg and b == 0:
                DMA("sp", dbg_d["d_x1"], X1.rearrange("p k t -> p (k t)"), (), (), "dbg")
            stopif("S4")

            SA = A.alloc([128, 2, 512], F32)
            OT = [A.alloc([128, D], F32) for _ in range(2)]
            JK = [A.alloc([128, D], BF) for _ in range(2)]
            assert A.off <= m_s + 64 * 1024
            it = 0
            for qi, (f0, f1) in enumerate(FQ):
                nf = f1 - f0
                kWD = WDMA(WD[:, 0:nf, :], wd_d[:, f0 * D:f1 * D], "WD", 2)
                for f in range(f0, f1):
                    wb = f % 3
                    W = WG[wb]
                    kW = wg_keys[wb]
                    if f + 2 < 22:
                        w2 = (f + 2) % 3
                        wg_keys[w2] = WDMA(WG[w2], wgu_d[f + 2], ("WG", w2), 2)
                    for tb in range(4):
                        it += 1
                        par = it % 2
                        ba_, bb_ = nb(), nb()
                        hk = [("HT", c) for c in range(4 * tb, 4 * tb + 4)]
                        for kc in range(8):
                            MM(PS[ba_][:, :], W[:, kc, 0:128], HT[:, kc, tb * 512:(tb + 1) * 512], kc == 0, kc == 7, kW + hk, [bk(ba_)])
                        for kc in range(8):
                            MM(PS[bb_][:, :], W[:, kc, 128:256], HT[:, kc, tb * 512:(tb + 1) * 512], kc == 0, kc == 7, kW + hk, [bk(bb_)])
                        ACT(SA[:, par, :], PS[ba_][:, :], AF.Silu, [bk(ba_)], [("SA", par)])
                        TT("dve", AT[:, f - f0, tb * 512:(tb + 1) * 512], PS[bb_][:, :], SA[:, par, :], ALU.mult,
                           [bk(bb_), ("SA", par)], [("AT", f - f0, tb)])
                last = qi == len(FQ) - 1

                def d_A(t, nf=nf, kWD=kWD, last=last):
                    ak_ = [("AT", fl, t // 4) for fl in range(nf)]
                    for hf in range(2):
                        bnk = nb()
                        for fl in range(nf):
                            MM(PS[bnk][:, :], AT[:, fl, t * 128:(t + 1) * 128], WD[:, fl, hf * 512:(hf + 1) * 512], fl == 0, fl == nf - 1,
                               kWD + ak_, [bk(bnk)])
                        TT("dve", X1[:, t, hf * 512:(hf + 1) * 512], PS[bnk][:, :], X1[:, t, hf * 512:(hf + 1) * 512], ALU.add,
                           [bk(bnk), ("X1", t, hf)], [("X1", t, hf)])
                    if last:
                        ACT(JK[t % 2], X1[:, t, :], AF.Square, [("X1", t, 0), ("X1", t, 1)], [("ss", t), ("JK", t % 2)],
                            accum_out=SS[:, t:t + 1])

                def d_B1(t):
                    TS("dve", SS[:, 20 + t:21 + t], SS[:, t:t + 1], 1.0 / D, 1e-6, ALU.mult, ALU.add, [("ss", t)], [("v1", t)])
                    ACT(SS[:, 40 + t:41 + t], SS[:, 20 + t:21 + t], AF.Sqrt, [("v1", t)], [("v2", t)])

                def d_B2(t):
                    par = t % 2
                    RC(SS[:, 20 + t:21 + t], SS[:, 40 + t:41 + t], [("v2", t)], [("rs", t)])
                    STT("dve", OT[par], X1[:, t, :], SS[:, 20 + t:21 + t], gfin, ALU.mult, ALU.mult,
                        [("X1", t, 0), ("X1", t, 1), ("rs", t), "gfin"], [("OT", par)])
                    DMA("sp", y_d[b, t * 128:(t + 1) * 128, :], OT[par], [("OT", par)], (), ("OT", par))

                if last:
                    pipeline(16, [(0, d_A), (1, d_B1), (2, d_B2)])
                else:
                    pipeline(16, [(0, d_A)])
            A.off = m_s
            P.barrier()

        for b_ in range(NSEQ):
            try:
                seq_body(b_)
            except _Stop:
                break
        P.barrier()
        P.plan(nc, st)
        with nc.Block() as block:
            P.emit(nc, block)
    return nc, P, A


def _tileK(W):
    K, N = W.shape
    return np.ascontiguousarray(W.reshape(K // 128, 128, N).transpose(1, 0, 2)).reshape(128, (K // 128) * N)


def _tables():
    i = np.arange(128, dtype=np.float32)
    r1 = np.maximum(i[None, :] - i[:, None], 0.0)
    r2 = np.maximum(i[:, None] - i[None, :], 0.0)
    iq = np.concatenate([np.tile((i + 1.0)[None], (64, 1)), np.tile((128.0 - i)[None], (64, 1))], 0)
    kt = np.stack([127.0 - i, i], 1)
    slopes = 2.0 ** (-8.0 * np.arange(1, 17, dtype=np.float64) / 16.0)
    s = np.arange(128)[:, None].astype(np.float64)
    t = np.arange(128)[None, :].astype(np.float64)
    d0 = t - s + 128.0
    d1 = np.abs(t - s)
    d2 = s + 128.0 - t
    em = np.zeros((128, 3, 4, 2, 2, 128), np.float64)
    for h in range(16):
        kh, g = h // 4, h % 4
        two, gp = g % 2, g // 2
        em[:, 0, kh, two, gp, :] = np.where(d0 <= 128, np.exp(-slopes[h] * d0), 0.0)
        em[:, 1, kh, two, gp, :] = np.exp(-slopes[h] * d1)
        em[:, 2, kh, two, gp, :] = np.where(d2 <= 128, np.exp(-slopes[h] * d2), 0.0)
    mm4 = np.zeros((128, 512), np.float32)
    mm4[112:, :] = 1.0
    swp = np.zeros((128, 128), np.float32)
    swp[np.arange(128), (np.arange(128) + 64) % 128] = 1.0
    return dict(ident=np.eye(128, dtype=np.float32), r1=r1.astype(np.float32), r2=r2.astype(np.float32),
                iq=iq.astype(np.float32), kt=kt.astype(np.float32),
                em=em.reshape(128, -1).astype(np.float32), mm4=mm4, swp=swp)


def _prep_shared(meta_tokens, w_in, ret_decay_logit_fwd, ret_decay_logit_bwd, ret_gn_gain, attn_sink,
                 w_branch_ret, w_branch_att, w_out, norm_mix, norm_ffn, w_gate_up, w_down, norm_final):
    f = np.float32
    wi = np.asarray(w_in[0], f)
    wbr = np.asarray(w_branch_ret[0], f)
    wba = np.asarray(w_branch_att[0], f)
    wgu = np.asarray(w_gate_up[0], f)
    wret = []
    for hp in range(4):
        h0, h1 = 2 * hp, 2 * hp + 1
        q0 = wi[:, h0 * 64:(h0 + 1) * 64]
        q1 = wi[:, h1 * 64:(h1 + 1) * 64]
        cols = [q0, q1, wi[:, 512 + hp * 128:512 + (hp + 1) * 128],
                wi[:, 1024 + hp * 256:1024 + (hp + 1) * 256], wi[:, 2048 + hp * 256:2048 + (hp + 1) * 256]]
        wret.append(_tileK(np.concatenate(cols, 1)))
    watt = []
    for kh in range(4):
        k_ = wi[:, 4096 + kh * 64:4096 + (kh + 1) * 64]
        watt.append(_tileK(np.concatenate([wi[:, 3072 + kh * 256:3072 + (kh + 1) * 256], k_, k_], 1)))
    wmrg = []
    for m in range(8):
        sl = slice(m * 128, (m + 1) * 128)
        wmrg.append(_tileK(np.concatenate([wi[:, 4608:5632][:, sl], wi[:, 5632:6656][:, sl], wbr[:, sl], wba[:, sl]], 1)))
    wgut = []
    for ff in range(22):
        wgut.append(_tileK(np.concatenate([wgu[:, ff * 128:(ff + 1) * 128], wgu[:, 2816 + ff * 128:2816 + (ff + 1) * 128]], 1)))
    d = dict(meta=np.asarray(meta_tokens, f), wret=np.stack(wret), watt=np.stack(watt), wav=_tileK(wi[:, 4352:4608]),
             wmrg=np.stack(wmrg), wout=_tileK(np.asarray(w_out[0], f)), wgu=np.stack(wgut), wd=_tileK(np.asarray(w_down[0], f)),
             lgf=np.asarray(ret_decay_logit_fwd[0], f), lgb=np.asarray(ret_decay_logit_bwd[0], f),
             gn=np.asarray(ret_gn_gain[0], f), sink=np.asarray(attn_sink[0], f), gmix=np.asarray(norm_mix[0], f),
             gffn=np.asarray(norm_ffn[0], f), gfin=np.asarray(norm_final, f))
    d.update(_tables())
    return {k: np.ascontiguousarray(v) for k, v in d.items()}


_CACHE = {}


def kernel(x, meta_tokens, w_in, ret_decay_logit_fwd, ret_decay_logit_bwd, ret_gn_gain, attn_sink,
           w_branch_ret, w_branch_att, w_out, norm_mix, norm_ffn, w_gate_up, w_down, norm_final):
    x = np.asarray(x, np.float32)
    shared = _prep_shared(meta_tokens, w_in, ret_decay_logit_fwd, ret_decay_logit_bwd, ret_gn_gain, attn_sink,
                          w_branch_ret, w_branch_att, w_out, norm_mix, norm_ffn, w_gate_up, w_down, norm_final)
    if "nc" not in _CACHE:
        _CACHE["nc"] = build()[0]
    nc = _CACHE["nc"]
    n = 8
    in_maps = []
    for c in range(n):
        m = dict(shared)
        m["x"] = np.ascontiguousarray(x[NSEQ * c:NSEQ * (c + 1)])
        in_maps.append(m)
    res = run_bass_kernel_spmd(nc, in_maps, core_ids=list(range(n)))
    return np.concatenate([np.asarray(r["y"], np.float32) for r in res.results], axis=0)
```

```python
import numpy as np
from contextlib import ExitStack
import concourse.bass as bass
import concourse.mybir as mb
from concourse.bass_utils import run_bass_kernel_spmd

F32 = mb.dt.float32
BF = mb.dt.bfloat16
ALU = mb.AluOpType
AF = mb.ActivationFunctionType
ENGS = ("pe", "act", "dve", "pool", "sp")


class _Op:
    __slots__ = ("eng", "fn", "deps", "dma", "sem", "val", "waits", "vc")

    def __init__(self, eng, fn, deps, dma):
        self.eng = eng
        self.fn = fn
        self.deps = deps
        self.dma = dma
        self.sem = None
        self.val = 0
        self.waits = ()
        self.vc = None


class Prog:
    CH = 8000

    def __init__(self):
        self.ops = []
        self.last_w = {}
        self.rd = {}
        self.last_eng = {}
        self.open_dma = []

    def op(self, eng, fn, r=(), w=(), dma=None):
        i = len(self.ops)
        deps = {}
        for k in r:
            lw = self.last_w.get(k)
            if lw is not None:
                deps[lw] = "raw"
            if eng != "pe" and isinstance(k, tuple) and k[0] == "ps":
                rr = self.rd.get(k)
                if rr:
                    for e2, x in rr[0].items():
                        if e2 != eng:
                            deps.setdefault(x, "xr")
        for k in w:
            lw = self.last_w.get(k)
            if lw is not None:
                deps.setdefault(lw, "waw")
            rr = self.rd.get(k)
            if rr:
                for x in rr[0].values():
                    deps.setdefault(x, "war")
                for x in rr[1]:
                    deps.setdefault(x, "war")
        final = []
        for d, kind in deps.items():
            od = self.ops[d]
            if od.eng == eng and od.dma is None and dma is None:
                if eng == "pe" or kind == "war":
                    continue
            final.append(d)
        self.ops.append(_Op(eng, fn, final, dma))
        for k in r:
            rr = self.rd.get(k)
            if rr is None:
                rr = self.rd[k] = ({}, [])
            if dma is None:
                rr[0][eng] = i
            else:
                rr[1].append(i)
        for k in w:
            self.last_w[k] = i
            self.rd[k] = ({}, [])
        if dma is None:
            self.last_eng[eng] = i
        else:
            self.open_dma.append(i)
        return i

    def barrier(self):
        deps = list(self.last_eng.values()) + list(self.open_dma)
        for e in ENGS:
            self.ops.append(_Op(e, None, list(deps), None))
        self.last_w = {}
        self.rd = {}
        self.open_dma = []

    def plan(self, nc, stack):
        ops = self.ops
        has_dep = [False] * len(ops)
        for o in ops:
            for d in o.deps:
                has_dep[d] = True
        eng_cnt = {e: 0 for e in ENGS}
        dma_cnt = {}
        semnames = {}
        for i, o in enumerate(ops):
            if o.fn is None:
                continue
            if o.dma is not None:
                c = dma_cnt.get(o.dma, 0) + 1
                dma_cnt[o.dma] = c
                o.sem = ("dma", o.dma)
                o.val = 16 * c
                semnames[o.sem] = None
            elif has_dep[i]:
                c = eng_cnt[o.eng]
                eng_cnt[o.eng] = c + 1
                o.sem = ("eng", o.eng, c // self.CH)
                o.val = c % self.CH + 1
                semnames[o.sem] = None
        K = {e: {} for e in ENGS}
        for o in ops:
            Ke = K[o.eng]
            waits = {}
            for d in sorted(o.deps):
                od = ops[d]
                sm, v = od.sem, od.val
                if Ke.get(sm, 0) >= v:
                    continue
                if waits.get(sm, 0) < v:
                    waits[sm] = v
                for s2, v2 in od.vc.items():
                    if Ke.get(s2, 0) < v2:
                        Ke[s2] = v2
            o.waits = list(waits.items())
            if o.sem is not None:
                vc = dict(Ke)
                vc[o.sem] = o.val
                if o.dma is None:
                    for ep in range(o.sem[2]):
                        vc[("eng", o.eng, ep)] = self.CH
                o.vc = vc
        self.semobj = {}
        for j, sm in enumerate(semnames):
            self.semobj[sm] = stack.enter_context(nc.semaphore("s%d" % j))

    def emit(self, nc, block):
        ops = self.ops
        semobj = self.semobj
        per = {e: [o for o in ops if o.eng == e] for e in ENGS}
        reg = {"pe": block.tensor, "act": block.scalar, "dve": block.vector,
               "pool": block.gpsimd, "sp": block.sync}

        def mk(lst):
            def body(e):
                for o in lst:
                    for sm, v in o.waits:
                        e.wait_ge(semobj[sm], v)
                    if o.fn is not None:
                        ins = o.fn(e)
                        if o.sem is not None:
                            ins.then_inc(semobj[o.sem], 16 if o.dma is not None else 1)
            return body

        for e in ENGS:
            if per[e]:
                reg[e](mk(per[e]))


class _Stop(Exception):
    pass


class Arena:
    def __init__(self, hf32, nbytes):
        self.hf = hf32
        self.hb = hf32.bitcast(BF)
        self.n = nbytes
        self.off = 0
        self.peak = 0

    def alloc(self, shape, dt, at=None):
        es = 4 if dt == F32 else 2
        nel = 1
        for s in shape[1:]:
            nel *= s
        if at is not None:
            off = at
        else:
            off = (self.off + 63) // 64 * 64
            self.off = off + nel * es
            self.peak = max(self.peak, self.off)
            assert self.off <= self.n, ("arena overflow", self.off, self.n)
        h = self.hf if dt == F32 else self.hb
        ap = h[:, off // es: off // es + nel]
        fd = shape[1:]
        if len(fd) > 1:
            names = "abcde"[:len(fd)]
            pat = "p (%s) -> p %s" % (" ".join(names), " ".join(names))
            ap = ap.rearrange(pat, **{n_: s for n_, s in zip(names, fd)})
        return ap


D = 1024
SEQ = 2048
NSEQ = 2
NT = 17
LP = NT * 128
FQ = ((0, 6), (6, 12), (12, 17), (17, 22))
ARENA_BYTES = 207 * 1024


def build(dbg=False, upto=None):
    nc = bass.Bass("TRN2", target_bir_lowering=False)

    def din(name, shape):
        return nc.dram_tensor(name, shape, F32, kind="ExternalInput").ap()

    x_d = din("x", [NSEQ, SEQ, D])
    meta_d = din("meta", [16, D])
    wret_d = din("wret", [4, 128, 8 * 768])
    watt_d = din("watt", [4, 128, 8 * 384])
    wav_d = din("wav", [128, 8 * 256])
    wmrg_d = din("wmrg", [8, 128, 8 * 512])
    wout_d = din("wout", [128, 8 * 1024])
    wgu_d = din("wgu", [22, 128, 8 * 256])
    wd_d = din("wd", [128, 22 * 1024])
    lgf_d = din("lgf", [8])
    lgb_d = din("lgb", [8])
    gn_d = din("gn", [D])
    sink_d = din("sink", [16])
    gmix_d = din("gmix", [D])
    gffn_d = din("gffn", [D])
    gfin_d = din("gfin", [D])
    ident_d = din("ident", [128, 128])
    r1_d = din("r1", [128, 128])
    r2_d = din("r2", [128, 128])
    iq_d = din("iq", [128, 128])
    kt_d = din("kt", [128, 2])
    em_d = din("em", [128, 48 * 128])
    mm_d = din("mm4", [128, 512])
    swp_d = din("swp", [128, 128])
    y_d = nc.dram_tensor("y", [NSEQ, SEQ, D], F32, kind="ExternalOutput").ap()
    dbg_d = {}
    if dbg:
        for nm, shp in (("d_ht", [128, 8 * LP]), ("d_ret", [128, 8 * SEQ]), ("d_att", [128, 8 * SEQ]),
                        ("d_mg", [128, 8 * SEQ]), ("d_x1", [128, 16 * D])):
            dbg_d[nm] = nc.dram_tensor(nm, shp, F32, kind="ExternalOutput").ap()

    P = Prog()
    with ExitStack() as st:
        arena_t = st.enter_context(nc.sbuf_tensor("arena", [128, ARENA_BYTES // 4], F32))
        A = Arena(arena_t, ARENA_BYTES)
        PS = [st.enter_context(nc.psum_tensor("ps%d" % i, [128, 512], F32)) for i in range(8)]
        PSB = [p.bitcast(BF) for p in PS]

        def MM(out, lhsT, rhs, start, stop, r, w):
            P.op("pe", lambda e: e.matmul(out, lhsT=lhsT, rhs=rhs, start=start, stop=stop), r, w)

        def TR(out, in_, r, w):
            P.op("pe", lambda e: e.transpose(out=out, in_=in_, identity=ident), r, w)

        def ACT(out, in_, func, r, w, **kw):
            P.op("act", lambda e: e.activation(out=out, in_=in_, func=func, **kw), r, w)

        def TT(eng, out, a, b, op, r, w):
            P.op(eng, lambda e: e.tensor_tensor(out=out, in0=a, in1=b, op=op), r, w)

        def TS(eng, out, a, s1, s2, op0, op1, r, w):
            if s2 is None:
                P.op(eng, lambda e: e.tensor_scalar(out=out, in0=a, scalar1=s1, scalar2=None, op0=op0), r, w)
            else:
                P.op(eng, lambda e: e.tensor_scalar(out=out, in0=a, scalar1=s1, scalar2=s2, op0=op0, op1=op1), r, w)

        def STT(eng, out, a, s, b, op0, op1, r, w):
            P.op(eng, lambda e: e.scalar_tensor_tensor(out=out, in0=a, scalar=s, in1=b, op0=op0, op1=op1), r, w)

        def CP(eng, out, in_, r, w):
            if eng == "act":
                ACT(out, in_, AF.Copy, r, w)
            else:
                P.op(eng, lambda e: e.tensor_copy(out=out, in_=in_), r, w)

        def MS(eng, ap, val, w):
            P.op(eng, lambda e: e.memset(ap, val), (), w)

        def RC(out, in_, r, w):
            P.op("dve", lambda e: e.reciprocal(out=out, in_=in_), r, w)

        def DMA(eng, out, in_, r, w, sem):
            P.op(eng, lambda e: e.dma_start(out=out, in_=in_), r, w, dma=sem)

        def WDMA(dst, src2d, key, nsplit):
            K_ = dst.shape[1]
            N_ = dst.shape[2]
            step = (K_ + nsplit - 1) // nsplit
            keys = []
            for pi, k0 in enumerate(range(0, K_, step)):
                k1 = min(K_, k0 + step)
                kk = (key, pi)
                DMA("pool", dst[:, k0:k1, :], src2d[:, k0 * N_:k1 * N_].rearrange("p (k n) -> p k n", k=k1 - k0), (), [kk], kk)
                keys.append(kk)
            return keys

        bank_state = {"pool": [0, 1, 2, 3], "i": 0, "a": 0, "b": 0}

        def nb():
            p = bank_state["pool"]
            b = p[bank_state["i"] % len(p)]
            bank_state["i"] += 1
            return b

        def nbA():
            bank_state["a"] += 1
            return 4 + bank_state["a"] % 2

        def nbB():
            bank_state["b"] += 1
            return 6 + bank_state["b"] % 2

        def bk(b):
            return ("ps", b)

        def pipeline(n, phases):
            mx = max(sk for sk, _ in phases)
            for i in range(n + mx):
                for sk, fn in phases:
                    j = i - sk
                    if 0 <= j < n:
                        fn(j)

        ident = A.alloc([128, 128], BF)
        SWP = A.alloc([128, 128], BF)
        DT = A.alloc([128, 8, 128], BF)
        QD = A.alloc([128, 8, 128], F32)
        KD = A.alloc([128, 2, 8], F32)
        CD = A.alloc([128, 8], F32)
        SC = A.alloc([128, 8], F32)
        LG = A.alloc([128, 16], F32)
        ES = A.alloc([128, 16], F32)
        KT = A.alloc([128, 2], F32)
        EM = A.alloc([128, 3, 4, 2, 2, 128], BF)
        MM4 = A.alloc([128, 2, 2, 128], BF)
        gmix = A.alloc([128, D], F32)
        gffn = A.alloc([128, D], F32)
        gfin = A.alloc([128, D], F32)
        gng = A.alloc([128, D], F32)
        SS = A.alloc([128, 64], F32)
        HT = A.alloc([128, 8, LP], BF)
        X1 = A.alloc([128, 16, D], F32)
        off_x1 = A.off - 16 * D * 4
        RETATT = A.hb[:, off_x1 // 2: off_x1 // 2 + 16 * SEQ].rearrange("p (a b) -> p a b", a=16)
        RET = RETATT[:, 0:8, :]
        ATT = RETATT[:, 8:16, :]
        base_mark = A.off

        m0 = A.off
        R1 = A.alloc([128, 128], F32)
        R2 = A.alloc([128, 128], F32)
        IQ = A.alloc([128, 128], F32)
        TMPa = A.alloc([128, 8, 128], F32)
        TMPb = A.alloc([128, 8, 128], F32)
        LGt = A.alloc([128, 16], F32)
        for (dst, src, nm) in ((R1, r1_d, "R1"), (R2, r2_d, "R2"), (IQ, iq_d, "IQ"), (KT, kt_d, "KT")):
            DMA("sp", dst, src, (), [nm], nm)
        DMA("pool", ident, ident_d, (), ["ident"], "ident")
        DMA("pool", SWP, swp_d, (), ["SWP"], "SWP")
        DMA("pool", EM.rearrange("p a b c d e -> p (a b c d e)"), em_d, (), ["EM"], "EM")
        DMA("pool", MM4.rearrange("p a b c -> p (a b c)"), mm_d, (), ["MM4"], "MM4")
        for (dst, src, nm) in ((gmix, gmix_d, "gmix"), (gffn, gffn_d, "gffn"), (gfin, gfin_d, "gfin"), (gng, gn_d, "gng")):
            DMA("sp", dst, src.partition_broadcast(128), (), [nm], nm)
        DMA("sp", LG[:, 0:8], lgf_d.partition_broadcast(128), (), ["LGa"], "LGa")
        DMA("sp", LG[:, 8:16], lgb_d.partition_broadcast(128), (), ["LGb"], "LGb")
        DMA("sp", ES, sink_d.partition_broadcast(128), (), ["ESr"], "ESr")
        ACT(LGt, LG, AF.Exp, ["LGa", "LGb"], ["LGt"], scale=-1.0)
        TS("dve", LGt, LGt, 1.0, None, ALU.add, None, ["LGt"], ["LGt2"])
        ACT(LG, LGt, AF.Ln, ["LGt2"], ["LG1"])
        TS("dve", LG, LG, -1.0, None, ALU.mult, None, ["LG1"], ["LG"])
        CP("dve", SC[0:64, :], LG[0:64, 0:8], ["LG"], ["SCa"])
        CP("dve", SC[64:128, :], LG[64:128, 8:16], ["LG"], ["SCb"])
        ACT(ES, ES, AF.Exp, ["ESr"], ["ES"])
        ACT(CD, SC, AF.Exp, ["SCa", "SCb"], ["CD"], scale=128.0)
        ACT(KD[:, 0, :], LG[:, 0:8], AF.Exp, ["LG", "KT"], ["KDa"], scale=KT[:, 0:1])
        ACT(KD[:, 1, :], LG[:, 8:16], AF.Exp, ["LG", "KT"], ["KDb"], scale=KT[:, 1:2])
        TS("dve", KD, KD, 1.0, None, ALU.mult, None, ["KDa", "KDb"], ["KD"])
        for h in range(8):
            TS("dve", TMPa[:, h, :], R1, LG[:, h:h + 1], None, ALU.mult, None, ["R1", "LG"], [("TMPa", h)])
            STT("dve", TMPb[:, h, :], R2, LG[:, 8 + h:9 + h], TMPa[:, h, :], ALU.mult, ALU.add,
                ["R2", "LG", ("TMPa", h)], [("TMPb", h)])
            ACT(DT[:, h, :], TMPb[:, h, :], AF.Exp, [("TMPb", h)], ["DT"])
            ACT(QD[:, h, :], IQ, AF.Exp, ["IQ", "SCa", "SCb"], ["QD"], scale=SC[:, h:h + 1])
        P.barrier()
        A.off = m0

        def stopif(nm):
            if upto == nm:
                raise _Stop()

        def seq_body(b):
            m_s = A.off
            off_att = off_x1 + 8 * SEQ * 2
            WR = [A.alloc([128, 8, 768], BF, at=off_att + i_ * 8 * 768 * 2) for i_ in range(2)]
            wr_keys = {0: WDMA(WR[0], wret_d[0], ("WR", 0), 4)}
            XT = [A.alloc([128, D], F32) for _ in range(4)]
            HN = [A.alloc([128, D], BF) for _ in range(2)]
            JK = [A.alloc([128, D], BF) for _ in range(2)]
            bank_state["pool"] = [0, 1, 2, 3]

            def s0_A(c):
                xt = XT[c % 4]
                kx = ("xt", c % 4)
                if c == 0:
                    MS("pool", xt, 0.0, [kx])
                    DMA("sp", xt[112:128, :], meta_d, (), [kx], kx)
                else:
                    DMA("sp", xt, x_d[b, (c - 1) * 128:c * 128, :], (), [kx], kx)
                ACT(JK[c % 2], xt, AF.Square, [kx], [("ss", c), ("JK", c % 2)], accum_out=SS[:, c:c + 1])

            def s0_B1(c):
                TS("dve", SS[:, 20 + c:21 + c], SS[:, c:c + 1], 1.0 / D, 1e-6, ALU.mult, ALU.add, [("ss", c)], [("v1", c)])
                ACT(SS[:, 40 + c:41 + c], SS[:, 20 + c:21 + c], AF.Sqrt, [("v1", c)], [("v2", c)])

            def s0_B2(c):
                xt = XT[c % 4]
                kx = ("xt", c % 4)
                RC(SS[:, 20 + c:21 + c], SS[:, 40 + c:41 + c], [("v2", c)], [("rs", c)])
                STT("dve", HN[c % 2], xt, SS[:, 20 + c:21 + c], gmix, ALU.mult, ALU.mult, [kx, ("rs", c), "gmix"], [("hn", c % 2)])

            def s0_C(c):
                hn = HN[c % 2]
                bnk = nb()
                for k in range(8):
                    TR(PSB[bnk][:, k * 128:(k + 1) * 128], hn[:, k * 128:(k + 1) * 128], [("hn", c % 2)], [bk(bnk)])
                CP("act", HT[:, :, c * 128:(c + 1) * 128],
                   PSB[bnk][:, :].rearrange("p (k t) -> p k t", k=8), [bk(bnk)], [("HT", c)])

            pipeline(NT, [(0, s0_A), (1, s0_B1), (2, s0_B2), (3, s0_C)])
            A.off = m_s
            P.barrier()
            if dbg and b == 0:
                DMA("pool", dbg_d["d_ht"], HT.rearrange("p k t -> p (k t)"), [("HT", 0)], (), "dbg")
            stopif("S0")

            QT = A.alloc([128, SEQ], BF)
            QDT = A.alloc([128, 2, SEQ], BF)
            KTt = A.alloc([128, LP], BF)
            KDEC = A.alloc([128, NT, 2, 128], BF)
            V = A.alloc([128, NT, 256], BF)
            SST = A.alloc([128, 2, 2, 128], F32)
            SBF = A.alloc([128, NT, 2, 128], BF)
            SGT = A.alloc([128, 2, 256], F32)
            SG = A.alloc([128, 16, 256], BF)
            STt = A.alloc([128, 2, 2, 128], BF)
            BST = A.alloc([128, 2, 2, 6], F32)
            MV = A.alloc([128, 4, 2, 2], F32)
            VE = A.alloc([128, 2, 2], F32)
            VS = A.alloc([128, 2, 2], F32)
            RS = A.alloc([128, 2, 2], F32)
            YF = A.alloc([128, 2, 256], F32)
            YB = A.alloc([128, 3, 256], BF)
            NMR = A.alloc([128, 2, 2], F32)
            OB = A.alloc([128, 4, 256], F32)
            pre_off = (A.off + 63) // 64 * 64
            assert pre_off >= m_s + 64 * 1024 - 16 * 1024 and pre_off + 10240 <= ARENA_BYTES, (pre_off, m_s)
            WAV = A.alloc([128, 8, 256], BF, at=pre_off)
            WA0 = A.alloc([128, 8, 384], BF, at=pre_off + 4096)
            bank_state["pool"] = [0, 1, 2, 3]
            for hp in range(4):
                wb = hp % 2
                W = WR[wb]
                kW = wr_keys[wb]
                if hp + 1 < 4:
                    wr_keys[1 - wb] = WDMA(WR[1 - wb], wret_d[hp + 1], ("WR", 1 - wb), 4)
                else:
                    kWAV = WDMA(WAV, wav_d, "WAV", 1)
                    wa_keys = {0: WDMA(WA0, watt_d[0], ("WA", 0), 2)}
                stopif("S1w")
                h0, h1 = 2 * hp, 2 * hp + 1
                for tb in range(4):
                    bnk = nb()
                    hk = [("HT", c) for c in range(1 + 4 * tb, 5 + 4 * tb)]
                    for kc in range(8):
                        MM(PS[bnk][:, :], W[:, kc, 0:128], HT[:, kc, 128 + tb * 512:128 + (tb + 1) * 512],
                           kc == 0, kc == 7, kW + hk, [bk(bnk)])
                    CP("act", QT[:, tb * 512:(tb + 1) * 512], PS[bnk][:, :], [bk(bnk)], [("QT", tb)])
                    TT("dve", QDT[0:64, 0, tb * 512:(tb + 1) * 512].rearrange("p (a t) -> p a t", a=4),
                       PS[bnk][0:64, :].rearrange("p (a t) -> p a t", a=4),
                       QD[0:64, h0:h0 + 1, :].to_broadcast([64, 4, 128]), ALU.mult, [bk(bnk), "QD"], [("QDT", 0, tb, "o")])
                    TT("dve", QDT[64:128, 1, tb * 512:(tb + 1) * 512].rearrange("p (a t) -> p a t", a=4),
                       PS[bnk][64:128, :].rearrange("p (a t) -> p a t", a=4),
                       QD[64:128, h1:h1 + 1, :].to_broadcast([64, 4, 128]), ALU.mult, [bk(bnk), "QD"], [("QDT", 1, tb, "o")])
                    for ts_ in ([tb - 1] if tb > 0 else []) + ([3] if tb == 3 else []):
                        bs = nb()
                        MM(PS[bs][:, :], SWP, QT[:, ts_ * 512:(ts_ + 1) * 512], True, True, [("QT", ts_), "SWP"], [bk(bs)])
                        TT("dve", QDT[0:64, 1, ts_ * 512:(ts_ + 1) * 512].rearrange("p (a t) -> p a t", a=4),
                           PS[bs][0:64, :].rearrange("p (a t) -> p a t", a=4),
                           QD[0:64, h1:h1 + 1, :].to_broadcast([64, 4, 128]), ALU.mult, [bk(bs), "QD"], [("QDT", 1, ts_, "s")])
                        TT("dve", QDT[64:128, 0, ts_ * 512:(ts_ + 1) * 512].rearrange("p (a t) -> p a t", a=4),
                           PS[bs][64:128, :].rearrange("p (a t) -> p a t", a=4),
                           QD[64:128, h0:h0 + 1, :].to_broadcast([64, 4, 128]), ALU.mult, [bk(bs), "QD"], [("QDT", 0, ts_, "s")])
                stopif("S1a")
                for tb in range(5):
                    n_ = 512 if tb < 4 else 128
                    c0 = tb * 512
                    bnk = nb()
                    hk = [("HT", c) for c in range(4 * tb, min(4 * tb + 4, NT))]
                    for kc in range(8):
                        MM(PS[bnk][:, 0:n_], W[:, kc, 128:256], HT[:, kc, c0:c0 + n_], kc == 0, kc == 7, kW + hk, [bk(bnk)])
                    ACT(KTt[:, c0:c0 + n_], PS[bnk][:, 0:n_], AF.Copy, [bk(bnk)], [("KT", tb)], scale=0.125)
                stopif("S1b")
                for c in range(NT):
                    bt = nb()
                    TR(PSB[bt][:, 0:128], KTt[:, c * 128:(c + 1) * 128], [("KT", c // 4)], [bk(bt)])
                    bnk = nb()
                    for kc in range(8):
                        MM(PS[bnk][:, 0:256], HT[:, kc, c * 128:(c + 1) * 128], W[:, kc, 256:512], kc == 0, kc == 7,
                           kW + [("HT", c)], [bk(bnk)])
                    for dr in range(2):
                        TT("dve", KDEC[:, c, :, dr * 64:(dr + 1) * 64], PSB[bt][:, 0:128].rearrange("p (h d) -> p h d", h=2),
                           KD[:, dr, 2 * hp:2 * hp + 2].unsqueeze(2).to_broadcast([128, 2, 64]), ALU.mult,
                           [bk(bt), "KD"], [("KDEC", c, dr)])
                    CP("act", V[:, c, :], PS[bnk][:, 0:256], [bk(bnk)], [("V", c)])
                stopif("S1c")
                MS("pool", SST[0:64, 0], 0.0, [("SST", 0, 0, 0), ("SST", 0, 0, 1)])
                MS("pool", SBF[0:64, 0], 0.0, [("SBF", 0, 0)])
                MS("pool", SST[64:128, 0], 0.0, [("SST", 1, 0, 0), ("SST", 1, 0, 1)])
                MS("pool", SBF[64:128, 16], 0.0, [("SBF", 1, 16)])
                def scan_step(dr, j, hp=hp):
                    lo = dr * 64
                    c = j if dr == 0 else 16 - j
                    cn = c + 1 if dr == 0 else c - 1
                    bnk = nb()
                    for h in range(2):
                        MM(PS[bnk][:, h * 128:(h + 1) * 128], KDEC[:, c, h, :], V[:, c, h * 128:(h + 1) * 128], True, True,
                           [("KDEC", c, 0), ("KDEC", c, 1), ("V", c)], [bk(bnk)])
                    for h in range(2):
                        hh = 2 * hp + h
                        STT("dve", SST[lo:lo + 64, (j + 1) % 2, h, :], SST[lo:lo + 64, j % 2, h, :], CD[lo:lo + 64, hh:hh + 1],
                            PS[bnk][lo:lo + 64, h * 128:(h + 1) * 128], ALU.mult, ALU.add,
                            [("SST", dr, j % 2, h), bk(bnk), "CD"], [("SST", dr, (j + 1) % 2, h)])
                    CP("act", SBF[lo:lo + 64, cn], SST[lo:lo + 64, (j + 1) % 2],
                       [("SST", dr, (j + 1) % 2, 0), ("SST", dr, (j + 1) % 2, 1)], [("SBF", dr, cn)])

                def gate_group(c, hp=hp, W=W, kW=kW):
                    bnk = nb()
                    for kc in range(8):
                        MM(PS[bnk][:, 0:256], HT[:, kc, c * 128:(c + 1) * 128], W[:, kc, 512:768], kc == 0, kc == 7,
                           kW + [("HT", c)], [bk(bnk)])
                    ACT(SGT[:, c % 2, :], PS[bnk][:, 0:256], AF.Silu, [bk(bnk)], [("SGT", c % 2)])
                    TT("pool", SG[:, c - 1, :], SGT[:, c % 2, :], gng[:, hp * 256:(hp + 1) * 256], ALU.mult,
                       [("SGT", c % 2), "gng"], [("SG", c - 1)])

                for j in range(16):
                    scan_step(0, j)
                    scan_step(1, j)
                    gate_group(j + 1)
                stopif("S1d")
                stopif("S1e1")

                def e_A(j, hp=hp):
                    c = j + 1
                    par = c % 2
                    bA, bB = nbA(), nbB()
                    MM(PS[bA][:, 0:128], KTt[0:64, c * 128:(c + 1) * 128], QT[0:64, (c - 1) * 128:c * 128], True, True,
                       [("KT", c // 4), ("QT", (c - 1) // 4)], [bk(bA)])
                    MM(PS[bB][:, 0:128], KTt[64:128, c * 128:(c + 1) * 128], QT[64:128, (c - 1) * 128:c * 128], True, True,
                       [("KT", c // 4), ("QT", (c - 1) // 4)], [bk(bB)])
                    TT("dve", STt[:, par, 0, :], PS[bA][:, 0:128], DT[:, 2 * hp, :], ALU.mult, [bk(bA), "DT"], [("ST", par, 0)])
                    TT("dve", STt[:, par, 1, :], PS[bB][:, 0:128], DT[:, 2 * hp + 1, :], ALU.mult, [bk(bB), "DT"], [("ST", par, 1)])

                def e_B1(j, hp=hp):
                    c = j + 1
                    par = c % 2
                    o3 = c % 4
                    bo = 2 + (c % 2)
                    for h in range(2):
                        MM(PS[bo][:, h * 128:(h + 1) * 128], STt[:, par, h, :], V[:, c, h * 128:(h + 1) * 128], True, False,
                           [("ST", par, h), ("V", c)], [bk(bo)])
                        MM(PS[bo][:, h * 128:(h + 1) * 128], QDT[:, h, (c - 1) * 128:c * 128], SBF[:, c, h, :], False, True,
                           [("QDT", h, (c - 1) // 4, "o"), ("QDT", h, (c - 1) // 4, "s"), ("SBF", 0, c), ("SBF", 1, c)], [bk(bo)])
                    CP("act", OB[:, o3, :], PS[bo][:, 0:256], [bk(bo)], [("OB", o3)])

                def e_B1b(j, hp=hp):
                    c = j + 1
                    par = c % 2
                    o3, m4 = c % 4, c % 4
                    for h in range(2):
                        P.op("dve", lambda e, o_=BST[:, par, h, :], i_=OB[:, o3, h * 128:(h + 1) * 128]: e.bn_stats(out=o_, in_=i_),
                             [("OB", o3)], [("BST", par, h)])
                        P.op("dve", lambda e, o_=MV[:, m4, h, :], i_=BST[:, par, h, :]: e.bn_aggr(out=o_, in_=i_),
                             [("BST", par, h)], [("MV", m4, h)])
                    TS("dve", VE[:, par, :], MV[:, m4, :, 1], 1e-5, None, ALU.add, None, [("MV", m4, 0), ("MV", m4, 1)], [("VE", par)])

                def e_S(j, hp=hp):
                    c = j + 1
                    par = c % 2
                    ACT(VS[:, par, :], VE[:, par, :], AF.Sqrt, [("VE", par)], [("VS", par)])

                def e_B2(j, hp=hp):
                    c = j + 1
                    par = c % 2
                    o3, m4 = c % 4, c % 4
                    RC(RS[:, par, :], VS[:, par, :], [("VS", par)], [("RS", par)])
                    STT("dve", NMR[:, par, :], MV[:, m4, :, 0], -1.0, RS[:, par, :], ALU.mult, ALU.mult,
                        [("MV", m4, 0), ("MV", m4, 1), ("RS", par)], [("NMR", par)])
                    for h in range(2):
                        ACT(YF[:, par, h * 128:(h + 1) * 128], OB[:, o3, h * 128:(h + 1) * 128], AF.Identity,
                            [("OB", o3), ("RS", par), ("NMR", par)], [("YF", par, h)],
                            scale=RS[:, par, h:h + 1], bias=NMR[:, par, h:h + 1])
                    TT("pool", YB[:, c % 3, :], YF[:, par, :], SG[:, c - 1, :], ALU.mult,
                       [("YF", par, 0), ("YF", par, 1), ("SG", c - 1)], [("YB", c % 3)])

                def e_C(j, hp=hp):
                    c = j + 1
                    par = c % 2
                    bt = c % 2
                    for h in range(2):
                        TR(PSB[bt][:, h * 128:(h + 1) * 128], YB[:, c % 3, h * 128:(h + 1) * 128], [("YB", c % 3)], [bk(bt)])
                    CP("act", RET[:, 2 * hp:2 * hp + 2, (c - 1) * 128:c * 128],
                       PSB[bt][:, 0:256].rearrange("p (h t) -> p h t", h=2), [bk(bt)], [("RET", hp, c)])

                pipeline(16, [(0, e_A), (1, e_B1), (2, e_B1b), (3, e_S), (4, e_B2), (6, e_C)])
            A.off = m_s
            WA = [WA0, A.alloc([128, 8, 384], BF)]
            WM0 = A.alloc([128, 8, 512], BF, at=m_s + 40 * 1024)
            P.barrier()
            if dbg and b == 0:
                DMA("pool", dbg_d["d_ret"], RET.rearrange("p k t -> p (k t)"), (), (), "dbg")
            stopif("S1")

            VA = A.alloc([128, NT, 4, 80], BF)
            AQ = A.alloc([128, 2, SEQ], BF)
            AK = A.alloc([128, LP], BF)
            PT = A.alloc([128, 2, 4, 2, 2, 128], BF)
            DEN = A.alloc([128, 2, 4], F32)
            RDEN = A.alloc([128, 2, 4], F32)
            ATM = A.alloc([128, 2, 4, 64], BF)
            MS("pool", VA, 1.0, ["VA1"])
            for c in range(NT):
                bnk = nb()
                for kc in range(8):
                    MM(PS[bnk][:, 0:256], HT[:, kc, c * 128:(c + 1) * 128], WAV[:, kc, :], kc == 0, kc == 7, kWAV + [("HT", c)], [bk(bnk)])
                CP("act" if c % 2 else "dve", VA[:, c, :, 0:64], PS[bnk][:, 0:256].rearrange("p (g d) -> p g d", g=4),
                   [bk(bnk), "VA1"], [("VA", c)])
            mstate = {"c": 0}
            for kh in range(4):
                wb = kh % 2
                W = WA[wb]
                kW = wa_keys[wb]
                if kh + 1 < 4:
                    wa_keys[1 - wb] = WDMA(WA[1 - wb], watt_d[kh + 1], ("WA", 1 - wb), 2)
                else:
                    assert A.off <= m_s + 40 * 1024, A.off - m_s
                    wm_keys = {0: WDMA(WM0, wmrg_d[0], ("WM", 0), 2)}
                for pr in range(2):
                    for tb in range(4):
                        bnk = nb()
                        hk = [("HT", c) for c in range(1 + 4 * tb, 5 + 4 * tb)]
                        for kc in range(8):
                            MM(PS[bnk][:, :], W[:, kc, pr * 128:(pr + 1) * 128], HT[:, kc, 128 + tb * 512:128 + (tb + 1) * 512],
                               kc == 0, kc == 7, kW + hk, [bk(bnk)])
                        CP("act" if tb % 2 else "dve", AQ[:, pr, tb * 512:(tb + 1) * 512], PS[bnk][:, :], [bk(bnk)], [("AQ", pr, tb)])
                for tb in range(5):
                    n_ = 512 if tb < 4 else 128
                    c0 = tb * 512
                    bnk = nb()
                    hk = [("HT", c) for c in range(4 * tb, min(4 * tb + 4, NT))]
                    for kc in range(8):
                        MM(PS[bnk][:, 0:n_], W[:, kc, 256:384], HT[:, kc, c0:c0 + n_], kc == 0, kc == 7, kW + hk, [bk(bnk)])
                    CP("act" if tb % 2 else "dve", AK[:, c0:c0 + n_], PS[bnk][:, 0:n_], [bk(bnk)], [("AK", tb)])
                def blocks_of(n):
                    return [(0, None)] + [(bb, bb - (n - 1)) for bb in (n - 1, n, n + 1) if 1 <= bb <= 16]

                def a_A(j, kh=kh, W=W, kW=kW):
                    n = j + 1
                    par = n % 2
                    blocks = blocks_of(n)
                    qk = [("AQ", 0, (n - 1) // 4), ("AQ", 1, (n - 1) // 4)]
                    for r0 in range(0, len(blocks), 2):
                        rb = blocks[r0:r0 + 2]
                        bA, bB = nbA(), nbB()
                        for j_, (bb, o_) in enumerate(rb):
                            col = j_ * 256
                            MM(PS[bA][:, col:col + 256].rearrange("p (g t) -> p g t", g=2), AK[0:64, bb * 128:(bb + 1) * 128],
                               AQ[0:64, :, (n - 1) * 128:n * 128], True, True, [("AK", bb // 4)] + qk, [bk(bA)])
                            MM(PS[bB][:, col:col + 256].rearrange("p (g t) -> p g t", g=2), AK[64:128, bb * 128:(bb + 1) * 128],
                               AQ[64:128, :, (n - 1) * 128:n * 128], True, True, [("AK", bb // 4)] + qk, [bk(bB)])
                        nr = len(rb)
                        for two, bX in ((0, bA), (1, bB)):
                            ACT(PT[:, par, r0:r0 + nr, two, :, :], PS[bX][:, 0:nr * 256].rearrange("p (j g t) -> p j g t", j=nr, g=2),
                                AF.Exp, [bk(bX)], [("PT", par, r0 + j2, two) for j2 in range(nr)], scale=0.125)
                        for j_, (bb, o_) in enumerate(rb):
                            si = r0 + j_
                            msk = MM4 if o_ is None else EM[:, o_, kh]
                            mstate["c"] += 1
                            TT("dve", PT[:, par, si], PT[:, par, si], msk, ALU.mult,
                               [("PT", par, si, 0), ("PT", par, si, 1), "EM", "MM4"], [("PT", par, si, 0), ("PT", par, si, 1)])

                def a_B(j, kh=kh):
                    n = j + 1
                    par = n % 2
                    blocks = blocks_of(n)
                    bv = nb()
                    pvbank[n] = bv
                    nbk = len(blocks)
                    for g in range(4):
                        for si, (bb, o_) in enumerate(blocks):
                            MM(PS[bv][:, g * 128:g * 128 + 65], PT[:, par, si, g % 2, g // 2, :], VA[:, bb, kh, 0:65],
                               si == 0, si == nbk - 1, [("PT", par, si, g % 2), ("VA", bb)], [bk(bv)])
                    pv = PS[bv][:, :].rearrange("p (g t) -> p g t", g=4)
                    TT("dve", DEN[:, par, :], pv[:, :, 64], ES[:, 4 * kh:4 * kh + 4], ALU.add, [bk(bv), "ES"], [("DEN", par)])
                    RC(RDEN[:, par, :], DEN[:, par, :], [("DEN", par)], [("RDEN", par)])
                    TT("dve", ATM[:, par], pv[:, :, 0:64], RDEN[:, par, :].unsqueeze(2).to_broadcast([128, 4, 64]), ALU.mult,
                       [bk(bv), ("RDEN", par)], [("ATM", par)])

                def a_C(j, kh=kh):
                    n = j + 1
                    par = n % 2
                    bt = nb()
                    atf = ATM[:, par].rearrange("p g d -> p (g d)")
                    for pr in range(2):
                        TR(PSB[bt][:, pr * 128:(pr + 1) * 128], atf[:, pr * 128:(pr + 1) * 128], [("ATM", par)], [bk(bt)])
                    CP("dve", ATT[:, 2 * kh:2 * kh + 2, (n - 1) * 128:n * 128],
                       PSB[bt][:, 0:256].rearrange("p (h t) -> p h t", h=2), [bk(bt)], [("ATT", kh, n)])

                pvbank = {}
                pipeline(16, [(0, a_A), (1, a_B), (2, a_C)])
            A.off = m_s
            MG = A.alloc([128, 8, SEQ], BF)
            m_mg = A.off
            WM1 = A.alloc([128, 8, 512], BF)
            assert A.off <= m_s + 40 * 1024
            A.off = m_s + 48 * 1024
            WM = [WM0, WM1]
            P.barrier()
            if dbg and b == 0:
                DMA("pool", dbg_d["d_att"], ATT.rearrange("p k t -> p (k t)"), (), (), "dbg")
            stopif("S2")

            m_s3 = A.off
            SGA = A.alloc([128, 2, 512], F32)
            SGB = A.alloc([128, 2, 512], F32)
            T1 = A.alloc([128, 2, 512], F32)
            T2 = A.alloc([128, 2, 512], F32)
            bank_state["pool"] = [0, 1, 2, 3]
            it = 0
            for m in range(8):
                wb = m % 2
                W = WM[wb]
                kW = wm_keys[wb]
                if m + 1 < 8:
                    wm_keys[1 - wb] = WDMA(WM[1 - wb], wmrg_d[m + 1], ("WM", 1 - wb), 2)
                for tb in range(4):
                    it += 1
                    par = it % 2
                    bs = [nb() for _ in range(4)]
                    srcs = (HT[:, :, 128 + tb * 512:128 + (tb + 1) * 512], HT[:, :, 128 + tb * 512:128 + (tb + 1) * 512],
                            RET[:, :, tb * 512:(tb + 1) * 512], ATT[:, :, tb * 512:(tb + 1) * 512])
                    for q_ in range(4):
                        for kc in range(8):
                            MM(PS[bs[q_]][:, :], W[:, kc, q_ * 128:(q_ + 1) * 128], srcs[q_][:, kc, :], kc == 0, kc == 7, kW, [bk(bs[q_])])
                    stopif("S3a")
                    ACT(SGA[:, par, :], PS[bs[0]][:, :], AF.Sigmoid, [bk(bs[0])], [("SGA", par)])
                    ACT(SGB[:, par, :], PS[bs[1]][:, :], AF.Sigmoid, [bk(bs[1])], [("SGB", par)])
                    TT("dve", T1[:, par, :], PS[bs[2]][:, :], SGA[:, par, :], ALU.mult, [bk(bs[2]), ("SGA", par)], [("T1", par)])
                    TT("dve", T2[:, par, :], PS[bs[3]][:, :], SGB[:, par, :], ALU.mult, [bk(bs[3]), ("SGB", par)], [("T2", par)])
                    stopif("S3c")
                    TT("pool", MG[:, m, tb * 512:(tb + 1) * 512], T1[:, par, :], T2[:, par, :], ALU.add,
                       [("T1", par), ("T2", par)], [("MG", m, tb)])
                    stopif("S3d")
            A.off = m_mg
            WO = A.alloc([128, 8, D], BF)
            P.barrier()
            kWO = WDMA(WO, wout_d, "WO", 4)
            if dbg and b == 0:
                DMA("pool", dbg_d["d_mg"], MG.rearrange("p k t -> p (k t)"), (), (), "dbg")
            stopif("S3")

            XR = [A.alloc([128, D], F32) for _ in range(2)]
            WG01 = [A.alloc([128, 8, 256], BF, at=m_s + 64 * 1024 + i_ * 4096) for i_ in range(2)]
            wg_keys = {0: WDMA(WG01[0], wgu_d[0], ("WG", 0), 2), 1: WDMA(WG01[1], wgu_d[1], ("WG", 1), 2)}
            H2 = [A.alloc([128, D], BF) for _ in range(2)]
            JK = [A.alloc([128, D], BF) for _ in range(2)]
            assert A.off <= m_s + 64 * 1024 and m_s + 72 * 1024 <= ARENA_BYTES, (A.off - m_s, m_s)
            def s4_A(t):
                par = t % 2
                kx = ("XR", par)
                DMA("sp", XR[par], x_d[b, t * 128:(t + 1) * 128, :], (), [kx], kx)
                for hf in range(2):
                    bnk = nb()
                    for m in range(8):
                        MM(PS[bnk][:, :], MG[:, m, t * 128:(t + 1) * 128], WO[:, m, hf * 512:(hf + 1) * 512], m == 0, m == 7, kWO, [bk(bnk)])
                    TT("dve", X1[:, t, hf * 512:(hf + 1) * 512], PS[bnk][:, :], XR[par][:, hf * 512:(hf + 1) * 512], ALU.add,
                       [bk(bnk), kx], [("X1", t, hf)])
                ACT(JK[t % 2], X1[:, t, :], AF.Square, [("X1", t, 0), ("X1", t, 1)], [("ss", t), ("JK", t % 2)], accum_out=SS[:, t:t + 1])

            def s4_B1(t):
                TS("dve", SS[:, 20 + t:21 + t], SS[:, t:t + 1], 1.0 / D, 1e-6, ALU.mult, ALU.add, [("ss", t)], [("v1", t)])
                ACT(SS[:, 40 + t:41 + t], SS[:, 20 + t:21 + t], AF.Sqrt, [("v1", t)], [("v2", t)])

            def s4_B2(t):
                par = t % 2
                RC(SS[:, 20 + t:21 + t], SS[:, 40 + t:41 + t], [("v2", t)], [("rs", t)])
                STT("dve", H2[par], X1[:, t, :], SS[:, 20 + t:21 + t], gffn, ALU.mult, ALU.mult,
                    [("X1", t, 0), ("X1", t, 1), ("rs", t), "gffn"], [("H2", par)])

            def s4_C(t):
                par = t % 2
                bnk = nb()
                for k in range(8):
                    TR(PSB[bnk][:, k * 128:(k + 1) * 128], H2[par][:, k * 128:(k + 1) * 128], [("H2", par)], [bk(bnk)])
                CP("act", HT[:, :, t * 128:(t + 1) * 128], PSB[bnk][:, :].rearrange("p (k t) -> p k t", k=8), [bk(bnk)], [("HT", t)])

            pipeline(16, [(0, s4_A), (1, s4_B1), (2, s4_B2), (3, s4_C)])
            A.off = m_s
            AT = A.alloc([128, 6, SEQ], BF)
            WD = A.alloc([128, 6, D], BF)
            WG = [WG01[0], WG01[1], A.alloc([128, 8, 256], BF)]
            P.barrier()
            if dbg and b == 0:
                DMA("sp", dbg_d["d_x1"], X1.rearrange("p k t -> p (k t)"), (), (), "dbg")
            stopif("S4")

            SA = A.alloc([128, 2, 512], F32)
            OT = [A.alloc([128, D], F32) for _ in range(2)]
            JK = [A.alloc([128, D], BF) for _ in range(2)]
            assert A.off <= m_s + 64 * 1024
            it = 0
            for qi, (f0, f1) in enumerate(FQ):
                nf = f1 - f0
                kWD = WDMA(WD[:, 0:nf, :], wd_d[:, f0 * D:f1 * D], "WD", 2)
                for f in range(f0, f1):
                    wb = f % 3
                    W = WG[wb]
                    kW = wg_keys[wb]
                    if f + 2 < 22:
                        w2 = (f + 2) % 3
                        wg_keys[w2] = WDMA(WG[w2], wgu_d[f + 2], ("WG", w2), 2)
                    for tb in range(4):
                        it += 1
                        par = it % 2
                        ba_, bb_ = nb(), nb()
                        hk = [("HT", c) for c in range(4 * tb, 4 * tb + 4)]
                        for kc in range(8):
                            MM(PS[ba_][:, :], W[:, kc, 0:128], HT[:, kc, tb * 512:(tb + 1) * 512], kc == 0, kc == 7, kW + hk, [bk(ba_)])
                        for kc in range(8):
                            MM(PS[bb_][:, :], W[:, kc, 128:256], HT[:, kc, tb * 512:(tb + 1) * 512], kc == 0, kc == 7, kW + hk, [bk(bb_)])
                        ACT(SA[:, par, :], PS[ba_][:, :], AF.Silu, [bk(ba_)], [("SA", par)])
                        TT("dve", AT[:, f - f0, tb * 512:(tb + 1) * 512], PS[bb_][:, :], SA[:, par, :], ALU.mult,
                           [bk(bb_), ("SA", par)], [("AT", f - f0, tb)])
                last = qi == len(FQ) - 1

                def d_A(t, nf=nf, kWD=kWD, last=last):
                    ak_ = [("AT", fl, t // 4) for fl in range(nf)]
                    for hf in range(2):
                        bnk = nb()
                        for fl in range(nf):
                            MM(PS[bnk][:, :], AT[:, fl, t * 128:(t + 1) * 128], WD[:, fl, hf * 512:(hf + 1) * 512], fl == 0, fl == nf - 1,
                               kWD + ak_, [bk(bnk)])
                        TT("dve", X1[:, t, hf * 512:(hf + 1) * 512], PS[bnk][:, :], X1[:, t, hf * 512:(hf + 1) * 512], ALU.add,
                           [bk(bnk), ("X1", t, hf)], [("X1", t, hf)])
                    if last:
                        ACT(JK[t % 2], X1[:, t, :], AF.Square, [("X1", t, 0), ("X1", t, 1)], [("ss", t), ("JK", t % 2)],
                            accum_out=SS[:, t:t + 1])

                def d_B1(t):
                    TS("dve", SS[:, 20 + t:21 + t], SS[:, t:t + 1], 1.0 / D, 1e-6, ALU.mult, ALU.add, [("ss", t)], [("v1", t)])
                    ACT(SS[:, 40 + t:41 + t], SS[:, 20 + t:21 + t], AF.Sqrt, [("v1", t)], [("v2", t)])

                def d_B2(t):
                    par = t % 2
                    RC(SS[:, 20 + t:21 + t], SS[:, 40 + t:41 + t], [("v2", t)], [("rs", t)])
                    STT("dve", OT[par], X1[:, t, :], SS[:, 20 + t:21 + t], gfin, ALU.mult, ALU.mult,
                        [("X1", t, 0), ("X1", t, 1), ("rs", t), "gfin"], [("OT", par)])
                    DMA("sp", y_d[b, t * 128:(t + 1) * 128, :], OT[par], [("OT", par)], (), ("OT", par))

                if last:
                    pipeline(16, [(0, d_A), (1, d_B1), (2, d_B2)])
                else:
                    pipeline(16, [(0, d_A)])
            A.off = m_s
            P.barrier()

        for b_ in range(NSEQ):
            try:
                seq_body(b_)
            except _Stop:
                break
        P.barrier()
        P.plan(nc, st)
        with nc.Block() as block:
            P.emit(nc, block)
    return nc, P, A


def _tileK(W):
    K, N = W.shape
    return np.ascontiguousarray(W.reshape(K // 128, 128, N).transpose(1, 0, 2)).reshape(128, (K // 128) * N)


def _tables():
    i = np.arange(128, dtype=np.float32)
    r1 = np.maximum(i[None, :] - i[:, None], 0.0)
    r2 = np.maximum(i[:, None] - i[None, :], 0.0)
    iq = np.concatenate([np.tile((i + 1.0)[None], (64, 1)), np.tile((128.0 - i)[None], (64, 1))], 0)
    kt = np.stack([127.0 - i, i], 1)
    slopes = 2.0 ** (-8.0 * np.arange(1, 17, dtype=np.float64) / 16.0)
    s = np.arange(128)[:, None].astype(np.float64)
    t = np.arange(128)[None, :].astype(np.float64)
    d0 = t - s + 128.0
    d1 = np.abs(t - s)
    d2 = s + 128.0 - t
    em = np.zeros((128, 3, 4, 2, 2, 128), np.float64)
    for h in range(16):
        kh, g = h // 4, h % 4
        two, gp = g % 2, g // 2
        em[:, 0, kh, two, gp, :] = np.where(d0 <= 128, np.exp(-slopes[h] * d0), 0.0)
        em[:, 1, kh, two, gp, :] = np.exp(-slopes[h] * d1)
        em[:, 2, kh, two, gp, :] = np.where(d2 <= 128, np.exp(-slopes[h] * d2), 0.0)
    mm4 = np.zeros((128, 512), np.float32)
    mm4[112:, :] = 1.0
    swp = np.zeros((128, 128), np.float32)
    swp[np.arange(128), (np.arange(128) + 64) % 128] = 1.0
    return dict(ident=np.eye(128, dtype=np.float32), r1=r1.astype(np.float32), r2=r2.astype(np.float32),
                iq=iq.astype(np.float32), kt=kt.astype(np.float32),
                em=em.reshape(128, -1).astype(np.float32), mm4=mm4, swp=swp)


def _prep_shared(meta_tokens, w_in, ret_decay_logit_fwd, ret_decay_logit_bwd, ret_gn_gain, attn_sink,
                 w_branch_ret, w_branch_att, w_out, norm_mix, norm_ffn, w_gate_up, w_down, norm_final):
    f = np.float32
    wi = np.asarray(w_in[0], f)
    wbr = np.asarray(w_branch_ret[0], f)
    wba = np.asarray(w_branch_att[0], f)
    wgu = np.asarray(w_gate_up[0], f)
    wret = []
    for hp in range(4):
        h0, h1 = 2 * hp, 2 * hp + 1
        q0 = wi[:, h0 * 64:(h0 + 1) * 64]
        q1 = wi[:, h1 * 64:(h1 + 1) * 64]
        cols = [q0, q1, wi[:, 512 + hp * 128:512 + (hp + 1) * 128],
                wi[:, 1024 + hp * 256:1024 + (hp + 1) * 256], wi[:, 2048 + hp * 256:2048 + (hp + 1) * 256]]
        wret.append(_tileK(np.concatenate(cols, 1)))
    watt = []
    for kh in range(4):
        k_ = wi[:, 4096 + kh * 64:4096 + (kh + 1) * 64]
        watt.append(_tileK(np.concatenate([wi[:, 3072 + kh * 256:3072 + (kh + 1) * 256], k_, k_], 1)))
    wmrg = []
    for m in range(8):
        sl = slice(m * 128, (m + 1) * 128)
        wmrg.append(_tileK(np.concatenate([wi[:, 4608:5632][:, sl], wi[:, 5632:6656][:, sl], wbr[:, sl], wba[:, sl]], 1)))
    wgut = []
    for ff in range(22):
        wgut.append(_tileK(np.concatenate([wgu[:, ff * 128:(ff + 1) * 128], wgu[:, 2816 + ff * 128:2816 + (ff + 1) * 128]], 1)))
    d = dict(meta=np.asarray(meta_tokens, f), wret=np.stack(wret), watt=np.stack(watt), wav=_tileK(wi[:, 4352:4608]),
             wmrg=np.stack(wmrg), wout=_tileK(np.asarray(w_out[0], f)), wgu=np.stack(wgut), wd=_tileK(np.asarray(w_down[0], f)),
             lgf=np.asarray(ret_decay_logit_fwd[0], f), lgb=np.asarray(ret_decay_logit_bwd[0], f),
             gn=np.asarray(ret_gn_gain[0], f), sink=np.asarray(attn_sink[0], f), gmix=np.asarray(norm_mix[0], f),
             gffn=np.asarray(norm_ffn[0], f), gfin=np.asarray(norm_final, f))
    d.update(_tables())
    return {k: np.ascontiguousarray(v) for k, v in d.items()}


_CACHE = {}


def kernel(x, meta_tokens, w_in, ret_decay_logit_fwd, ret_decay_logit_bwd, ret_gn_gain, attn_sink,
           w_branch_ret, w_branch_att, w_out, norm_mix, norm_ffn, w_gate_up, w_down, norm_final):
    x = np.asarray(x, np.float32)
    shared = _prep_shared(meta_tokens, w_in, ret_decay_logit_fwd, ret_decay_logit_bwd, ret_gn_gain, attn_sink,
                          w_branch_ret, w_branch_att, w_out, norm_mix, norm_ffn, w_gate_up, w_down, norm_final)
    if "nc" not in _CACHE:
        _CACHE["nc"] = build()[0]
    nc = _CACHE["nc"]
    n = 8
    in_maps = []
    for c in range(n):
        m = dict(shared)
        m["x"] = np.ascontiguousarray(x[NSEQ * c:NSEQ * (c + 1)])
        in_maps.append(m)
    res = run_bass_kernel_spmd(nc, in_maps, core_ids=list(range(n)))
    return np.concatenate([np.asarray(r["y"], np.float32) for r in res.results], axis=0)
```

```python
import numpy as np
from contextlib import ExitStack
import concourse.bass as bass
import concourse.mybir as mb
from concourse.bass_utils import run_bass_kernel_spmd

F32 = mb.dt.float32
BF = mb.dt.bfloat16
ALU = mb.AluOpType
AF = mb.ActivationFunctionType
ENGS = ("pe", "act", "dve", "pool", "sp")


class _Op:
    __slots__ = ("eng", "fn", "deps", "dma", "sem", "val", "waits", "vc")

    def __init__(self, eng, fn, deps, dma):
        self.eng = eng
        self.fn = fn
        self.deps = deps
        self.dma = dma
        self.sem = None
        self.val = 0
        self.waits = ()
        self.vc = None


class Prog:
    CH = 8000

    def __init__(self):
        self.ops = []
        self.last_w = {}
        self.rd = {}
        self.last_eng = {}
        self.open_dma = []

    def op(self, eng, fn, r=(), w=(), dma=None):
        i = len(self.ops)
        deps = {}
        for k in r:
            lw = self.last_w.get(k)
            if lw is not None:
                deps[lw] = "raw"
            if eng != "pe" and isinstance(k, tuple) and k[0] == "ps":
                rr = self.rd.get(k)
                if rr:
                    for e2, x in rr[0].items():
                        if e2 != eng:
                            deps.setdefault(x, "xr")
        for k in w:
            lw = self.last_w.get(k)
            if lw is not None:
                deps.setdefault(lw, "waw")
            rr = self.rd.get(k)
            if rr:
                for x in rr[0].values():
                    deps.setdefault(x, "war")
                for x in rr[1]:
                    deps.setdefault(x, "war")
        final = []
        for d, kind in deps.items():
            od = self.ops[d]
            if od.eng == eng and od.dma is None and dma is None:
                if eng == "pe" or kind == "war":
                    continue
            final.append(d)
        self.ops.append(_Op(eng, fn, final, dma))
        for k in r:
            rr = self.rd.get(k)
            if rr is None:
                rr = self.rd[k] = ({}, [])
            if dma is None:
                rr[0][eng] = i
            else:
                rr[1].append(i)
        for k in w:
            self.last_w[k] = i
            self.rd[k] = ({}, [])
        if dma is None:
            self.last_eng[eng] = i
        else:
            self.open_dma.append(i)
        return i

    def barrier(self):
        deps = list(self.last_eng.values()) + list(self.open_dma)
        for e in ENGS:
            self.ops.append(_Op(e, None, list(deps), None))
        self.last_w = {}
        self.rd = {}
        self.open_dma = []

    def plan(self, nc, stack):
        ops = self.ops
        has_dep = [False] * len(ops)
        for o in ops:
            for d in o.deps:
                has_dep[d] = True
        eng_cnt = {e: 0 for e in ENGS}
        dma_cnt = {}
        semnames = {}
        for i, o in enumerate(ops):
            if o.fn is None:
                continue
            if o.dma is not None:
                c = dma_cnt.get(o.dma, 0) + 1
                dma_cnt[o.dma] = c
                o.sem = ("dma", o.dma)
                o.val = 16 * c
                semnames[o.sem] = None
            elif has_dep[i]:
                c = eng_cnt[o.eng]
                eng_cnt[o.eng] = c + 1
                o.sem = ("eng", o.eng, c // self.CH)
                o.val = c % self.CH + 1
                semnames[o.sem] = None
        K = {e: {} for e in ENGS}
        for o in ops:
            Ke = K[o.eng]
            waits = {}
            for d in sorted(o.deps):
                od = ops[d]
                sm, v = od.sem, od.val
                if Ke.get(sm, 0) >= v:
                    continue
                if waits.get(sm, 0) < v:
                    waits[sm] = v
                for s2, v2 in od.vc.items():
                    if Ke.get(s2, 0) < v2:
                        Ke[s2] = v2
            o.waits = list(waits.items())
            if o.sem is not None:
                vc = dict(Ke)
                vc[o.sem] = o.val
                if o.dma is None:
                    for ep in range(o.sem[2]):
                        vc[("eng", o.eng, ep)] = self.CH
                o.vc = vc
        self.semobj = {}
        for j, sm in enumerate(semnames):
            self.semobj[sm] = stack.enter_context(nc.semaphore("s%d" % j))

    def emit(self, nc, block):
        ops = self.ops
        semobj = self.semobj
        per = {e: [o for o in ops if o.eng == e] for e in ENGS}
        reg = {"pe": block.tensor, "act": block.scalar, "dve": block.vector,
               "pool": block.gpsimd, "sp": block.sync}

        def mk(lst):
            def body(e):
                for o in lst:
                    for sm, v in o.waits:
                        e.wait_ge(semobj[sm], v)
                    if o.fn is not None:
                        ins = o.fn(e)
                        if o.sem is not None:
                            ins.then_inc(semobj[o.sem], 16 if o.dma is not None else 1)
            return body

        for e in ENGS:
            if per[e]:
                reg[e](mk(per[e]))


class _Stop(Exception):
    pass


class Arena:
    def __init__(self, hf32, nbytes):
        self.hf = hf32
        self.hb = hf32.bitcast(BF)
        self.n = nbytes
        self.off = 0
        self.peak = 0

    def alloc(self, shape, dt, at=None):
        es = 4 if dt == F32 else 2
        nel = 1
        for s in shape[1:]:
            nel *= s
        if at is not None:
            off = at
        else:
            off = (self.off + 63) // 64 * 64
            self.off = off + nel * es
            self.peak = max(self.peak, self.off)
            assert self.off <= self.n, ("arena overflow", self.off, self.n)
        h = self.hf if dt == F32 else self.hb
        ap = h[:, off // es: off // es + nel]
        fd = shape[1:]
        if len(fd) > 1:
            names = "abcde"[:len(fd)]
            pat = "p (%s) -> p %s" % (" ".join(names), " ".join(names))
            ap = ap.rearrange(pat, **{n_: s for n_, s in zip(names, fd)})
        return ap


D = 1024
SEQ = 2048
NSEQ = 2
NT = 17
LP = NT * 128
FQ = ((0, 6), (6, 12), (12, 17), (17, 22))
ARENA_BYTES = 207 * 1024


def build(dbg=False, upto=None):
    nc = bass.Bass("TRN2", target_bir_lowering=False)

    def din(name, shape):
        return nc.dram_tensor(name, shape, F32, kind="ExternalInput").ap()

    x_d = din("x", [NSEQ, SEQ, D])
    meta_d = din("meta", [16, D])
    wret_d = din("wret", [4, 128, 8 * 768])
    watt_d = din("watt", [4, 128, 8 * 384])
    wav_d = din("wav", [128, 8 * 256])
    wmrg_d = din("wmrg", [8, 128, 8 * 512])
    wout_d = din("wout", [128, 8 * 1024])
    wgu_d = din("wgu", [22, 128, 8 * 256])
    wd_d = din("wd", [128, 22 * 1024])
    lgf_d = din("lgf", [8])
    lgb_d = din("lgb", [8])
    gn_d = din("gn", [D])
    sink_d = din("sink", [16])
    gmix_d = din("gmix", [D])
    gffn_d = din("gffn", [D])
    gfin_d = din("gfin", [D])
    ident_d = din("ident", [128, 128])
    r1_d = din("r1", [128, 128])
    r2_d = din("r2", [128, 128])
    iq_d = din("iq", [128, 128])
    kt_d = din("kt", [128, 2])
    em_d = din("em", [128, 48 * 128])
    mm_d = din("mm4", [128, 512])
    swp_d = din("swp", [128, 128])
    y_d = nc.dram_tensor("y", [NSEQ, SEQ, D], F32, kind="ExternalOutput").ap()
    dbg_d = {}
    if dbg:
        for nm, shp in (("d_ht", [128, 8 * LP]), ("d_ret", [128, 8 * SEQ]), ("d_att", [128, 8 * SEQ]),
                        ("d_mg", [128, 8 * SEQ]), ("d_x1", [128, 16 * D])):
            dbg_d[nm] = nc.dram_tensor(nm, shp, F32, kind="ExternalOutput").ap()

    P = Prog()
    with ExitStack() as st:
        arena_t = st.enter_context(nc.sbuf_tensor("arena", [128, ARENA_BYTES // 4], F32))
        A = Arena(arena_t, ARENA_BYTES)
        PS = [st.enter_context(nc.psum_tensor("ps%d" % i, [128, 512], F32)) for i in range(8)]
        PSB = [p.bitcast(BF) for p in PS]

        def MM(out, lhsT, rhs, start, stop, r, w):
            P.op("pe", lambda e: e.matmul(out, lhsT=lhsT, rhs=rhs, start=start, stop=stop), r, w)

        def TR(out, in_, r, w):
            P.op("pe", lambda e: e.transpose(out=out, in_=in_, identity=ident), r, w)

        def ACT(out, in_, func, r, w, **kw):
            P.op("act", lambda e: e.activation(out=out, in_=in_, func=func, **kw), r, w)

        def TT(eng, out, a, b, op, r, w):
            P.op(eng, lambda e: e.tensor_tensor(out=out, in0=a, in1=b, op=op), r, w)

        def TS(eng, out, a, s1, s2, op0, op1, r, w):
            if s2 is None:
                P.op(eng, lambda e: e.tensor_scalar(out=out, in0=a, scalar1=s1, scalar2=None, op0=op0), r, w)
            else:
                P.op(eng, lambda e: e.tensor_scalar(out=out, in0=a, scalar1=s1, scalar2=s2, op0=op0, op1=op1), r, w)

        def STT(eng, out, a, s, b, op0, op1, r, w):
            P.op(eng, lambda e: e.scalar_tensor_tensor(out=out, in0=a, scalar=s, in1=b, op0=op0, op1=op1), r, w)

        def CP(eng, out, in_, r, w):
            if eng == "act":
                ACT(out, in_, AF.Copy, r, w)
            else:
                P.op(eng, lambda e: e.tensor_copy(out=out, in_=in_), r, w)

        def MS(eng, ap, val, w):
            P.op(eng, lambda e: e.memset(ap, val), (), w)

        def RC(out, in_, r, w):
            P.op("dve", lambda e: e.reciprocal(out=out, in_=in_), r, w)

        def DMA(eng, out, in_, r, w, sem):
            P.op(eng, lambda e: e.dma_start(out=out, in_=in_), r, w, dma=sem)

        def WDMA(dst, src2d, key, nsplit):
            K_ = dst.shape[1]
            N_ = dst.shape[2]
            step = (K_ + nsplit - 1) // nsplit
            keys = []
            for pi, k0 in enumerate(range(0, K_, step)):
                k1 = min(K_, k0 + step)
                kk = (key, pi)
                DMA("pool", dst[:, k0:k1, :], src2d[:, k0 * N_:k1 * N_].rearrange("p (k n) -> p k n", k=k1 - k0), (), [kk], kk)
                keys.append(kk)
            return keys

        bank_state = {"pool": [0, 1, 2, 3], "i": 0, "a": 0, "b": 0}

        def nb():
            p = bank_state["pool"]
            b = p[bank_state["i"] % len(p)]
            bank_state["i"] += 1
            return b

        def nbA():
            bank_state["a"] += 1
            return 4 + bank_state["a"] % 2

        def nbB():
            bank_state["b"] += 1
            return 6 + bank_state["b"] % 2

        def bk(b):
            return ("ps", b)

        def pipeline(n, phases):
            mx = max(sk for sk, _ in phases)
            for i in range(n + mx):
                for sk, fn in phases:
                    j = i - sk
                    if 0 <= j < n:
                        fn(j)

        ident = A.alloc([128, 128], BF)
        SWP = A.alloc([128, 128], BF)
        DT = A.alloc([128, 8, 128], BF)
        QD = A.alloc([128, 8, 128], F32)
        KD = A.alloc([128, 2, 8], F32)
        CD = A.alloc([128, 8], F32)
        SC = A.alloc([128, 8], F32)
        LG = A.alloc([128, 16], F32)
        ES = A.alloc([128, 16], F32)
        KT = A.alloc([128, 2], F32)
        EM = A.alloc([128, 3, 4, 2, 2, 128], BF)
        MM4 = A.alloc([128, 2, 2, 128], BF)
        gmix = A.alloc([128, D], F32)
        gffn = A.alloc([128, D], F32)
        gfin = A.alloc([128, D], F32)
        gng = A.alloc([128, D], F32)
        SS = A.alloc([128, 64], F32)
        HT = A.alloc([128, 8, LP], BF)
        X1 = A.alloc([128, 16, D], F32)
        off_x1 = A.off - 16 * D * 4
        RETATT = A.hb[:, off_x1 // 2: off_x1 // 2 + 16 * SEQ].rearrange("p (a b) -> p a b", a=16)
        RET = RETATT[:, 0:8, :]
        ATT = RETATT[:, 8:16, :]
        base_mark = A.off

        m0 = A.off
        R1 = A.alloc([128, 128], F32)
        R2 = A.alloc([128, 128], F32)
        IQ = A.alloc([128, 128], F32)
        TMPa = A.alloc([128, 8, 128], F32)
        TMPb = A.alloc([128, 8, 128], F32)
        LGt = A.alloc([128, 16], F32)
        for (dst, src, nm) in ((R1, r1_d, "R1"), (R2, r2_d, "R2"), (IQ, iq_d, "IQ"), (KT, kt_d, "KT")):
            DMA("sp", dst, src, (), [nm], nm)
        DMA("pool", ident, ident_d, (), ["ident"], "ident")
        DMA("pool", SWP, swp_d, (), ["SWP"], "SWP")
        DMA("pool", EM.rearrange("p a b c d e -> p (a b c d e)"), em_d, (), ["EM"], "EM")
        DMA("pool", MM4.rearrange("p a b c -> p (a b c)"), mm_d, (), ["MM4"], "MM4")
        for (dst, src, nm) in ((gmix, gmix_d, "gmix"), (gffn, gffn_d, "gffn"), (gfin, gfin_d, "gfin"), (gng, gn_d, "gng")):
            DMA("sp", dst, src.partition_broadcast(128), (), [nm], nm)
        DMA("sp", LG[:, 0:8], lgf_d.partition_broadcast(128), (), ["LGa"], "LGa")
        DMA("sp", LG[:, 8:16], lgb_d.partition_broadcast(128), (), ["LGb"], "LGb")
        DMA("sp", ES, sink_d.partition_broadcast(128), (), ["ESr"], "ESr")
        ACT(LGt, LG, AF.Exp, ["LGa", "LGb"], ["LGt"], scale=-1.0)
        TS("dve", LGt, LGt, 1.0, None, ALU.add, None, ["LGt"], ["LGt2"])
        ACT(LG, LGt, AF.Ln, ["LGt2"], ["LG1"])
        TS("dve", LG, LG, -1.0, None, ALU.mult, None, ["LG1"], ["LG"])
        CP("dve", SC[0:64, :], LG[0:64, 0:8], ["LG"], ["SCa"])
        CP("dve", SC[64:128, :], LG[64:128, 8:16], ["LG"], ["SCb"])
        ACT(ES, ES, AF.Exp, ["ESr"], ["ES"])
        ACT(CD, SC, AF.Exp, ["SCa", "SCb"], ["CD"], scale=128.0)
        ACT(KD[:, 0, :], LG[:, 0:8], AF.Exp, ["LG", "KT"], ["KDa"], scale=KT[:, 0:1])
        ACT(KD[:, 1, :], LG[:, 8:16], AF.Exp, ["LG", "KT"], ["KDb"], scale=KT[:, 1:2])
        TS("dve", KD, KD, 1.0, None, ALU.mult, None, ["KDa", "KDb"], ["KD"])
        for h in range(8):
            TS("dve", TMPa[:, h, :], R1, LG[:, h:h + 1], None, ALU.mult, None, ["R1", "LG"], [("TMPa", h)])
            STT("dve", TMPb[:, h, :], R2, LG[:, 8 + h:9 + h], TMPa[:, h, :], ALU.mult, ALU.add,
                ["R2", "LG", ("TMPa", h)], [("TMPb", h)])
            ACT(DT[:, h, :], TMPb[:, h, :], AF.Exp, [("TMPb", h)], ["DT"])
            ACT(QD[:, h, :], IQ, AF.Exp, ["IQ", "SCa", "SCb"], ["QD"], scale=SC[:, h:h + 1])
        P.barrier()
        A.off = m0

        def stopif(nm):
            if upto == nm:
                raise _Stop()

        def seq_body(b):
            m_s = A.off
            off_att = off_x1 + 8 * SEQ * 2
            WR = [A.alloc([128, 8, 768], BF, at=off_att + i_ * 8 * 768 * 2) for i_ in range(2)]
            wr_keys = {0: WDMA(WR[0], wret_d[0], ("WR", 0), 4)}
            XT = [A.alloc([128, D], F32) for _ in range(4)]
            HN = [A.alloc([128, D], BF) for _ in range(2)]
            JK = [A.alloc([128, D], BF) for _ in range(2)]
            bank_state["pool"] = [0, 1, 2, 3]

            def s0_A(c):
                xt = XT[c % 4]
                kx = ("xt", c % 4)
                if c == 0:
                    MS("pool", xt, 0.0, [kx])
                    DMA("sp", xt[112:128, :], meta_d, (), [kx], kx)
                else:
                    DMA("sp", xt, x_d[b, (c - 1) * 128:c * 128, :], (), [kx], kx)
                ACT(JK[c % 2], xt, AF.Square, [kx], [("ss", c), ("JK", c % 2)], accum_out=SS[:, c:c + 1])

            def s0_B1(c):
                TS("dve", SS[:, 20 + c:21 + c], SS[:, c:c + 1], 1.0 / D, 1e-6, ALU.mult, ALU.add, [("ss", c)], [("v1", c)])
                ACT(SS[:, 40 + c:41 + c], SS[:, 20 + c:21 + c], AF.Sqrt, [("v1", c)], [("v2", c)])

            def s0_B2(c):
                xt = XT[c % 4]
                kx = ("xt", c % 4)
                RC(SS[:, 20 + c:21 + c], SS[:, 40 + c:41 + c], [("v2", c)], [("rs", c)])
                STT("dve", HN[c % 2], xt, SS[:, 20 + c:21 + c], gmix, ALU.mult, ALU.mult, [kx, ("rs", c), "gmix"], [("hn", c % 2)])

            def s0_C(c):
                hn = HN[c % 2]
                bnk = nb()
                for k in range(8):
                    TR(PSB[bnk][:, k * 128:(k + 1) * 128], hn[:, k * 128:(k + 1) * 128], [("hn", c % 2)], [bk(bnk)])
                CP("act" if c % 2 else "dve", HT[:, :, c * 128:(c + 1) * 128],
                   PSB[bnk][:, :].rearrange("p (k t) -> p k t", k=8), [bk(bnk)], [("HT", c)])

            pipeline(NT, [(0, s0_A), (1, s0_B1), (2, s0_B2), (3, s0_C)])
            A.off = m_s
            P.barrier()
            if dbg and b == 0:
                DMA("pool", dbg_d["d_ht"], HT.rearrange("p k t -> p (k t)"), [("HT", 0)], (), "dbg")
            stopif("S0")

            QT = A.alloc([128, SEQ], BF)
            QDT = A.alloc([128, 2, SEQ], BF)
            KTt = A.alloc([128, LP], BF)
            KDEC = A.alloc([128, NT, 2, 128], BF)
            V = A.alloc([128, NT, 256], BF)
            SST = A.alloc([128, 2, 2, 128], F32)
            SBF = A.alloc([128, NT, 2, 128], BF)
            SGT = A.alloc([128, 2, 256], F32)
            SG = A.alloc([128, 16, 256], BF)
            STt = A.alloc([128, 2, 2, 128], BF)
            BST = A.alloc([128, 2, 2, 6], F32)
            MV = A.alloc([128, 4, 2, 2], F32)
            VE = A.alloc([128, 2, 2], F32)
            VS = A.alloc([128, 2, 2], F32)
            RS = A.alloc([128, 2, 2], F32)
            YF = A.alloc([128, 2, 256], F32)
            YB = A.alloc([128, 3, 256], BF)
            NMR = A.alloc([128, 2, 2], F32)
            OB = A.alloc([128, 4, 256], F32)
            pre_off = (A.off + 63) // 64 * 64
            assert pre_off >= m_s + 64 * 1024 - 16 * 1024 and pre_off + 10240 <= ARENA_BYTES, (pre_off, m_s)
            WAV = A.alloc([128, 8, 256], BF, at=pre_off)
            WA0 = A.alloc([128, 8, 384], BF, at=pre_off + 4096)
            bank_state["pool"] = [0, 1, 2, 3]
            for hp in range(4):
                wb = hp % 2
                W = WR[wb]
                kW = wr_keys[wb]
                if hp + 1 < 4:
                    wr_keys[1 - wb] = WDMA(WR[1 - wb], wret_d[hp + 1], ("WR", 1 - wb), 4)
                else:
                    kWAV = WDMA(WAV, wav_d, "WAV", 1)
                    wa_keys = {0: WDMA(WA0, watt_d[0], ("WA", 0), 2)}
                stopif("S1w")
                h0, h1 = 2 * hp, 2 * hp + 1
                for tb in range(4):
                    bnk = nb()
                    hk = [("HT", c) for c in range(1 + 4 * tb, 5 + 4 * tb)]
                    for kc in range(8):
                        MM(PS[bnk][:, :], W[:, kc, 0:128], HT[:, kc, 128 + tb * 512:128 + (tb + 1) * 512],
                           kc == 0, kc == 7, kW + hk, [bk(bnk)])
                    CP("act", QT[:, tb * 512:(tb + 1) * 512], PS[bnk][:, :], [bk(bnk)], [("QT", tb)])
                    TT("dve", QDT[0:64, 0, tb * 512:(tb + 1) * 512].rearrange("p (a t) -> p a t", a=4),
                       PS[bnk][0:64, :].rearrange("p (a t) -> p a t", a=4),
                       QD[0:64, h0:h0 + 1, :].to_broadcast([64, 4, 128]), ALU.mult, [bk(bnk), "QD"], [("QDT", 0, tb, "o")])
                    TT("dve", QDT[64:128, 1, tb * 512:(tb + 1) * 512].rearrange("p (a t) -> p a t", a=4),
                       PS[bnk][64:128, :].rearrange("p (a t) -> p a t", a=4),
                       QD[64:128, h1:h1 + 1, :].to_broadcast([64, 4, 128]), ALU.mult, [bk(bnk), "QD"], [("QDT", 1, tb, "o")])
                    for ts_ in ([tb - 1] if tb > 0 else []) + ([3] if tb == 3 else []):
                        bs = nb()
                        MM(PS[bs][:, :], SWP, QT[:, ts_ * 512:(ts_ + 1) * 512], True, True, [("QT", ts_), "SWP"], [bk(bs)])
                        TT("dve", QDT[0:64, 1, ts_ * 512:(ts_ + 1) * 512].rearrange("p (a t) -> p a t", a=4),
                           PS[bs][0:64, :].rearrange("p (a t) -> p a t", a=4),
                           QD[0:64, h1:h1 + 1, :].to_broadcast([64, 4, 128]), ALU.mult, [bk(bs), "QD"], [("QDT", 1, ts_, "s")])
                        TT("dve", QDT[64:128, 0, ts_ * 512:(ts_ + 1) * 512].rearrange("p (a t) -> p a t", a=4),
                           PS[bs][64:128, :].rearrange("p (a t) -> p a t", a=4),
                           QD[64:128, h0:h0 + 1, :].to_broadcast([64, 4, 128]), ALU.mult, [bk(bs), "QD"], [("QDT", 0, ts_, "s")])
                stopif("S1a")
                for tb in range(5):
                    n_ = 512 if tb < 4 else 128
                    c0 = tb * 512
                    bnk = nb()
                    hk = [("HT", c) for c in range(4 * tb, min(4 * tb + 4, NT))]
                    for kc in range(8):
                        MM(PS[bnk][:, 0:n_], W[:, kc, 128:256], HT[:, kc, c0:c0 + n_], kc == 0, kc == 7, kW + hk, [bk(bnk)])
                    ACT(KTt[:, c0:c0 + n_], PS[bnk][:, 0:n_], AF.Copy, [bk(bnk)], [("KT", tb)], scale=0.125)
                stopif("S1b")
                for c in range(NT):
                    bt = nb()
                    TR(PSB[bt][:, 0:128], KTt[:, c * 128:(c + 1) * 128], [("KT", c // 4)], [bk(bt)])
                    bnk = nb()
                    for kc in range(8):
                        MM(PS[bnk][:, 0:256], HT[:, kc, c * 128:(c + 1) * 128], W[:, kc, 256:512], kc == 0, kc == 7,
                           kW + [("HT", c)], [bk(bnk)])
                    for dr in range(2):
                        TT("dve", KDEC[:, c, :, dr * 64:(dr + 1) * 64], PSB[bt][:, 0:128].rearrange("p (h d) -> p h d", h=2),
                           KD[:, dr, 2 * hp:2 * hp + 2].unsqueeze(2).to_broadcast([128, 2, 64]), ALU.mult,
                           [bk(bt), "KD"], [("KDEC", c, dr)])
                    CP("act", V[:, c, :], PS[bnk][:, 0:256], [bk(bnk)], [("V", c)])
                stopif("S1c")
                MS("pool", SST[0:64, 0], 0.0, [("SST", 0, 0, 0), ("SST", 0, 0, 1)])
                MS("pool", SBF[0:64, 0], 0.0, [("SBF", 0, 0)])
                MS("pool", SST[64:128, 0], 0.0, [("SST", 1, 0, 0), ("SST", 1, 0, 1)])
                MS("pool", SBF[64:128, 16], 0.0, [("SBF", 1, 16)])
                def scan_step(dr, j, hp=hp):
                    lo = dr * 64
                    c = j if dr == 0 else 16 - j
                    cn = c + 1 if dr == 0 else c - 1
                    bnk = nb()
                    for h in range(2):
                        MM(PS[bnk][:, h * 128:(h + 1) * 128], KDEC[:, c, h, :], V[:, c, h * 128:(h + 1) * 128], True, True,
                           [("KDEC", c, 0), ("KDEC", c, 1), ("V", c)], [bk(bnk)])
                    for h in range(2):
                        hh = 2 * hp + h
                        STT("dve", SST[lo:lo + 64, (j + 1) % 2, h, :], SST[lo:lo + 64, j % 2, h, :], CD[lo:lo + 64, hh:hh + 1],
                            PS[bnk][lo:lo + 64, h * 128:(h + 1) * 128], ALU.mult, ALU.add,
                            [("SST", dr, j % 2, h), bk(bnk), "CD"], [("SST", dr, (j + 1) % 2, h)])
                    CP("act", SBF[lo:lo + 64, cn], SST[lo:lo + 64, (j + 1) % 2],
                       [("SST", dr, (j + 1) % 2, 0), ("SST", dr, (j + 1) % 2, 1)], [("SBF", dr, cn)])

                def gate_group(c, hp=hp, W=W, kW=kW):
                    bnk = nb()
                    for kc in range(8):
                        MM(PS[bnk][:, 0:256], HT[:, kc, c * 128:(c + 1) * 128], W[:, kc, 512:768], kc == 0, kc == 7,
                           kW + [("HT", c)], [bk(bnk)])
                    ACT(SGT[:, c % 2, :], PS[bnk][:, 0:256], AF.Silu, [bk(bnk)], [("SGT", c % 2)])
                    TT("pool", SG[:, c - 1, :], SGT[:, c % 2, :], gng[:, hp * 256:(hp + 1) * 256], ALU.mult,
                       [("SGT", c % 2), "gng"], [("SG", c - 1)])

                for j in range(16):
                    scan_step(0, j)
                    scan_step(1, j)
                    gate_group(j + 1)
                stopif("S1d")
                stopif("S1e1")

                def e_A(j, hp=hp):
                    c = j + 1
                    par = c % 2
                    bA, bB = nbA(), nbB()
                    MM(PS[bA][:, 0:128], KTt[0:64, c * 128:(c + 1) * 128], QT[0:64, (c - 1) * 128:c * 128], True, True,
                       [("KT", c // 4), ("QT", (c - 1) // 4)], [bk(bA)])
                    MM(PS[bB][:, 0:128], KTt[64:128, c * 128:(c + 1) * 128], QT[64:128, (c - 1) * 128:c * 128], True, True,
                       [("KT", c // 4), ("QT", (c - 1) // 4)], [bk(bB)])
                    TT("dve", STt[:, par, 0, :], PS[bA][:, 0:128], DT[:, 2 * hp, :], ALU.mult, [bk(bA), "DT"], [("ST", par, 0)])
                    TT("dve", STt[:, par, 1, :], PS[bB][:, 0:128], DT[:, 2 * hp + 1, :], ALU.mult, [bk(bB), "DT"], [("ST", par, 1)])

                def e_B1(j, hp=hp):
                    c = j + 1
                    par = c % 2
                    o3 = c % 4
                    bo = 2 + (c % 2)
                    for h in range(2):
                        MM(PS[bo][:, h * 128:(h + 1) * 128], STt[:, par, h, :], V[:, c, h * 128:(h + 1) * 128], True, False,
                           [("ST", par, h), ("V", c)], [bk(bo)])
                        MM(PS[bo][:, h * 128:(h + 1) * 128], QDT[:, h, (c - 1) * 128:c * 128], SBF[:, c, h, :], False, True,
                           [("QDT", h, (c - 1) // 4, "o"), ("QDT", h, (c - 1) // 4, "s"), ("SBF", 0, c), ("SBF", 1, c)], [bk(bo)])
                    CP("act", OB[:, o3, :], PS[bo][:, 0:256], [bk(bo)], [("OB", o3)])

                def e_B1b(j, hp=hp):
                    c = j + 1
                    par = c % 2
                    o3, m4 = c % 4, c % 4
                    for h in range(2):
                        P.op("dve", lambda e, o_=BST[:, par, h, :], i_=OB[:, o3, h * 128:(h + 1) * 128]: e.bn_stats(out=o_, in_=i_),
                             [("OB", o3)], [("BST", par, h)])
                        P.op("dve", lambda e, o_=MV[:, m4, h, :], i_=BST[:, par, h, :]: e.bn_aggr(out=o_, in_=i_),
                             [("BST", par, h)], [("MV", m4, h)])
                    TS("dve", VE[:, par, :], MV[:, m4, :, 1], 1e-5, None, ALU.add, None, [("MV", m4, 0), ("MV", m4, 1)], [("VE", par)])

                def e_S(j, hp=hp):
                    c = j + 1
                    par = c % 2
                    ACT(VS[:, par, :], VE[:, par, :], AF.Sqrt, [("VE", par)], [("VS", par)])

                def e_B2(j, hp=hp):
                    c = j + 1
                    par = c % 2
                    o3, m4 = c % 4, c % 4
                    RC(RS[:, par, :], VS[:, par, :], [("VS", par)], [("RS", par)])
                    STT("dve", NMR[:, par, :], MV[:, m4, :, 0], -1.0, RS[:, par, :], ALU.mult, ALU.mult,
                        [("MV", m4, 0), ("MV", m4, 1), ("RS", par)], [("NMR", par)])
                    for h in range(2):
                        ACT(YF[:, par, h * 128:(h + 1) * 128], OB[:, o3, h * 128:(h + 1) * 128], AF.Identity,
                            [("OB", o3), ("RS", par), ("NMR", par)], [("YF", par, h)],
                            scale=RS[:, par, h:h + 1], bias=NMR[:, par, h:h + 1])
                    TT("pool", YB[:, c % 3, :], YF[:, par, :], SG[:, c - 1, :], ALU.mult,
                       [("YF", par, 0), ("YF", par, 1), ("SG", c - 1)], [("YB", c % 3)])

                def e_C(j, hp=hp):
                    c = j + 1
                    par = c % 2
                    bt = c % 2
                    for h in range(2):
                        TR(PSB[bt][:, h * 128:(h + 1) * 128], YB[:, c % 3, h * 128:(h + 1) * 128], [("YB", c % 3)], [bk(bt)])
                    CP("act", RET[:, 2 * hp:2 * hp + 2, (c - 1) * 128:c * 128],
                       PSB[bt][:, 0:256].rearrange("p (h t) -> p h t", h=2), [bk(bt)], [("RET", hp, c)])

                pipeline(16, [(0, e_A), (1, e_B1), (2, e_B1b), (3, e_S), (4, e_B2), (6, e_C)])
            A.off = m_s
            WA = [WA0, A.alloc([128, 8, 384], BF)]
            WM0 = A.alloc([128, 8, 512], BF, at=m_s + 40 * 1024)
            P.barrier()
            if dbg and b == 0:
                DMA("pool", dbg_d["d_ret"], RET.rearrange("p k t -> p (k t)"), (), (), "dbg")
            stopif("S1")

            VA = A.alloc([128, NT, 4, 80], BF)
            AQ = A.alloc([128, 2, SEQ], BF)
            AK = A.alloc([128, LP], BF)
            PT = A.alloc([128, 2, 4, 2, 2, 128], BF)
            DEN = A.alloc([128, 2, 4], F32)
            RDEN = A.alloc([128, 2, 4], F32)
            ATM = A.alloc([128, 2, 4, 64], BF)
            MS("pool", VA, 1.0, ["VA1"])
            MS("pool", VA[0:112, 0], 0.0, ["VA1"])
            for c in range(NT):
                bnk = nb()
                for kc in range(8):
                    MM(PS[bnk][:, 0:256], HT[:, kc, c * 128:(c + 1) * 128], WAV[:, kc, :], kc == 0, kc == 7, kWAV + [("HT", c)], [bk(bnk)])
                CP("act" if c % 2 else "dve", VA[:, c, :, 0:64], PS[bnk][:, 0:256].rearrange("p (g d) -> p g d", g=4),
                   [bk(bnk), "VA1"], [("VA", c)])
            mstate = {"c": 0}
            for kh in range(4):
                wb = kh % 2
                W = WA[wb]
                kW = wa_keys[wb]
                if kh + 1 < 4:
                    wa_keys[1 - wb] = WDMA(WA[1 - wb], watt_d[kh + 1], ("WA", 1 - wb), 2)
                else:
                    assert A.off <= m_s + 40 * 1024, A.off - m_s
                    wm_keys = {0: WDMA(WM0, wmrg_d[0], ("WM", 0), 2)}
                for pr in range(2):
                    for tb in range(4):
                        bnk = nb()
                        hk = [("HT", c) for c in range(1 + 4 * tb, 5 + 4 * tb)]
                        for kc in range(8):
                            MM(PS[bnk][:, :], W[:, kc, pr * 128:(pr + 1) * 128], HT[:, kc, 128 + tb * 512:128 + (tb + 1) * 512],
                               kc == 0, kc == 7, kW + hk, [bk(bnk)])
                        CP("act" if tb % 2 else "dve", AQ[:, pr, tb * 512:(tb + 1) * 512], PS[bnk][:, :], [bk(bnk)], [("AQ", pr, tb)])
                for tb in range(5):
                    n_ = 512 if tb < 4 else 128
                    c0 = tb * 512
                    bnk = nb()
                    hk = [("HT", c) for c in range(4 * tb, min(4 * tb + 4, NT))]
                    for kc in range(8):
                        MM(PS[bnk][:, 0:n_], W[:, kc, 256:384], HT[:, kc, c0:c0 + n_], kc == 0, kc == 7, kW + hk, [bk(bnk)])
                    CP("act" if tb % 2 else "dve", AK[:, c0:c0 + n_], PS[bnk][:, 0:n_], [bk(bnk)], [("AK", tb)])
                def blocks_of(n):
                    return [(0, None)] + [(bb, bb - (n - 1)) for bb in (n - 1, n, n + 1) if 1 <= bb <= 16]

                def a_A(j, kh=kh, W=W, kW=kW):
                    n = j + 1
                    par = n % 2
                    blocks = blocks_of(n)
                    qk = [("AQ", 0, (n - 1) // 4), ("AQ", 1, (n - 1) // 4)]
                    for r0 in range(0, len(blocks), 2):
                        rb = blocks[r0:r0 + 2]
                        bA, bB = nbA(), nbB()
                        for j_, (bb, o_) in enumerate(rb):
                            col = j_ * 256
                            MM(PS[bA][:, col:col + 256].rearrange("p (g t) -> p g t", g=2), AK[0:64, bb * 128:(bb + 1) * 128],
                               AQ[0:64, :, (n - 1) * 128:n * 128], True, True, [("AK", bb // 4)] + qk, [bk(bA)])
                            MM(PS[bB][:, col:col + 256].rearrange("p (g t) -> p g t", g=2), AK[64:128, bb * 128:(bb + 1) * 128],
                               AQ[64:128, :, (n - 1) * 128:n * 128], True, True, [("AK", bb // 4)] + qk, [bk(bB)])
                        nr = len(rb)
                        for two, bX in ((0, bA), (1, bB)):
                            ACT(PT[:, par, r0:r0 + nr, two, :, :], PS[bX][:, 0:nr * 256].rearrange("p (j g t) -> p j g t", j=nr, g=2),
                                AF.Exp, [bk(bX)], [("PT", par, r0 + j2, two) for j2 in range(nr)], scale=0.125)
                        for j_, (bb, o_) in enumerate(rb):
                            si = r0 + j_
                            if o_ is None:
                                continue
                            msk = EM[:, o_, kh]
                            mstate["c"] += 1
                            TT("dve", PT[:, par, si], PT[:, par, si], msk, ALU.mult,
                               [("PT", par, si, 0), ("PT", par, si, 1), "EM", "MM4"], [("PT", par, si, 0), ("PT", par, si, 1)])

                def a_B(j, kh=kh):
                    n = j + 1
                    par = n % 2
                    blocks = blocks_of(n)
                    bv = nb()
                    pvbank[n] = bv
                    nbk = len(blocks)
                    for g in range(4):
                        for si, (bb, o_) in enumerate(blocks):
                            MM(PS[bv][:, g * 128:g * 128 + 65], PT[:, par, si, g % 2, g // 2, :], VA[:, bb, kh, 0:65],
                               si == 0, si == nbk - 1, [("PT", par, si, g % 2), ("VA", bb)], [bk(bv)])
                    pv = PS[bv][:, :].rearrange("p (g t) -> p g t", g=4)
                    TT("dve", DEN[:, par, :], pv[:, :, 64], ES[:, 4 * kh:4 * kh + 4], ALU.add, [bk(bv), "ES"], [("DEN", par)])
                    RC(RDEN[:, par, :], DEN[:, par, :], [("DEN", par)], [("RDEN", par)])
                    TT("dve", ATM[:, par], pv[:, :, 0:64], RDEN[:, par, :].unsqueeze(2).to_broadcast([128, 4, 64]), ALU.mult,
                       [bk(bv), ("RDEN", par)], [("ATM", par)])

                def a_C(j, kh=kh):
                    n = j + 1
                    par = n % 2
                    bt = nb()
                    atf = ATM[:, par].rearrange("p g d -> p (g d)")
                    for pr in range(2):
                        TR(PSB[bt][:, pr * 128:(pr + 1) * 128], atf[:, pr * 128:(pr + 1) * 128], [("ATM", par)], [bk(bt)])
                    CP("dve", ATT[:, 2 * kh:2 * kh + 2, (n - 1) * 128:n * 128],
                       PSB[bt][:, 0:256].rearrange("p (h t) -> p h t", h=2), [bk(bt)], [("ATT", kh, n)])

                pvbank = {}
                pipeline(16, [(0, a_A), (1, a_B), (2, a_C)])
            A.off = m_s
            MG = A.alloc([128, 8, SEQ], BF)
            m_mg = A.off
            WM1 = A.alloc([128, 8, 512], BF)
            assert A.off <= m_s + 40 * 1024
            A.off = m_s + 48 * 1024
            WM = [WM0, WM1]
            P.barrier()
            if dbg and b == 0:
                DMA("pool", dbg_d["d_att"], ATT.rearrange("p k t -> p (k t)"), (), (), "dbg")
            stopif("S2")

            m_s3 = A.off
            SGA = A.alloc([128, 2, 512], F32)
            SGB = A.alloc([128, 2, 512], F32)
            T1 = A.alloc([128, 2, 512], F32)
            T2 = A.alloc([128, 2, 512], F32)
            bank_state["pool"] = [0, 1, 2, 3]
            it = 0
            for m in range(8):
                wb = m % 2
                W = WM[wb]
                kW = wm_keys[wb]
                if m + 1 < 8:
                    wm_keys[1 - wb] = WDMA(WM[1 - wb], wmrg_d[m + 1], ("WM", 1 - wb), 2)
                for tb in range(4):
                    it += 1
                    par = it % 2
                    bs = [nb() for _ in range(4)]
                    srcs = (HT[:, :, 128 + tb * 512:128 + (tb + 1) * 512], HT[:, :, 128 + tb * 512:128 + (tb + 1) * 512],
                            RET[:, :, tb * 512:(tb + 1) * 512], ATT[:, :, tb * 512:(tb + 1) * 512])
                    for q_ in range(4):
                        for kc in range(8):
                            MM(PS[bs[q_]][:, :], W[:, kc, q_ * 128:(q_ + 1) * 128], srcs[q_][:, kc, :], kc == 0, kc == 7, kW, [bk(bs[q_])])
                    stopif("S3a")
                    ACT(SGA[:, par, :], PS[bs[0]][:, :], AF.Sigmoid, [bk(bs[0])], [("SGA", par)])
                    ACT(SGB[:, par, :], PS[bs[1]][:, :], AF.Sigmoid, [bk(bs[1])], [("SGB", par)])
                    TT("dve", T1[:, par, :], PS[bs[2]][:, :], SGA[:, par, :], ALU.mult, [bk(bs[2]), ("SGA", par)], [("T1", par)])
                    TT("dve", T2[:, par, :], PS[bs[3]][:, :], SGB[:, par, :], ALU.mult, [bk(bs[3]), ("SGB", par)], [("T2", par)])
                    stopif("S3c")
                    TT("pool", MG[:, m, tb * 512:(tb + 1) * 512], T1[:, par, :], T2[:, par, :], ALU.add,
                       [("T1", par), ("T2", par)], [("MG", m, tb)])
                    stopif("S3d")
            A.off = m_mg
            WO = A.alloc([128, 8, D], BF)
            P.barrier()
            kWO = WDMA(WO, wout_d, "WO", 4)
            if dbg and b == 0:
                DMA("pool", dbg_d["d_mg"], MG.rearrange("p k t -> p (k t)"), (), (), "dbg")
            stopif("S3")

            XR = [A.alloc([128, D], F32) for _ in range(2)]
            WG01 = [A.alloc([128, 8, 256], BF, at=m_s + 64 * 1024 + i_ * 4096) for i_ in range(2)]
            wg_keys = {0: WDMA(WG01[0], wgu_d[0], ("WG", 0), 2), 1: WDMA(WG01[1], wgu_d[1], ("WG", 1), 2)}
            H2 = [A.alloc([128, D], BF) for _ in range(2)]
            JK = [A.alloc([128, D], BF) for _ in range(2)]
            assert A.off <= m_s + 64 * 1024 and m_s + 72 * 1024 <= ARENA_BYTES, (A.off - m_s, m_s)
            def s4_A(t):
                par = t % 2
                kx = ("XR", par)
                DMA("sp", XR[par], x_d[b, t * 128:(t + 1) * 128, :], (), [kx], kx)
                for hf in range(2):
                    bnk = nb()
                    for m in range(8):
                        MM(PS[bnk][:, :], MG[:, m, t * 128:(t + 1) * 128], WO[:, m, hf * 512:(hf + 1) * 512], m == 0, m == 7, kWO, [bk(bnk)])
                    TT("dve", X1[:, t, hf * 512:(hf + 1) * 512], PS[bnk][:, :], XR[par][:, hf * 512:(hf + 1) * 512], ALU.add,
                       [bk(bnk), kx], [("X1", t, hf)])
                ACT(JK[t % 2], X1[:, t, :], AF.Square, [("X1", t, 0), ("X1", t, 1)], [("ss", t), ("JK", t % 2)], accum_out=SS[:, t:t + 1])

            def s4_B1(t):
                TS("dve", SS[:, 20 + t:21 + t], SS[:, t:t + 1], 1.0 / D, 1e-6, ALU.mult, ALU.add, [("ss", t)], [("v1", t)])
                ACT(SS[:, 40 + t:41 + t], SS[:, 20 + t:21 + t], AF.Sqrt, [("v1", t)], [("v2", t)])

            def s4_B2(t):
                par = t % 2
                RC(SS[:, 20 + t:21 + t], SS[:, 40 + t:41 + t], [("v2", t)], [("rs", t)])
                STT("dve", H2[par], X1[:, t, :], SS[:, 20 + t:21 + t], gffn, ALU.mult, ALU.mult,
                    [("X1", t, 0), ("X1", t, 1), ("rs", t), "gffn"], [("H2", par)])

            def s4_C(t):
                par = t % 2
                bnk = nb()
                for k in range(8):
                    TR(PSB[bnk][:, k * 128:(k + 1) * 128], H2[par][:, k * 128:(k + 1) * 128], [("H2", par)], [bk(bnk)])
                CP("act", HT[:, :, t * 128:(t + 1) * 128], PSB[bnk][:, :].rearrange("p (k t) -> p k t", k=8), [bk(bnk)], [("HT", t)])

            pipeline(16, [(0, s4_A), (1, s4_B1), (2, s4_B2), (3, s4_C)])
            A.off = m_s
            AT = A.alloc([128, 6, SEQ], BF)
            WD = A.alloc([128, 6, D], BF)
            WG = [WG01[0], WG01[1], A.alloc([128, 8, 256], BF)]
            P.barrier()
            if dbg and b == 0:
                DMA("sp", dbg_d["d_x1"], X1.rearrange("p k t -> p (k t)"), (), (), "dbg")
            stopif("S4")

            SA = A.alloc([128, 2, 512], F32)
            OT = [A.alloc([128, D], F32) for _ in range(2)]
            JK = [A.alloc([128, D], BF) for _ in range(2)]
            assert A.off <= m_s + 64 * 1024
            it = 0
            for qi, (f0, f1) in enumerate(FQ):
                nf = f1 - f0
                kWD = WDMA(WD[:, 0:nf, :], wd_d[:, f0 * D:f1 * D], "WD", 2)
                for f in range(f0, f1):
                    wb = f % 3
                    W = WG[wb]
                    kW = wg_keys[wb]
                    if f + 2 < 22:
                        w2 = (f + 2) % 3
                        wg_keys[w2] = WDMA(WG[w2], wgu_d[f + 2], ("WG", w2), 2)
                    for tb in range(4):
                        it += 1
                        par = it % 2
                        ba_, bb_ = nb(), nb()
                        hk = [("HT", c) for c in range(4 * tb, 4 * tb + 4)]
                        for kc in range(8):
                            MM(PS[ba_][:, :], W[:, kc, 0:128], HT[:, kc, tb * 512:(tb + 1) * 512], kc == 0, kc == 7, kW + hk, [bk(ba_)])
                        for kc in range(8):
                            MM(PS[bb_][:, :], W[:, kc, 128:256], HT[:, kc, tb * 512:(tb + 1) * 512], kc == 0, kc == 7, kW + hk, [bk(bb_)])
                        ACT(SA[:, par, :], PS[ba_][:, :], AF.Silu, [bk(ba_)], [("SA", par)])
                        TT("dve", AT[:, f - f0, tb * 512:(tb + 1) * 512], PS[bb_][:, :], SA[:, par, :], ALU.mult,
                           [bk(bb_), ("SA", par)], [("AT", f - f0, tb)])
                last = qi == len(FQ) - 1

                def d_A(t, nf=nf, kWD=kWD, last=last):
                    ak_ = [("AT", fl, t // 4) for fl in range(nf)]
                    for hf in range(2):
                        bnk = nb()
                        for fl in range(nf):
                            MM(PS[bnk][:, :], AT[:, fl, t * 128:(t + 1) * 128], WD[:, fl, hf * 512:(hf + 1) * 512], fl == 0, fl == nf - 1,
                               kWD + ak_, [bk(bnk)])
                        TT("dve", X1[:, t, hf * 512:(hf + 1) * 512], PS[bnk][:, :], X1[:, t, hf * 512:(hf + 1) * 512], ALU.add,
                           [bk(bnk), ("X1", t, hf)], [("X1", t, hf)])
                    if last:
                        ACT(JK[t % 2], X1[:, t, :], AF.Square, [("X1", t, 0), ("X1", t, 1)], [("ss", t), ("JK", t % 2)],
                            accum_out=SS[:, t:t + 1])

                def d_B1(t):
                    TS("dve", SS[:, 20 + t:21 + t], SS[:, t:t + 1], 1.0 / D, 1e-6, ALU.mult, ALU.add, [("ss", t)], [("v1", t)])
                    ACT(SS[:, 40 + t:41 + t], SS[:, 20 + t:21 + t], AF.Sqrt, [("v1", t)], [("v2", t)])

                def d_B2(t):
                    par = t % 2
                    RC(SS[:, 20 + t:21 + t], SS[:, 40 + t:41 + t], [("v2", t)], [("rs", t)])
                    STT("dve", OT[par], X1[:, t, :], SS[:, 20 + t:21 + t], gfin, ALU.mult, ALU.mult,
                        [("X1", t, 0), ("X1", t, 1), ("rs", t), "gfin"], [("OT", par)])
                    DMA("sp", y_d[b, t * 128:(t + 1) * 128, :], OT[par], [("OT", par)], (), ("OT", par))

                if last:
                    pipeline(16, [(0, d_A), (1, d_B1), (2, d_B2)])
                else:
                    pipeline(16, [(0, d_A)])
            A.off = m_s
            P.barrier()

        for b_ in range(NSEQ):
            try:
                seq_body(b_)
            except _Stop:
                break
        P.barrier()
        P.plan(nc, st)
        with nc.Block() as block:
            P.emit(nc, block)
    return nc, P, A


def _tileK(W):
    K, N = W.shape
    return np.ascontiguousarray(W.reshape(K // 128, 128, N).transpose(1, 0, 2)).reshape(128, (K // 128) * N)


def _tables():
    i = np.arange(128, dtype=np.float32)
    r1 = np.maximum(i[None, :] - i[:, None], 0.0)
    r2 = np.maximum(i[:, None] - i[None, :], 0.0)
    iq = np.concatenate([np.tile((i + 1.0)[None], (64, 1)), np.tile((128.0 - i)[None], (64, 1))], 0)
    kt = np.stack([127.0 - i, i], 1)
    slopes = 2.0 ** (-8.0 * np.arange(1, 17, dtype=np.float64) / 16.0)
    s = np.arange(128)[:, None].astype(np.float64)
    t = np.arange(128)[None, :].astype(np.float64)
    d0 = t - s + 128.0
    d1 = np.abs(t - s)
    d2 = s + 128.0 - t
    em = np.zeros((128, 3, 4, 2, 2, 128), np.float64)
    for h in range(16):
        kh, g = h // 4, h % 4
        two, gp = g % 2, g // 2
        em[:, 0, kh, two, gp, :] = np.where(d0 <= 128, np.exp(-slopes[h] * d0), 0.0)
        em[:, 1, kh, two, gp, :] = np.exp(-slopes[h] * d1)
        em[:, 2, kh, two, gp, :] = np.where(d2 <= 128, np.exp(-slopes[h] * d2), 0.0)
    mm4 = np.zeros((128, 512), np.float32)
    mm4[112:, :] = 1.0
    swp = np.zeros((128, 128), np.float32)
    swp[np.arange(128), (np.arange(128) + 64) % 128] = 1.0
    return dict(ident=np.eye(128, dtype=np.float32), r1=r1.astype(np.float32), r2=r2.astype(np.float32),
                iq=iq.astype(np.float32), kt=kt.astype(np.float32),
                em=em.reshape(128, -1).astype(np.float32), mm4=mm4, swp=swp)


def _prep_shared(meta_tokens, w_in, ret_decay_logit_fwd, ret_decay_logit_bwd, ret_gn_gain, attn_sink,
                 w_branch_ret, w_branch_att, w_out, norm_mix, norm_ffn, w_gate_up, w_down, norm_final):
    f = np.float32
    wi = np.asarray(w_in[0], f)
    wbr = np.asarray(w_branch_ret[0], f)
    wba = np.asarray(w_branch_att[0], f)
    wgu = np.asarray(w_gate_up[0], f)
    wret = []
    for hp in range(4):
        h0, h1 = 2 * hp, 2 * hp + 1
        q0 = wi[:, h0 * 64:(h0 + 1) * 64]
        q1 = wi[:, h1 * 64:(h1 + 1) * 64]
        cols = [q0, q1, wi[:, 512 + hp * 128:512 + (hp + 1) * 128],
                wi[:, 1024 + hp * 256:1024 + (hp + 1) * 256], wi[:, 2048 + hp * 256:2048 + (hp + 1) * 256]]
        wret.append(_tileK(np.concatenate(cols, 1)))
    watt = []
    for kh in range(4):
        k_ = wi[:, 4096 + kh * 64:4096 + (kh + 1) * 64]
        watt.append(_tileK(np.concatenate([wi[:, 3072 + kh * 256:3072 + (kh + 1) * 256], k_, k_], 1)))
    wmrg = []
    for m in range(8):
        sl = slice(m * 128, (m + 1) * 128)
        wmrg.append(_tileK(np.concatenate([wi[:, 4608:5632][:, sl], wi[:, 5632:6656][:, sl], wbr[:, sl], wba[:, sl]], 1)))
    wgut = []
    for ff in range(22):
        wgut.append(_tileK(np.concatenate([wgu[:, ff * 128:(ff + 1) * 128], wgu[:, 2816 + ff * 128:2816 + (ff + 1) * 128]], 1)))
    d = dict(meta=np.asarray(meta_tokens, f), wret=np.stack(wret), watt=np.stack(watt), wav=_tileK(wi[:, 4352:4608]),
             wmrg=np.stack(wmrg), wout=_tileK(np.asarray(w_out[0], f)), wgu=np.stack(wgut), wd=_tileK(np.asarray(w_down[0], f)),
             lgf=np.asarray(ret_decay_logit_fwd[0], f), lgb=np.asarray(ret_decay_logit_bwd[0], f),
             gn=np.asarray(ret_gn_gain[0], f), sink=np.asarray(attn_sink[0], f), gmix=np.asarray(norm_mix[0], f),
             gffn=np.asarray(norm_ffn[0], f), gfin=np.asarray(norm_final, f))
    d.update(_tables())
    return {k: np.ascontiguousarray(v) for k, v in d.items()}


_CACHE = {}


def kernel(x, meta_tokens, w_in, ret_decay_logit_fwd, ret_decay_logit_bwd, ret_gn_gain, attn_sink,
           w_branch_ret, w_branch_att, w_out, norm_mix, norm_ffn, w_gate_up, w_down, norm_final):
    x = np.asarray(x, np.float32)
    shared = _prep_shared(meta_tokens, w_in, ret_decay_logit_fwd, ret_decay_logit_bwd, ret_gn_gain, attn_sink,
                          w_branch_ret, w_branch_att, w_out, norm_mix, norm_ffn, w_gate_up, w_down, norm_final)
    if "nc" not in _CACHE:
        _CACHE["nc"] = build()[0]
    nc = _CACHE["nc"]
    n = 8
    in_maps = []
    for c in range(n):
        m = dict(shared)
        m["x"] = np.ascontiguousarray(x[NSEQ * c:NSEQ * (c + 1)])
        in_maps.append(m)
    res = run_bass_kernel_spmd(nc, in_maps, core_ids=list(range(n)))
    return np.concatenate([np.asarray(r["y"], np.float32) for r in res.results], axis=0)
```
